# Optimizing a Trainium2 kernel written in Bass

```python
import jax, jax.numpy as jnp
from jax import lax
import numpy as np

D_MODEL = 1024
BATCH = 4
SEQ = 4096
DEPTH = 4
DEC_BATCH = 8
DEC_SEQ = 8192
PAST_LEN = 128

CONV_CH = D_MODEL // 4
CONV_W = 3
N_HEADS = 8
N_KV_HEADS = 2
HEAD_DIM = D_MODEL // 16
WINDOW = 128
BLOCK = 128
DN_HEADS = 4
DN_DIM = D_MODEL // 16
DN_CONV_W = 3
CHUNK = 64
D_FF = 4 * D_MODEL
ATT_Q = N_HEADS * HEAD_DIM
ATT_KV = N_KV_HEADS * HEAD_DIM
DN_W = DN_HEADS * DN_DIM
D_MIX = CONV_CH + ATT_Q + DN_W
OFF_ATT = 3 * CONV_CH
OFF_DN = OFF_ATT + ATT_Q + 2 * ATT_KV
D_IN = OFF_DN + 4 * DN_W + 4 * DN_HEADS
ALPHA = (2.0 * DEPTH) ** 0.25
BETA_INIT = (8.0 * DEPTH) ** -0.25
LN_EPS = 1e-5
RMS_EPS = 1e-6

kernel_name = 'hybrid_bidir_conv_swa_gdn_encoder'


def layer_norm(x, g, b):
    xf = x.astype(jnp.float32)
    mu = jnp.mean(xf, -1, keepdims=True)
    var = jnp.mean(jnp.square(xf - mu), -1, keepdims=True)
    y = (xf - mu) * lax.rsqrt(var + LN_EPS) * g.astype(jnp.float32) + b.astype(jnp.float32)
    return y.astype(x.dtype)


def modulate(x, shift, scale):
    return x * (1 + scale[:, None, :]) + shift[:, None, :]


def dwconv_centred(x, w):
    p = w.shape[0] // 2
    return lax.conv_general_dilated(x, w[:, None, :].astype(x.dtype), (1,), ((p, p),),
                                    dimension_numbers=('NWC', 'WIO', 'NWC'),
                                    feature_group_count=x.shape[-1])


def short_conv_mixer(p, conv_w):
    b_gate, c_gate, u = jnp.split(p, 3, axis=-1)
    return b_gate * dwconv_centred(c_gate * u, conv_w)


def window_attention(q, k, v, sink):
    bsz, s, _ = q.shape
    nb = s // BLOCK
    grp = N_HEADS // N_KV_HEADS
    qb = q.reshape(bsz, nb, BLOCK, N_KV_HEADS, grp, HEAD_DIM)

    def neighbours(t):
        tp = jnp.pad(t.reshape(bsz, s, N_KV_HEADS, HEAD_DIM), ((0, 0), (BLOCK, BLOCK), (0, 0), (0, 0)))
        tp = tp.reshape(bsz, nb + 2, BLOCK, N_KV_HEADS, HEAD_DIM)
        return jnp.concatenate([tp[:, :-2], tp[:, 1:-1], tp[:, 2:]], axis=2)

    kb, vb = neighbours(k), neighbours(v)
    scores = jnp.einsum('bnqhgd,bnkhd->bnhgqk', qb, kb).astype(jnp.float32) * HEAD_DIM ** -0.5
    rel = jnp.arange(3 * BLOCK)[None, :] - BLOCK - jnp.arange(BLOCK)[:, None]
    dist = jnp.abs(rel).astype(jnp.float32)
    key_pos = (jnp.arange(nb)[:, None] - 1) * BLOCK + jnp.arange(3 * BLOCK)[None, :]
    valid = (jnp.abs(rel) <= WINDOW)[None] & ((key_pos >= 0) & (key_pos < s))[:, None, :]
    slopes = jnp.exp2(-8.0 * jnp.arange(1, N_HEADS + 1, dtype=jnp.float32) / N_HEADS).reshape(N_KV_HEADS, grp)
    scores = scores - slopes[:, :, None, None] * dist
    scores = jnp.where(valid[:, None, None], scores, -jnp.inf)
    sink_l = sink.astype(jnp.float32).reshape(N_KV_HEADS, grp)[:, :, None, None]
    m = jnp.maximum(jnp.max(scores, -1, keepdims=True), sink_l)
    e = jnp.exp(scores - m)
    denom = jnp.sum(e, -1, keepdims=True) + jnp.exp(sink_l - m)
    probs = (e / denom).astype(v.dtype)
    o = jnp.einsum('bnhgqk,bnkhd->bnqhgd', probs, vb)
    return o.reshape(bsz, s, ATT_Q)


def gated_delta_chunked(q, k, v, g, beta):
    bsz, s, h, dk = q.shape
    n = s // CHUNK

    def chunks(t):
        return jnp.moveaxis(t.reshape(bsz, n, CHUNK, h, *t.shape[3:]), 3, 1)

    q, k, v, g, beta = (chunks(t) for t in (q, k, v, g, beta))
    q = q * dk ** -0.5
    gc = jnp.cumsum(g, axis=-1)
    idx = jnp.arange(CHUNK)
    causal = idx[:, None] >= idx[None, :]
    strict = idx[:, None] > idx[None, :]
    decay = jnp.exp(jnp.where(causal, gc[..., :, None] - gc[..., None, :], -jnp.inf))
    k_beta = k * beta[..., None]
    lower = jnp.where(strict, jnp.einsum('bhncd,bhnsd->bhncs', k_beta, k) * decay, 0.0)
    tmat = lower + jnp.eye(CHUNK, dtype=lower.dtype)
    u = lax.linalg.triangular_solve(tmat, v * beta[..., None], left_side=True, lower=True)
    w = lax.linalg.triangular_solve(tmat, k_beta * jnp.exp(gc)[..., None], left_side=True, lower=True)
    a_intra = jnp.where(causal, jnp.einsum('bhncd,bhnsd->bhncs', q, k) * decay, 0.0)

    def step(state, inp):
        qc, kc, uc, wc, ac, gcc = inp
        v_new = uc - jnp.einsum('bhcd,bhde->bhce', wc, state)
        o = (jnp.einsum('bhcd,bhde->bhce', qc * jnp.exp(gcc)[..., None], state)
             + jnp.einsum('bhcs,bhse->bhce', ac, v_new))
        g_last = gcc[..., -1:]
        state = (state * jnp.exp(g_last)[..., None]
                 + jnp.einsum('bhcd,bhce->bhde', kc * jnp.exp(g_last - gcc)[..., None], v_new))
        return state, o

    xs = tuple(jnp.moveaxis(t, 2, 0) for t in (q, k, u, w, a_intra, gc))
    state0 = jnp.zeros((bsz, h, dk, v.shape[-1]), jnp.float32)
    _, o = lax.scan(step, state0, xs)
    return jnp.transpose(o, (1, 0, 3, 2, 4)).reshape(bsz, s, h, -1)


def l2norm(t):
    return t * lax.rsqrt(jnp.sum(jnp.square(t), -1, keepdims=True) + RMS_EPS)


def delta_mixer(p, conv_w, a_log_f, a_log_b, dt_f, dt_b, norm_g):
    bsz, s, _ = p.shape
    f32 = jnp.float32
    qkv = jax.nn.silu(dwconv_centred(p[..., :3 * DN_W], conv_w)).astype(f32)
    q, k, v = (t.reshape(bsz, s, DN_HEADS, DN_DIM) for t in jnp.split(qkv, 3, axis=-1))
    q, k = l2norm(q), l2norm(k)
    z = p[..., 3 * DN_W:4 * DN_W].astype(f32)
    a_f, a_b, b_f, b_b = jnp.split(p[..., 4 * DN_W:].astype(f32), 4, axis=-1)
    g_f = -jnp.exp(a_log_f.astype(f32)) * jax.nn.softplus(a_f + dt_f.astype(f32))
    g_b = -jnp.exp(a_log_b.astype(f32)) * jax.nn.softplus(a_b + dt_b.astype(f32))
    o_f = gated_delta_chunked(q, k, v, g_f, jax.nn.sigmoid(b_f))
    rev = lambda t: jnp.flip(t, axis=1)
    o_b = rev(gated_delta_chunked(rev(q), rev(k), rev(v), rev(g_b), rev(jax.nn.sigmoid(b_b))))
    o = o_f + o_b
    o = o * lax.rsqrt(jnp.mean(jnp.square(o), -1, keepdims=True) + RMS_EPS) * norm_g.astype(f32)
    return (o.reshape(bsz, s, DN_W) * jax.nn.silu(z)).astype(p.dtype)


def encoder_layer(x, c, w_mod, b_mod, w_in, conv_a_w, attn_sink, dn_conv_w, dn_a_log_f, dn_a_log_b,
                  dn_dt_bias_f, dn_dt_bias_b, dn_norm_g, w_out, ln1_g, ln1_b, w1, b1, w2, b2, ln2_g, ln2_b):
    mod = jax.nn.silu(c) @ w_mod + b_mod
    sh1, sc1, g1, sh2, sc2, g2 = jnp.split(mod, 6, axis=-1)
    h = modulate(x, sh1, sc1)
    p = h @ w_in
    y_a = short_conv_mixer(p[..., :OFF_ATT], conv_a_w)
    q = p[..., OFF_ATT:OFF_ATT + ATT_Q]
    k = p[..., OFF_ATT + ATT_Q:OFF_ATT + ATT_Q + ATT_KV]
    v = p[..., OFF_ATT + ATT_Q + ATT_KV:OFF_DN]
    y_b = window_attention(q, k, v, attn_sink)
    y_c = delta_mixer(p[..., OFF_DN:], dn_conv_w, dn_a_log_f, dn_a_log_b, dn_dt_bias_f, dn_dt_bias_b, dn_norm_g)
    y = jnp.concatenate([y_a, y_b, y_c], axis=-1) @ w_out
    x = layer_norm(ALPHA * x + (1 + g1[:, None, :]) * y, ln1_g, ln1_b)
    h = modulate(x, sh2, sc2)
    f = jnp.square(jax.nn.relu(h @ w1 + b1)) @ w2 + b2
    return layer_norm(ALPHA * x + (1 + g2[:, None, :]) * f, ln2_g, ln2_b)


def trunk(x, c, ln_in_g, ln_in_b, w_mod, b_mod, w_in, conv_a_w, attn_sink, dn_conv_w, dn_a_log_f, dn_a_log_b,
          dn_dt_bias_f, dn_dt_bias_b, dn_norm_g, w_out, ln1_g, ln1_b, w1, b1, w2, b2, ln2_g, ln2_b):
    x = layer_norm(x, ln_in_g, ln_in_b)
    for l in range(DEPTH):
        x = encoder_layer(x, c, w_mod[l], b_mod[l], w_in[l], conv_a_w[l], attn_sink[l], dn_conv_w[l],
                          dn_a_log_f[l], dn_a_log_b[l], dn_dt_bias_f[l], dn_dt_bias_b[l], dn_norm_g[l],
                          w_out[l], ln1_g[l], ln1_b[l], w1[l], b1[l], w2[l], b2[l], ln2_g[l], ln2_b[l])
    return x


def setup_inputs(seed: int = 0) -> dict:
    key = jax.random.key(seed)
    ks = jax.random.split(key, 32)
    f32 = jnp.float32
    L = DEPTH
    nrm = lambda k, shape, scale: jax.random.normal(k, shape, f32) * scale
    return {
        'x_prompt': nrm(ks[0], (BATCH, SEQ, D_MODEL), 1.0),
        'x_sample': nrm(ks[1], (DEC_BATCH, DEC_SEQ, D_MODEL), 1.0),
        'c_prompt': nrm(ks[2], (BATCH, D_MODEL), 1.0),
        'c_sample': nrm(ks[3], (DEC_BATCH, D_MODEL), 1.0),
        'ln_in_g': 1.0 + nrm(ks[4], (D_MODEL,), 0.02),
        'ln_in_b': nrm(ks[5], (D_MODEL,), 0.02),
        'w_mod': nrm(ks[6], (L, D_MODEL, 6 * D_MODEL), 0.2 * D_MODEL ** -0.5),
        'b_mod': nrm(ks[7], (L, 6 * D_MODEL), 0.01),
        'w_in': nrm(ks[8], (L, D_MODEL, D_IN), D_MODEL ** -0.5),
        'conv_a_w': nrm(ks[9], (L, CONV_W, CONV_CH), CONV_W ** -0.5),
        'attn_sink': nrm(ks[10], (L, N_HEADS), 1.0),
        'dn_conv_w': nrm(ks[11], (L, DN_CONV_W, 3 * DN_W), DN_CONV_W ** -0.5),
        'dn_a_log_f': jnp.log(jax.random.uniform(ks[12], (L, DN_HEADS), f32, 1.0, 16.0)),
        'dn_a_log_b': jnp.log(jax.random.uniform(ks[13], (L, DN_HEADS), f32, 1.0, 16.0)),
        'dn_dt_bias_f': jnp.log(jnp.expm1(jax.random.uniform(ks[14], (L, DN_HEADS), f32, 1e-3, 1e-1))),
        'dn_dt_bias_b': jnp.log(jnp.expm1(jax.random.uniform(ks[15], (L, DN_HEADS), f32, 1e-3, 1e-1))),
        'dn_norm_g': 1.0 + nrm(ks[16], (L, DN_DIM), 0.02),
        'w_out': nrm(ks[17], (L, D_MIX, D_MODEL), BETA_INIT * D_MIX ** -0.5),
        'ln1_g': 1.0 + nrm(ks[18], (L, D_MODEL), 0.02),
        'ln1_b': nrm(ks[19], (L, D_MODEL), 0.02),
        'w1': nrm(ks[20], (L, D_MODEL, D_FF), D_MODEL ** -0.5),
        'b1': nrm(ks[21], (L, D_FF), 0.01),
        'w2': nrm(ks[22], (L, D_FF, D_MODEL), BETA_INIT * D_FF ** -0.5),
        'b2': nrm(ks[23], (L, D_MODEL), 0.01),
        'ln2_g': 1.0 + nrm(ks[24], (L, D_MODEL), 0.02),
        'ln2_b': nrm(ks[25], (L, D_MODEL), 0.02),
    }


def reference(x_prompt, x_sample, c_prompt, c_sample, ln_in_g, ln_in_b, w_mod, b_mod, w_in, conv_a_w, attn_sink,
              dn_conv_w, dn_a_log_f, dn_a_log_b, dn_dt_bias_f, dn_dt_bias_b, dn_norm_g, w_out, ln1_g, ln1_b,
              w1, b1, w2, b2, ln2_g, ln2_b):
    params = (ln_in_g, ln_in_b, w_mod, b_mod, w_in, conv_a_w, attn_sink, dn_conv_w, dn_a_log_f, dn_a_log_b,
              dn_dt_bias_f, dn_dt_bias_b, dn_norm_g, w_out, ln1_g, ln1_b, w1, b1, w2, b2, ln2_g, ln2_b)
    y_prompt = trunk(x_prompt, c_prompt, *params)
    y_sample = trunk(x_sample, c_sample, *params)
    return (y_prompt, y_sample)
```

```python
import numpy as np
import os as _os
from contextlib import ExitStack
import concourse.bass as bass
import concourse.mybir as mybir
from concourse.bass_utils import run_bass_kernel_spmd

F32 = mybir.dt.float32
BF16 = mybir.dt.bfloat16
AF = mybir.ActivationFunctionType
ALU = mybir.AluOpType
AX = mybir.AxisListType

D = 1024
DFF = 4096
DIN = 2576
NL = 4
ALPHA = (2.0 * 4) ** 0.25
LN_EPS = 1e-5
RMS_EPS = 1e-6
OFF_ATT = 768
OFF_DN = 1536
NFM = 17
NTM = 400
NPT = 15
BIG = 30000.0


class Res:
    __slots__ = ("name", "w", "rs", "chan")

    def __init__(self, name=""):
        self.name = name
        self.w = None
        self.rs = {}
        self.chan = {}


class Op:
    __slots__ = ("eng", "fn", "chan", "deps", "signal", "sem", "val", "idx")

    def __init__(self, eng, fn, chan):
        self.eng = eng
        self.fn = fn
        self.chan = chan
        self.deps = []
        self.signal = chan is not None
        self.sem = None
        self.val = 0


class Sched:
    ENGS = ("tensor", "vector", "scalar", "gpsimd", "sync")

    def __init__(self):
        self.ops = {e: [] for e in self.ENGS}
        self.all = []
        self.chan_last = {}
        self.fence = []
        self.nchan = 0

    def op(self, eng, fn, R=(), W=(), chan=None):
        o = Op(eng, fn, chan)
        deps = {}

        pos = len(self.ops[eng])

        def add(d):
            if d is None:
                return
            if d.chan is None and d.eng == eng:
                if eng == "tensor":
                    return
                if eng in ("vector", "scalar") and pos - d.idx >= 3:
                    return
            deps[id(d)] = d

        for r in R:
            add(r.w)
        for w in W:
            add(w.w)
            for d in w.rs.values():
                add(d)
        if chan is not None:
            add(self.chan_last.get(chan))
            self.chan_last[chan] = o
        for d in self.fence:
            if d is not o:
                add(d)
        o.idx = pos
        o.deps = list(deps.values())
        for d in o.deps:
            d.signal = True
        key = eng if chan is None else ("dma", chan)
        for r in R:
            r.rs[key] = o
        for w in W:
            w.w = o
            w.rs = {}
        self.ops[eng].append(o)
        self.all.append(o)
        return o

    def barrier(self):
        f = []
        for e in self.ENGS:
            for o in reversed(self.ops[e]):
                if o.chan is None:
                    f.append(o)
                    break
        f.extend(self.chan_last.values())
        for o in f:
            o.signal = True
        self.fence = f

    def emit(self, nc, es):
        esem = {}
        for e in self.ENGS:
            if e != "sync":
                esem[e] = es.enter_context(nc.semaphore("sem_" + e))
        chans = sorted(self.chan_last.keys())
        nsem = min(len(chans), 90)
        csem_list = [es.enter_context(nc.semaphore("semd%d" % i)) for i in range(nsem)]
        assert len(chans) <= 90, len(chans)
        csem = {c: csem_list[i] for i, c in enumerate(chans)}
        ccount = {c: 0 for c in chans}
        for o in self.all:
            if o.chan is not None:
                ccount[o.chan] += 16
                o.sem = csem[o.chan]
                o.val = ccount[o.chan]
        for e in self.ENGS:
            cnt = 0
            for o in self.ops[e]:
                if o.chan is None and o.signal:
                    cnt += 1
                    o.sem = esem[e]
                    o.val = cnt
        block = es.enter_context(nc.Block())

        def run(engname):
            def body(e):
                waited = {}
                for o in self.ops[engname]:
                    need = {}
                    for d in o.deps:
                        k = id(d.sem)
                        if waited.get(k, 0) >= d.val:
                            continue
                        if k not in need or need[k][1] < d.val:
                            need[k] = (d.sem, d.val)
                    for k, (s, v) in need.items():
                        e.wait_ge(s, v)
                        waited[k] = v
                    ins = o.fn(e)
                    if o.chan is not None:
                        ins.then_inc(o.sem, 16)
                    elif o.signal:
                        ins.then_inc(o.sem, 1)
                last = {}
                for o in self.ops[engname]:
                    if o.chan is not None:
                        last[id(o.sem)] = (o.sem, max(o.val, last.get(id(o.sem), (None, 0))[1]))
                for k, (s, v) in last.items():
                    if waited.get(k, 0) < v:
                        e.wait_ge(s, v)
            return body

        block.sync(run("sync"))
        block.gpsimd(run("gpsimd"))
        block.scalar(run("scalar"))
        block.vector(run("vector"))
        block.tensor(run("tensor"))


class T:
    __slots__ = ("ap", "res")

    def __init__(self, ap, res):
        self.ap = ap
        self.res = res


class Arena:
    def __init__(self, ap2d, ncols):
        self.ap = ap2d
        self.n = ncols
        self.base = 0
        self.top = 0

    def mark(self):
        self.base = self.top

    def reset(self):
        self.top = self.base

    def f32(self, cols, name=""):
        a = self.top
        self.top += cols
        assert self.top <= self.n, ("arena overflow", name, self.top, self.n)
        return T(self.ap[:, a:a + cols], Res(name))

    def bf16(self, cols, name=""):
        c32 = (cols + 1) // 2
        a = self.top
        self.top += c32
        assert self.top <= self.n, ("arena overflow", name, self.top, self.n)
        return T(self.ap[:, a:a + c32].bitcast(BF16)[:, 0:cols], Res(name))


class Ring:
    def __init__(self, tiles):
        self.tiles = tiles
        self.i = 0

    def next(self):
        t = self.tiles[self.i % len(self.tiles)]
        self.i += 1
        return t


class Cfg:
    def __init__(self, Ss=8192, Sp=4096, nl=NL, dbg=False, phases=None):
        self.phases = phases
        self.Ss = Ss
        self.Sp = Sp
        self.nl = nl
        self.St = Ss + Sp
        self.seqs = [(0, Ss), (Ss, Sp)]
        self.dbg = dbg


def build(cfg):
    nc = bass.Bass("TRN2", target_bir_lowering=False)
    S = Sched()
    St = cfg.St
    nl = cfg.nl

    def din(name, shape, dt=F32):
        return nc.dram_tensor(name, list(shape), dt, kind="ExternalInput").ap()

    def dscr(name, shape, dt=F32):
        kind = "ExternalOutput" if cfg.dbg else "Internal"
        return nc.dram_tensor(name, list(shape), dt, kind=kind).ap()

    xs = din("xs", [cfg.Ss, D])
    xp = din("xp", [cfg.Sp, D])
    cT = din("cT", [128, 8, 2])
    ln_in_g = din("ln_in_g", [1, D])
    ln_in_b = din("ln_in_b", [1, D])
    w_mod = din("w_mod", [nl, D, 6 * D])
    b_mod = din("b_mod", [nl, 6 * D])
    b_modc = din("b_modc", [nl, 128, 48])
    w_in = din("w_in", [nl, D, DIN])
    conv_a = din("conv_a", [nl, 128, 2, 3])
    sink = din("sink", [nl, 8])
    dn_conv = din("dn_conv", [nl, 128, 6, 3])
    a_log = din("a_log", [nl, 8])
    dt_bias = din("dt_bias", [nl, 8])
    norm_g = din("norm_g", [nl, 256])
    w_out = din("w_out", [nl, D, D])
    ln1_g = din("ln1_g", [nl, D])
    ln1_b = din("ln1_b", [nl, D])
    w1 = din("w1", [nl, D, DFF])
    b1c = din("b1c", [nl, 128, 32])
    w2 = din("w2", [nl, DFF, D])
    b2 = din("b2", [nl, D])
    ln2_g = din("ln2_g", [nl, D])
    ln2_b = din("ln2_b", [nl, D])
    c_ident = din("c_ident", [128, 128])
    c_emask = din("c_emask", [128, 8 * 3 * 128], BF16)
    c_bones = din("c_bones", [128, 128])
    c_ones = din("c_ones", [128, 128])
    c_U = din("c_U", [2, 128, 128])
    c_PM = din("c_PM", [2, 128, 128])
    c_NM = din("c_NM", [2, 128, 128])
    c_sel = din("c_sel", [4, 8 * 128])
    c_blk = din("c_blk", [128, 4 * 128])

    ys = nc.dram_tensor("ys", [cfg.Ss, D], F32, kind="ExternalOutput").ap()
    yp = nc.dram_tensor("yp", [cfg.Sp, D], F32, kind="ExternalOutput").ap()

    xbuf = [dscr("xbuf0", [St, D]), dscr("xbuf1", [St, D])]
    x1buf = dscr("x1buf", [St, D])
    modrow = dscr("modrow", [nl, 2, 6 * D])
    PT = dscr("PT", [NPT, 128, St], BF16)
    Pv = dscr("Pv", [St, 132], BF16)
    Pzg = dscr("Pzg", [St, 272])
    qnT = dscr("qnT", [2, 128, St], BF16)
    knT = dscr("knT", [2, 128, St], BF16)
    kvtm = dscr("kvtm", [St, 512], BF16)
    gbd = dscr("gbd", [St, 16])
    ofd = dscr("ofd", [St, 256])
    ycT = dscr("ycT", [8, 128, St], BF16)

    dres = {}

    def DR(name, i):
        k = (name, i)
        if k not in dres:
            dres[k] = Res("%s%s" % (name, i))
        return dres[k]

    es = ExitStack()
    ACOLS = 50000
    arena_t = es.enter_context(nc.sbuf_tensor("arena", [128, ACOLS], F32))
    A = Arena(arena_t[:], ACOLS)
    banks = []
    for i in range(8):
        pt = es.enter_context(nc.psum_tensor("psb%d" % i, [128, 512], F32))
        banks.append(T(pt[:], Res("bank%d" % i)))

    chan_cnt = {"sync": 0, "gpsimd": 0}

    def chan_of(t, eng):
        if eng not in t.res.chan:
            t.res.chan[eng] = (eng, chan_cnt[eng] % 44)
            chan_cnt[eng] += 1
        return t.res.chan[eng]

    def dma(eng, out, in_, R, W, chan, **kw):
        return S.op(eng, lambda e: e.dma_start(out=out, in_=in_, **kw), R, W, chan=chan)

    def load(t, out, in_, R=(), eng="sync", **kw):
        return dma(eng, out, in_, R, [t.res], chan_of(t, eng), **kw)

    def store(t, out, in_, W, eng="gpsimd", **kw):
        return dma(eng, out, in_, [t.res], W, chan_of(t, eng), **kw)

    def mm(out, lhsT, rhs, start, stop, R, W):
        return S.op("tensor", lambda e: e.matmul(out, lhsT, rhs, start=start, stop=stop), R, W)

    def tr(out, in_, ident, R, W):
        return S.op("tensor", lambda e: e.transpose(out, in_, ident), R, W)

    def act(out, in_, func, R, W, bias=None, scale=None):
        kw = {}
        if bias is not None:
            kw["bias"] = bias
        if scale is not None:
            kw["scale"] = scale
        return S.op("scalar", lambda e: e.activation(out, in_, func, **kw), R, W)

    def tt(eng, out, in0, in1, op, R, W):
        return S.op(eng, lambda e: e.tensor_tensor(out, in0, in1, op), R, W)

    def ts(eng, out, in0, s1, s2, op0, op1, R, W):
        if s2 is None:
            return S.op(eng, lambda e: e.tensor_scalar(out, in0, s1, None, op0), R, W)
        return S.op(eng, lambda e: e.tensor_scalar(out, in0, s1, s2, op0, op1), R, W)

    def stt(eng, out, in0, sc, in1, op0, op1, R, W):
        return S.op(eng, lambda e: e.scalar_tensor_tensor(out, in0, sc, in1, op0, op1), R, W)

    def cp(eng, out, in_, R, W):
        if eng == "scalar":
            return S.op(eng, lambda e: e.copy(out, in_), R, W)
        return S.op(eng, lambda e: e.tensor_copy(out, in_), R, W)

    def memset(eng, out, val, W):
        return S.op(eng, lambda e: e.memset(out, val), (), W)

    def rsqrt_eps(out, in_, eps, Rr, Wr):
        ts("vector", out, in_, eps, None, ALU.add, None, Rr, Wr)
        act(out, out, AF.Sqrt, Wr, Wr)
        S.op("vector", lambda e: e.reciprocal(out, out), Wr, Wr)

    def bc(ap2, n):
        return ap2.unsqueeze(2).broadcast_to([ap2.shape[0], ap2.shape[1], n])

    def bfview(bank):
        return bank.ap.bitcast(BF16)

    ident_f = A.f32(128, "ident_f")
    ident_b = A.bf16(128, "ident_b")
    modcol = A.f32(nl * 96, "modcol")
    modcp1 = A.f32(nl * 96, "modcp1")
    load(ident_f, ident_f.ap, c_ident)
    cp("vector", ident_b.ap, ident_f.ap, [ident_f.res], [ident_b.res])
    mcv = modcol.ap.rearrange("p (l j s) -> p l j s", l=nl, j=48)
    mc1v = modcp1.ap.rearrange("p (l j s) -> p l j s", l=nl, j=48)
    A.mark()

    bankring = Ring(banks)

    def phase_mod():
        A.reset()
        sc = A.f32(16, "sc")
        bm = A.f32(6 * D, "bm")
        mrow = A.f32(6 * D, "mrow")
        wts = Ring([A.f32(8 * 512, "wmod%d" % i) for i in range(2)])
        scv = sc.ap.rearrange("p (k s) -> p k s", k=8)
        load(sc, scv, cT)
        act(sc.ap, sc.ap, AF.Silu, [sc.res], [sc.res])
        mring = Ring(banks[0:6])
        bmc = A.f32(48, "bmc")
        for l in range(nl):
            load(bm, bm.ap[0:2, :], b_mod[l:l + 1, :].broadcast_to([2, 6 * D]))
            load(bmc, bmc.ap, b_modc[l])
            bkc = banks[7]
            for cc in range(12):
                wt = wts.next()
                wv = wt.ap.rearrange("p (k n) -> p k n", k=8)
                load(wt, wv, w_mod[l, :, cc * 512:(cc + 1) * 512].rearrange("(k p) n -> p k n", p=128))
                bk = mring.next()
                for kc in range(8):
                    mm(bk.ap[0:2, :], scv[:, kc, :], wv[:, kc, :], kc == 0, kc == 7,
                       [sc.res, wt.res], [bk.res])
                tt("vector", mrow.ap[0:2, cc * 512:(cc + 1) * 512], bk.ap[0:2, :],
                   bm.ap[0:2, cc * 512:(cc + 1) * 512], ALU.add, [bk.res, bm.res], [mrow.res])
                for j4 in range(4):
                    j = cc * 4 + j4
                    for kc in range(8):
                        mm(bkc.ap[:, 2 * j:2 * j + 2], wv[:, kc, j4 * 128:(j4 + 1) * 128], scv[:, kc, :],
                           kc == 0, kc == 7, [sc.res, wt.res], [bkc.res])
            store(mrow, modrow[l], mrow.ap[0:2, :], [DR("modrow", l)])
            tt("vector", modcol.ap[:, l * 96:(l + 1) * 96].rearrange("p (j s) -> p j s", s=2),
               bkc.ap[:, 0:96].rearrange("p (j s) -> p j s", s=2), bc(bmc.ap, 2), ALU.add,
               [bkc.res, bmc.res], [modcol.res])
        ts("vector", modcp1.ap, modcol.ap, 1.0, None, ALU.add, None, [modcol.res], [modcp1.res])
        if cfg.dbg:
            dbgmc = dscr("dbg_modcol", [128, nl * 96])
            store(modcp1, dbgmc, modcp1.ap, [DR("dbgmc", 0)])
        S.barrier()

    def layer_norm(r, out, gb, bb, small, scratch, eng2="gpsimd"):
        st = small.ap[:, 0:12].rearrange("p (c s) -> p c s", c=2)
        for c in range(2):
            S.op("vector", (lambda o, i: (lambda e: e.bn_stats(o, i)))(st[:, c, :], r.ap[:, c * 512:(c + 1) * 512]),
                 [r.res], [small.res])
        mv = small.ap[:, 12:14]
        S.op("vector", lambda e: e.bn_aggr(mv, small.ap[:, 0:12]), [small.res], [small.res])
        rstd = small.ap[:, 14:15]
        nb = small.ap[:, 15:16]
        rsqrt_eps(rstd, small.ap[:, 13:14], LN_EPS, [small.res], [small.res])
        stt("vector", nb, small.ap[:, 12:13], -1.0, rstd, ALU.mult, ALU.mult, [small.res], [small.res])
        act(scratch.ap, r.ap, AF.Identity, [r.res, small.res], [scratch.res], bias=nb, scale=rstd)
        tt(eng2, scratch.ap, scratch.ap, gb.ap, ALU.mult, [scratch.res, gb.res], [scratch.res])
        tt(eng2, out.ap, scratch.ap, bb.ap, ALU.add, [scratch.res, bb.res], [out.res])

    def bload(t, src_row):
        n = src_row.shape[-1]
        load(t, t.ap, src_row.broadcast_to([128, n]))

    def phase_P(l):
        A.reset()
        cur = xbuf[l % 2]
        Wt = A.bf16(8 * DIN, "Wt")
        Wv = Wt.ap.rearrange("p (k n) -> p k n", k=8)
        for kc in range(8):
            load(Wt, Wv[:, kc, :], w_in[l, kc * 128:(kc + 1) * 128, :], eng="gpsimd")
        if l == 0:
            g_in = A.f32(D, "g_in")
            b_in = A.f32(D, "b_in")
            bload(g_in, ln_in_g)
            bload(b_in, ln_in_b)
            lnscr = A.f32(D, "lnscr")
            smalls = Ring([A.f32(24, "small%d" % i) for i in range(2)])
        xr = Ring([A.f32(D, "xg%d" % i) for i in range(8)])
        hr = Ring([A.bf16(8 * 512, "hT%d" % i) for i in range(2)])
        ptr = Ring([A.bf16(512, "pt%d" % i) for i in range(4)])
        cfr = Ring([A.f32(512, "cf%d" % i) for i in range(2)])
        vr = Ring([A.bf16(132, "va%d" % i) for i in range(2)])
        zr = Ring([A.f32(272, "zg%d" % i) for i in range(2)])
        for v in vr.tiles:
            memset("vector", v.ap, 1.0, [v.res])
        for si, (soff, slen) in enumerate(cfg.seqs):
            src = (xs, xp)[si]
            for gi in range(slen // 512):
                t0 = soff + gi * 512
                xg = []
                for b in range(4):
                    x = xr.next()
                    tb = t0 + b * 128
                    if l == 0:
                        load(x, x.ap, src[tb - soff:tb - soff + 128, :])
                        layer_norm(x, x, g_in, b_in, smalls.next(), lnscr)
                        store(x, cur[tb:tb + 128, :], x.ap, [DR("x%d" % (l % 2), tb // 128)])
                    else:
                        load(x, x.ap, cur[tb:tb + 128, :], R=[DR("x%d" % (l % 2), tb // 128)])
                    xg.append(x)
                import os as _os
                pstop = int(_os.environ.get("PSTOP", "9"))
                if pstop == 0:
                    continue
                hT = hr.next()
                hv = hT.ap.rearrange("p (k n) -> p k n", k=8)
                for kc in range(8):
                    bk = bankring.next()
                    for b in range(4):
                        tr(bk.ap[:, b * 128:(b + 1) * 128], xg[b].ap[:, kc * 128:(kc + 1) * 128], ident_f.ap,
                           [xg[b].res, ident_f.res], [bk.res])
                    act(hv[:, kc, :], bk.ap, AF.Identity, [bk.res, modcol.res, modcp1.res], [hT.res],
                        bias=mcv[:, l, 0 * 8 + kc, si:si + 1], scale=mc1v[:, l, 1 * 8 + kc, si:si + 1])
                if pstop == 1:
                    continue
                cu_c = {}
                for m in range(NFM):
                    bk = bankring.next()
                    for kc in range(8):
                        mm(bk.ap, Wv[:, kc, m * 128:(m + 1) * 128], hv[:, kc, :], kc == 0, kc == 7,
                           [Wt.res, hT.res], [bk.res])
                    if m in (2, 3):
                        cf = cfr.next()
                        cp("scalar", cf.ap, bk.ap, [bk.res], [cf.res])
                        cu_c[m] = cf
                        continue
                    pt = ptr.next()
                    if m in (4, 5):
                        cf = cu_c[m - 2]
                        tt("vector", pt.ap, bk.ap, cf.ap, ALU.mult, [bk.res, cf.res], [pt.res])
                        slot = m - 2
                    else:
                        if m % 2 == 0:
                            cp("scalar", pt.ap, bk.ap, [bk.res], [pt.res])
                        else:
                            cp("vector", pt.ap, bk.ap, [bk.res], [pt.res])
                        slot = m if m < 2 else m - 2
                    store(pt, PT[slot, :, t0:t0 + 512], pt.ap, [DR("PT%d_" % slot, t0 // 512)])
                if pstop == 2:
                    continue
                for b in range(4):
                    tb = t0 + b * 128
                    bk = bankring.next()
                    for kc in range(8):
                        mm(bk.ap[:, 0:NTM], hv[:, kc, b * 128:(b + 1) * 128], Wv[:, kc, NFM * 128:DIN],
                           kc == 0, kc == 7, [Wt.res, hT.res], [bk.res])
                    if pstop == 5:
                        continue
                    va = vr.next()
                    vv = va.ap.rearrange("p (g c) -> p g c", g=2)
                    cp("vector", vv[:, :, 0:64], bk.ap[:, 0:128].rearrange("p (g c) -> p g c", g=2),
                       [bk.res], [va.res])
                    if pstop == 6:
                        continue
                    zg = zr.next()
                    cp("vector", zg.ap, bk.ap[:, 128:NTM], [bk.res], [zg.res])
                    if pstop == 3:
                        continue
                    store(va, Pv[tb:tb + 128, :], va.ap, [DR("Pv", tb // 128)])
                    if pstop == 4:
                        continue
                    store(zg, Pzg[tb:tb + 128, :], zg.ap, [DR("Pzg", tb // 128)])
        S.barrier()

    def halo_load(t, view, slot_ap, t0, soff, slen, R_name, nblk512=True):
        lo = t0 - 1
        hi = t0 + 513
        a = 0
        b_ = 514
        if t0 == soff:
            memset("gpsimd", view[:, 0:1], 0.0, [t.res])
            lo = t0
            a = 1
        if t0 + 512 == soff + slen:
            memset("gpsimd", view[:, 513:514], 0.0, [t.res])
            hi = t0 + 512
            b_ = 513
        rs = [DR(R_name, g) for g in range(max(lo, soff) // 512, (hi - 1) // 512 + 1)]
        load(t, view[:, a:b_], slot_ap[:, lo:hi], R=rs)

    def phase_MA(l):
        A.reset()
        wa = A.f32(6, "wa")
        wav = wa.ap.rearrange("p (j k) -> p j k", j=2)
        load(wa, wav, conv_a[l])
        cur_ = Ring([A.bf16(2 * 514, "cu%d" % i) for i in range(2)])
        br = Ring([A.bf16(2 * 512, "bg%d" % i) for i in range(2)])
        yr = Ring([A.f32(512, "ya%d" % i) for i in range(2)])
        yor = Ring([A.bf16(512, "yo%d" % i) for i in range(2)])
        for si, (soff, slen) in enumerate(cfg.seqs):
            for gi in range(slen // 512):
                t0 = soff + gi * 512
                cu = cur_.next()
                cuv = cu.ap.rearrange("p (j n) -> p j n", j=2)
                bg = br.next()
                bgv = bg.ap.rearrange("p (j n) -> p j n", j=2)
                for j in range(2):
                    halo_load(cu, cuv[:, j, :], PT[2 + j], t0, soff, slen, "PT%d_" % (2 + j))
                    load(bg, bgv[:, j, :], PT[j, :, t0:t0 + 512], R=[DR("PT%d_" % j, t0 // 512)])
                for j in range(2):
                    y = yr.next()
                    ts("vector", y.ap, cuv[:, j, 0:512], wav[:, j, 0:1], None, ALU.mult, None,
                       [cu.res, wa.res], [y.res])
                    stt("vector", y.ap, cuv[:, j, 1:513], wav[:, j, 1:2], y.ap, ALU.mult, ALU.add,
                        [cu.res, wa.res, y.res], [y.res])
                    stt("vector", y.ap, cuv[:, j, 2:514], wav[:, j, 2:3], y.ap, ALU.mult, ALU.add,
                        [cu.res, wa.res, y.res], [y.res])
                    yo = yor.next()
                    tt("gpsimd", yo.ap, y.ap, bgv[:, j, :], ALU.mult, [y.res, bg.res], [yo.res])
                    store(yo, ycT[j, :, t0:t0 + 512], yo.ap, [DR("ycT%d_" % j, t0 // 512)])
        S.barrier()

    def phase_MB(l):
        A.reset()
        em = A.bf16(8 * 3 * 128, "emask")
        emv = em.ap.rearrange("p (h r q) -> p h r q", h=8, r=3)
        load(em, em.ap, c_emask)
        esk = A.f32(8, "esink")
        bload(esk, sink[l:l + 1, :])
        act(esk.ap, esk.ap, AF.Exp, [esk.res], [esk.res])
        qr = Ring([A.bf16(4 * 512, "qT%d" % i) for i in range(2)])
        kr = Ring([A.bf16(768, "kT%d" % i) for i in range(2)])
        vrr = Ring([A.bf16(6 * 132, "vg%d" % i) for i in range(2)])
        per = Ring([A.bf16(512, "pe%d" % i) for i in range(3)])
        pmr = Ring([A.bf16(512, "pm%d" % i) for i in range(3)])
        dnr = Ring([A.f32(16, "den%d" % i) for i in range(2)])
        ybr = Ring([A.bf16(512, "yb%d" % i) for i in range(2)])
        ytr = Ring([A.bf16(4 * 512, "yt%d" % i) for i in range(2)])
        sring = Ring(banks[0:5])
        oring = Ring([(banks[5], banks[6])])
        tbank = banks[7]
        for si, (soff, slen) in enumerate(cfg.seqs):
            nblk = slen // 128
            for gi in range(slen // 512):
                t0 = soff + gi * 512
                qT = qr.next()
                qv = qT.ap.rearrange("p (m n) -> p m n", m=4)
                load(qT, qv, PT[4:8, :, t0:t0 + 512].rearrange("m p t -> p m t"),
                     R=[DR("PT%d_" % s_, t0 // 512) for s_ in range(4, 8)])
                kT = kr.next()
                lo = max(t0 - 128, soff)
                hi = min(t0 + 640, soff + slen)
                load(kT, kT.ap[:, lo - (t0 - 128):hi - (t0 - 128)], PT[8, :, lo:hi],
                     R=[DR("PT8_", g) for g in range(lo // 512, (hi - 1) // 512 + 1)])
                vg = vrr.next()
                vgv = vg.ap.rearrange("p (j c) -> p j c", j=6)
                j0 = (lo - (t0 - 128)) // 128
                j1 = (hi - (t0 - 128)) // 128
                load(vg, vgv[:, j0:j1, :], Pv[lo:hi, :].rearrange("(j p) c -> p j c", p=128),
                     R=[DR("Pv", b_) for b_ in range(lo // 128, hi // 128)])
                yT = ytr.next()
                ytv = yT.ap.rearrange("p (m n) -> p m n", m=4)
                for b in range(4):
                    ib = gi * 4 + b
                    rels = [r for r in range(3) if 0 <= ib + r - 1 < nblk]
                    ob = oring.next()
                    for g in range(2):
                        ps_ = slice(g * 64, (g + 1) * 64)
                        obv = ob[g].ap[:, 0:264].rearrange("p (h c) -> p h c", h=4)
                        for r in rels:
                            jb = b + r
                            sb = sring.next()
                            mm(sb.ap.rearrange("p (m n) -> p m n", m=4), kT.ap[ps_, jb * 128:(jb + 1) * 128],
                               qv[ps_, :, b * 128:(b + 1) * 128], True, True, [kT.res, qT.res], [sb.res])
                            pe = per.next()
                            act(pe.ap, sb.ap, AF.Exp, [sb.res], [pe.res], scale=0.125)
                            pm = pmr.next()
                            tt("gpsimd", pm.ap.rearrange("p (h q) -> p h q", h=4),
                               pe.ap.rearrange("p (h q) -> p h q", h=4), emv[:, g * 4:(g + 1) * 4, r, :],
                               ALU.mult, [pe.res, em.res], [pm.res])
                            for hl in range(4):
                                mm(obv[:, hl, :], pm.ap[:, hl * 128:(hl + 1) * 128],
                                   vgv[:, jb, g * 66:(g + 1) * 66], (r == rels[0] and hl == 0), r == rels[-1],
                                   [pm.res, vg.res], [ob[g].res])
                    den = dnr.next()
                    yb = ybr.next()
                    for g in range(2):
                        obv = ob[g].ap[:, 0:264].rearrange("p (h c) -> p h c", h=4)
                        tt("vector", den.ap[:, g * 4:(g + 1) * 4], obv[:, :, 64], esk.ap[:, g * 4:(g + 1) * 4],
                           ALU.add, [ob[g].res, esk.res], [den.res])
                    S.op("vector", (lambda o, i: (lambda e: e.reciprocal(o, i)))(den.ap[:, 8:16], den.ap[:, 0:8]),
                         [den.res], [den.res])
                    for g in range(2):
                        obv = ob[g].ap[:, 0:264].rearrange("p (h c) -> p h c", h=4)
                        tt("vector", yb.ap[:, g * 256:(g + 1) * 256].rearrange("p (h c) -> p h c", h=4),
                           obv[:, :, 0:64], bc(den.ap[:, 8 + g * 4:8 + (g + 1) * 4], 64), ALU.mult,
                           [ob[g].res, den.res], [yb.res])
                    tv = bfview(tbank)
                    for c in range(4):
                        tr(tv[:, c * 128:(c + 1) * 128], yb.ap[:, c * 128:(c + 1) * 128], ident_b.ap,
                           [yb.res, ident_b.res], [tbank.res])
                    cp("scalar", ytv[:, :, b * 128:(b + 1) * 128],
                       tv[:, 0:512].rearrange("p (m n) -> p m n", m=4), [tbank.res], [yT.res])
                store(yT, ycT[2:6, :, t0:t0 + 512].rearrange("m p t -> p m t"), ytv,
                      [DR("ycT%d_" % s_, t0 // 512) for s_ in range(2, 6)])
        S.barrier()

    def phase_MC0(l):
        A.reset()
        wd = A.f32(18, "wd")
        wdv = wd.ap.rearrange("p (j k) -> p j k", j=6)
        load(wd, wdv, dn_conv[l])
        bon = A.f32(128, "bones")
        load(bon, bon.ap, c_bones)
        nega = A.f32(8, "nega")
        bload(nega, a_log[l:l + 1, :])
        act(nega.ap, nega.ap, AF.Exp, [nega.res], [nega.res])
        ts("vector", nega.ap, nega.ap, -1.0, None, ALU.mult, None, [nega.res], [nega.res])
        dtb = A.f32(8, "dtb")
        bload(dtb, dt_bias[l:l + 1, :])
        xr_ = Ring([A.bf16(6 * 514, "xin%d" % i) for i in range(2)])
        yr = Ring([A.f32(6 * 512, "cy%d" % i) for i in range(2)])
        sqr = Ring([A.f32(512, "sq%d" % i) for i in range(2)])
        rsr = Ring([A.f32(512, "rs%d" % i) for i in range(2)])
        qkr = Ring([A.bf16(6 * 512, "qk%d" % i) for i in range(2)])
        kvr = Ring([A.bf16(512, "kv%d" % i) for i in range(2)])
        gr = Ring([A.f32(4 * 16, "gt%d" % i) for i in range(2)])
        gor = Ring([A.f32(4 * 16, "go%d" % i) for i in range(2)])
        tbr = Ring(banks[6:8])
        nbr = Ring(banks[0:6])
        for si, (soff, slen) in enumerate(cfg.seqs):
            for gi in range(slen // 512):
                t0 = soff + gi * 512
                xin = xr_.next()
                xv = xin.ap.rearrange("p (j n) -> p j n", j=6)
                for j in range(6):
                    halo_load(xin, xv[:, j, :], PT[9 + j], t0, soff, slen, "PT%d_" % (9 + j))
                y = yr.next()
                yv = y.ap.rearrange("p (j n) -> p j n", j=6)
                for j in range(6):
                    eng = "vector"
                    ts(eng, yv[:, j, :], xv[:, j, 0:512], wdv[:, j, 0:1], None, ALU.mult, None,
                       [xin.res, wd.res], [y.res])
                    stt(eng, yv[:, j, :], xv[:, j, 1:513], wdv[:, j, 1:2], yv[:, j, :], ALU.mult, ALU.add,
                        [xin.res, wd.res, y.res], [y.res])
                    stt(eng, yv[:, j, :], xv[:, j, 2:514], wdv[:, j, 2:3], yv[:, j, :], ALU.mult, ALU.add,
                        [xin.res, wd.res, y.res], [y.res])
                act(y.ap, y.ap, AF.Silu, [y.res], [y.res])
                qk = qkr.next()
                qkv = qk.ap.rearrange("p (j n) -> p j n", j=6)
                for j in range(4):
                    sq = sqr.next()
                    tt("gpsimd", sq.ap, yv[:, j, :], yv[:, j, :], ALU.mult, [y.res], [sq.res])
                    bk = nbr.next()
                    mm(bk.ap, bon.ap, sq.ap, True, True, [bon.res, sq.res], [bk.res])
                    rs = rsr.next()
                    rsqrt_eps(rs.ap, bk.ap, RMS_EPS, [bk.res], [rs.res])
                    if j < 2:
                        stt("vector", qkv[:, j, :], yv[:, j, :], 0.125, rs.ap, ALU.mult, ALU.mult,
                            [y.res, rs.res], [qk.res])
                    else:
                        tt("vector", qkv[:, j, :], yv[:, j, :], rs.ap, ALU.mult, [y.res, rs.res], [qk.res])
                cp("gpsimd", qkv[:, 4:6, :], yv[:, 4:6, :], [y.res], [qk.res])
                store(qk, qnT[:, :, t0:t0 + 512].rearrange("m p t -> p m t"), qkv[:, 0:2, :],
                      [DR("qnT", t0 // 512)])
                store(qk, knT[:, :, t0:t0 + 512].rearrange("m p t -> p m t"), qkv[:, 2:4, :],
                      [DR("knT", t0 // 512)])
                gt = gr.next()
                gtv = gt.ap.rearrange("p (b c) -> p b c", b=4)
                load(gt, gtv, Pzg[t0:t0 + 512, 256:272].rearrange("(b p) c -> p b c", p=128),
                     R=[DR("Pzg", t0 // 128 + b_) for b_ in range(4)])
                go = gor.next()
                gov = go.ap.rearrange("p (b c) -> p b c", b=4)
                tt("vector", gov[:, :, 0:8], gtv[:, :, 0:8], dtb.ap.unsqueeze(1).broadcast_to([128, 4, 8]),
                   ALU.add, [gt.res, dtb.res], [go.res])
                act(gov[:, :, 0:8], gov[:, :, 0:8], AF.Exp, [go.res], [go.res])
                act(gov[:, :, 0:8], gov[:, :, 0:8], AF.Ln, [go.res], [go.res], bias=1.0)
                tt("vector", gov[:, :, 0:8], gov[:, :, 0:8], nega.ap.unsqueeze(1).broadcast_to([128, 4, 8]),
                   ALU.mult, [go.res, nega.res], [go.res])
                act(gov[:, :, 8:16], gtv[:, :, 8:16], AF.Sigmoid, [gt.res], [go.res])
                store(go, gbd[t0:t0 + 512, :].rearrange("(b p) c -> p b c", p=128), gov,
                      [DR("gbd", t0 // 128 + b_) for b_ in range(4)])
                for b in range(4):
                    tb = t0 + b * 128
                    tbk = tbr.next()
                    tv = bfview(tbk)
                    for j in range(4):
                        tr(tv[:, j * 128:(j + 1) * 128], qkv[:, 2 + j, b * 128:(b + 1) * 128], ident_b.ap,
                           [qk.res, ident_b.res], [tbk.res])
                    kv = kvr.next()
                    cp("scalar", kv.ap, tv[:, 0:512], [tbk.res], [kv.res])
                    store(kv, kvtm[tb:tb + 128, :], kv.ap, [DR("kvtm", tb // 128)])
        S.barrier()

    def phase_MC(l, d):
        A.reset()
        U = A.f32(128, "U")
        load(U, U.ap, c_U[d])
        ones = A.f32(128, "ones")
        load(ones, ones.ap, c_ones)
        PM = A.f32(128, "PM")
        load(PM, PM.ap, c_PM[d])
        NM = A.f32(128, "NM")
        load(NM, NM.ap, c_NM[d])
        mk_f = A.f32(512, "mk_f")
        load(mk_f, mk_f.ap, c_blk)
        B16 = A.bf16(128, "B16")
        MK1 = A.bf16(128, "MK1")
        MK2 = A.bf16(128, "MK2")
        MK3 = A.bf16(128, "MK3")
        for i_, t_ in enumerate((B16, MK1, MK2, MK3)):
            cp("vector", t_.ap, mk_f.ap[:, i_ * 128:(i_ + 1) * 128], [mk_f.res], [t_.res])
        Mr = Ring([A.bf16(512, "Mr%d" % i) for i in range(24)])
        Ub = A.bf16(128, "Ub")
        cp("vector", Ub.ap, U.ap, [U.res], [Ub.res])
        onesb = A.bf16(128, "onesb")
        memset("vector", onesb.ap, 1.0, [onesb.res])
        nonesb = A.bf16(128, "nonesb")
        memset("vector", nonesb.ap, -1.0, [nonesb.res])
        dgr = Ring([A.bf16(1024, "Dg%d" % i) for i in range(2)])
        ghr = Ring([A.bf16(16, "gh%d" % i) for i in range(2)])
        if d == 1:
            ng8 = A.f32(256, "ng8")
            bload(ng8, norm_g[l:l + 1, :])
            ts("vector", ng8.ap, ng8.ap, 8.0, None, ALU.mult, None, [ng8.res], [ng8.res])
        Sf = A.f32(256, "Sf")
        Sb = A.bf16(256, "Sb")
        Sfv = Sf.ap.rearrange("p (h n) -> p h n", h=4)
        Sbv = Sb.ap.rearrange("p (h n) -> p h n", h=4)
        knr = Ring([A.bf16(256, "kn%d" % i) for i in range(2)])
        qnr = Ring([A.bf16(256, "qn%d" % i) for i in range(2)])
        knzr = Ring([A.bf16(512, "knz%d" % i) for i in range(2)])
        bon = A.f32(128, "bonesMC")
        load(bon, bon.ap, c_bones)
        kvr = Ring([A.bf16(512, "kvl%d" % i) for i in range(2)])
        gbr = Ring([A.f32(16, "gb%d" % i) for i in range(2)])
        g8r = Ring([A.f32(24, "g8%d" % i) for i in range(2)])
        grr = Ring([A.f32(128, "grow%d" % i) for i in range(2)])
        x1r = Ring([A.f32(512, "x1%d" % i) for i in range(2)])
        dsr = Ring([A.f32(512, "dst%d" % i) for i in range(2)])
        dtr = Ring([A.f32(512, "dti%d" % i) for i in range(2)])
        tmr = Ring([A.f32(512, "tmpf%d" % i) for i in range(2)])
        Lr = Ring([A.bf16(512, "L%d" % i) for i in range(3)])
        LTr = Ring([A.bf16(512, "LT%d" % i) for i in range(3)])
        ATr = Ring([A.bf16(512, "AT%d" % i) for i in range(2)])
        Xr = Ring([A.bf16(512, "X%d" % i) for i in range(3)])
        wTr = Ring([A.bf16(512, "wT%d" % i) for i in range(2)])
        kdr = Ring([A.bf16(256, "kd%d" % i) for i in range(2)])
        vnr = Ring([A.bf16(256, "vn%d" % i) for i in range(2)])
        smr = Ring([A.f32(24, "sm%d" % i) for i in range(2)])
        o1r = Ring([A.f32(256, "o1%d" % i) for i in range(2)])
        oor = Ring([A.f32(256, "oo%d" % i) for i in range(2)])
        if d == 1:
            ofr = Ring([A.f32(256, "ofl%d" % i) for i in range(2)])
            zr = Ring([A.f32(256, "zl%d" % i) for i in range(2)])
            sqr = Ring([A.f32(256, "osq%d" % i) for i in range(2)])
            ycr = Ring([A.bf16(256, "yc%d" % i) for i in range(2)])
            yctr = Ring([A.bf16(256, "yct%d" % i) for i in range(2)])
        bring = Ring(banks[0:7])
        tbank = banks[7]
        for si, (soff, slen) in enumerate(cfg.seqs):
            nblk = slen // 128
            memset("vector", Sf.ap, 0.0, [Sf.res])
            memset("vector", Sb.ap, 0.0, [Sb.res])
            order = range(nblk) if d == 0 else range(nblk - 1, -1, -1)
            for ib in order:
                tb = soff + ib * 128
                kn = knr.next()
                knv = kn.ap.rearrange("p (c n) -> p c n", c=2)
                load(kn, knv, knT[:, :, tb:tb + 128].rearrange("m p t -> p m t"), R=[DR("knT", tb // 512)])
                qn = qnr.next()
                qnv = qn.ap.rearrange("p (c n) -> p c n", c=2)
                load(qn, qnv, qnT[:, :, tb:tb + 128].rearrange("m p t -> p m t"), R=[DR("qnT", tb // 512)])
                kv = kvr.next()
                load(kv, kv.ap, kvtm[tb:tb + 128, :], R=[DR("kvtm", tb // 128)])
                k4 = kv.ap[:, 0:256].rearrange("p (h c) -> p h c", h=4)
                v4 = kv.ap[:, 256:512].rearrange("p (h c) -> p h c", h=4)
                gb = gbr.next()
                load(gb, gb.ap, gbd[tb:tb + 128, :], R=[DR("gbd", tb // 128)])
                g_d = gb.ap[:, d * 4:(d + 1) * 4]
                beta = gb.ap[:, 8 + d * 4:8 + (d + 1) * 4]
                mcstop = int(_os.environ.get('MCSTOP', '99'))
                if mcstop == -1:
                    continue
                mcstop = int(_os.environ.get('MCSTOP', '99'))
                gh = ghr.next()
                cp("vector", gh.ap[:, 0:4], g_d, [gb.res], [gh.res])
                tt("vector", gh.ap[:, 4:8], g_d, gh.ap[:, 0:4], ALU.subtract, [gb.res, gh.res], [gh.res])
                bA = bring.next()
                mm(bA.ap[:, 0:4], Ub.ap, gh.ap[:, 0:4], True, False, [Ub.res, gh.res], [bA.res])
                mm(bA.ap[:, 0:4], Ub.ap, gh.ap[:, 4:8], False, True, [Ub.res, gh.res], [bA.res])
                mm(bA.ap[:, 4:8], onesb.ap, gh.ap[:, 0:4], True, False, [onesb.res, gh.res], [bA.res])
                mm(bA.ap[:, 4:8], onesb.ap, gh.ap[:, 4:8], False, True, [onesb.res, gh.res], [bA.res])
                g8 = g8r.next()
                cp("vector", g8.ap[:, 0:8], bA.ap[:, 0:8], [bA.res], [g8.res])
                if mcstop == 0:
                    continue
                cp("vector", gh.ap[:, 8:12], g8.ap[:, 0:4], [g8.res], [gh.res])
                tt("vector", gh.ap[:, 12:16], g8.ap[:, 0:4], gh.ap[:, 8:12], ALU.subtract, [g8.res, gh.res], [gh.res])
                Dg = dgr.next()
                Dgv = Dg.ap.rearrange("p (a h n) -> p a h n", a=2, h=4)
                for a_ in range(2):
                    tt("gpsimd", Dgv[:, a_, :, :], ident_b.ap.unsqueeze(1).broadcast_to([128, 4, 128]),
                       bc(gh.ap[:, 8 + 4 * a_:12 + 4 * a_], 128), ALU.mult, [ident_b.res, gh.res], [Dg.res])
                bE = bring.next()
                for h in range(4):
                    hs = slice(h * 128, (h + 1) * 128)
                    mm(bE.ap[:, hs], onesb.ap, Dgv[:, 0, h, :], True, False, [onesb.res, Dg.res], [bE.res])
                    mm(bE.ap[:, hs], onesb.ap, Dgv[:, 1, h, :], False, False, [onesb.res, Dg.res], [bE.res])
                    mm(bE.ap[:, hs], Dgv[:, 0, h, :], nonesb.ap, False, False, [nonesb.res, Dg.res], [bE.res])
                    mm(bE.ap[:, hs], Dgv[:, 1, h, :], nonesb.ap, False, True, [nonesb.res, Dg.res], [bE.res])
                bEv = bE.ap.rearrange("p (h n) -> p h n", h=4)
                x1 = x1r.next()
                tt("vector", x1.ap.rearrange("p (h n) -> p h n", h=4), bEv,
                   PM.ap.unsqueeze(1).broadcast_to([128, 4, 128]), ALU.max, [bE.res, PM.res], [x1.res])
                dst = dsr.next()
                act(dst.ap, x1.ap, AF.Exp, [x1.res], [dst.res], scale=-1.0)
                x2 = x1r.next()
                tt("vector", x2.ap.rearrange("p (h n) -> p h n", h=4), bEv,
                   NM.ap.unsqueeze(1).broadcast_to([128, 4, 128]), ALU.min, [bE.res, NM.res], [x2.res])
                dti = dtr.next()
                act(dti.ap, x2.ap, AF.Exp, [x2.res], [dti.res])
                if mcstop == 1:
                    continue
                knz = knzr.next()
                knzv = knz.ap.rearrange("p (r c n) -> p r c n", r=2, c=2)
                for r in range(2):
                    ts("gpsimd", knz.ap[:, r * 256:(r + 1) * 256], kn.ap, bon.ap[:, r * 64:r * 64 + 1], None,
                       ALU.mult, None, [kn.res, bon.res], [knz.res])
                bK = bring.next()
                bQ = bring.next()
                for h in range(4):
                    c = h // 2
                    r = h % 2
                    mm(bK.ap[:, h * 128:(h + 1) * 128], knzv[:, r, c, :], knv[:, c, :], True, True,
                       [knz.res, kn.res], [bK.res])
                    mm(bQ.ap[:, h * 128:(h + 1) * 128], knzv[:, r, c, :], qnv[:, c, :], True, True,
                       [knz.res, qn.res], [bQ.res])
                if mcstop == 20:
                    continue
                tmpf = tmr.next()
                tt("vector", tmpf.ap, bK.ap, dst.ap, ALU.mult, [bK.res, dst.res], [tmpf.res])
                if mcstop == 21:
                    continue
                Lm = Lr.next()
                tt("gpsimd", Lm.ap.rearrange("p (h n) -> p h n", h=4), tmpf.ap.rearrange("p (h n) -> p h n", h=4),
                   bc(beta, 128), ALU.mult, [tmpf.res, gb.res], [Lm.res])
                if mcstop == 22:
                    continue
                AT = ATr.next()
                tt("vector", AT.ap, bQ.ap, dti.ap, ALU.mult, [bQ.res, dti.res], [AT.res])
                if mcstop == 2:
                    continue
                tv = bfview(tbank)
                for h in range(4):
                    tr(tv[:, h * 128:(h + 1) * 128], Lm.ap[:, h * 128:(h + 1) * 128], ident_b.ap,
                       [Lm.res, ident_b.res], [tbank.res])
                LT = LTr.next()
                cp("scalar", LT.ap, tv[:, 0:512], [tbank.res], [LT.res])
                if mcstop == 3:
                    continue
                act(g8.ap[:, 8:12], g8.ap[:, 0:4], AF.Exp, [g8.res], [g8.res])
                tt("vector", g8.ap[:, 12:16], g8.ap[:, 8:12], beta, ALU.mult, [g8.res, gb.res], [g8.res])
                tt("vector", g8.ap[:, 16:20], g8.ap[:, 4:8], g8.ap[:, 0:4], ALU.subtract, [g8.res], [g8.res])
                act(g8.ap[:, 16:20], g8.ap[:, 16:20], AF.Exp, [g8.res], [g8.res])
                gt2 = g8.ap[:, 4:8].rearrange("p (c r) -> p c r", r=2)
                for r in range(2):
                    rs_ = slice(r * 64, (r + 1) * 64)
                    act(g8.ap[rs_, 20:22], gt2[rs_, :, r], AF.Exp, [g8.res], [g8.res])
                if mcstop == 4:
                    continue
                X = Xr.next()
                k5 = kv.ap[:, 0:256].rearrange("p (c r n) -> p c r n", c=2, r=2)
                v5 = kv.ap[:, 256:512].rearrange("p (c r n) -> p c r n", c=2, r=2)
                beta5 = beta.rearrange("p (c r) -> p c r", r=2)
                bg5 = g8.ap[:, 12:16].rearrange("p (c r) -> p c r", r=2)

                def x5(Xt):
                    return Xt.ap.rearrange("p (c r n) -> p c r n", c=2, r=2)

                for r in range(2):
                    uo = 64 if r == 0 else 0
                    wo = 0 if r == 0 else 64
                    tt("gpsimd", x5(X)[:, :, r, uo:uo + 64], v5[:, :, r, :], bc(beta5[:, :, r], 64), ALU.mult,
                       [kv.res, gb.res], [X.res])
                    tt("gpsimd", x5(X)[:, :, r, wo:wo + 64], k5[:, :, r, :], bc(bg5[:, :, r], 64), ALU.mult,
                       [kv.res, g8.res], [X.res])
                if mcstop == 5:
                    continue
                def h4(t):
                    return t.ap.rearrange("p (h n) -> p h n", h=4)

                def mbc(m):
                    return m.ap.unsqueeze(1).broadcast_to([128, 4, 128])

                def hmm(lhsT_t, rhs_t):
                    bkx = bring.next()
                    for h in range(4):
                        hs = slice(h * 128, (h + 1) * 128)
                        mm(bkx.ap[:, hs], lhsT_t.ap[:, hs], rhs_t.ap[:, hs], True, True,
                           [lhsT_t.res, rhs_t.res], [bkx.res])
                    return bkx

                L0 = Mr.next()
                tt("gpsimd", h4(L0), h4(Lm), mbc(B16), ALU.mult, [Lm.res, B16.res], [L0.res])
                L0T = Mr.next()
                tt("gpsimd", h4(L0T), h4(LT), mbc(B16), ALU.mult, [LT.res, B16.res], [L0T.res])
                Pk, PkT = L0, L0T
                Ps = []
                for j in range(3):
                    bM = hmm(PkT, Pk)
                    bMT = hmm(Pk, PkT)
                    Pn = Mr.next()
                    cp("scalar", Pn.ap, bM.ap, [bM.res], [Pn.res])
                    PnT = Mr.next()
                    cp("scalar", PnT.ap, bMT.ap, [bMT.res], [PnT.res])
                    Ps.append((Pn, PnT))
                    Pk, PkT = Pn, PnT
                G = Mr.next()
                tt("gpsimd", h4(G), mbc(ident_b), h4(L0T), ALU.subtract, [ident_b.res, L0T.res], [G.res])
                H = Mr.next()
                tt("gpsimd", h4(H), mbc(ident_b), h4(L0), ALU.subtract, [ident_b.res, L0.res], [H.res])
                for (Pn, PnT) in Ps:
                    bG = hmm(Pn, G)
                    G2 = Mr.next()
                    tt("vector", G2.ap, G.ap, bG.ap, ALU.add, [G.res, bG.res], [G2.res])
                    G = G2
                    bH = hmm(PnT, H)
                    H2 = Mr.next()
                    tt("vector", H2.ap, H.ap, bH.ap, ALU.add, [H.res, bH.res], [H2.res])
                    H = H2
                Dm, DT = H, G
                for lev, Mk in enumerate((MK1, MK2, MK3)):
                    lastl = lev == 2
                    Ck = Mr.next()
                    tt("gpsimd", h4(Ck), h4(Lm), mbc(Mk), ALU.mult, [Lm.res, Mk.res], [Ck.res])
                    bY = hmm(Ck, DT)
                    YT = Mr.next()
                    cp("scalar", YT.ap, bY.ap, [bY.res], [YT.res])
                    bZ = hmm(Dm, YT)
                    DT2 = Mr.next()
                    tt("vector", DT2.ap, DT.ap, bZ.ap, ALU.subtract, [DT.res, bZ.res], [DT2.res])
                    if not lastl:
                        CkT = Mr.next()
                        tt("gpsimd", h4(CkT), h4(LT), mbc(Mk), ALU.mult, [LT.res, Mk.res], [CkT.res])
                        bY2 = hmm(CkT, Dm)
                        Y = Mr.next()
                        cp("scalar", Y.ap, bY2.ap, [bY2.res], [Y.res])
                        bZ2 = hmm(DT, Y)
                        D2 = Mr.next()
                        tt("vector", D2.ap, Dm.ap, bZ2.ap, ALU.subtract, [Dm.res, bZ2.res], [D2.res])
                        Dm = D2
                    DT = DT2
                bX = hmm(DT, X)
                X2 = Xr.next()
                cp("scalar", X2.ap, bX.ap, [bX.res], [X2.res])
                X = X2
                if mcstop == 6:
                    continue
                tv = bfview(tbank)
                for h in range(4):
                    tr(tv[:, h * 128:(h + 1) * 128], X.ap[:, h * 128:(h + 1) * 128], ident_b.ap,
                       [X.res, ident_b.res], [tbank.res])
                wT = wTr.next()
                wTv = wT.ap.rearrange("p (h n) -> p h n", h=4)
                cp("scalar", wT.ap, tv[:, 0:512], [tbank.res], [wT.res])
                kd = kdr.next()
                kdv = kd.ap.rearrange("p (h c) -> p h c", h=4)
                tt("gpsimd", kdv, k4, bc(g8.ap[:, 16:20], 64), ALU.mult, [kv.res, g8.res], [kd.res])
                if mcstop == 7:
                    continue
                bW = bring.next()
                for h in range(4):
                    hp = slice((h % 2) * 64, (h % 2) * 64 + 64)
                    c = h // 2
                    mm(bW.ap[:, h * 64:(h + 1) * 64], wTv[:, h, :], Sbv[:, h, :], True, True,
                       [wT.res, Sb.res], [bW.res])
                    mm(bW.ap[:, 256 + h * 64:256 + (h + 1) * 64], qnv[:, c, :], Sbv[:, h, :], True, True,
                       [qn.res, Sb.res], [bW.res])
                vn = vnr.next()
                vnv = vn.ap.rearrange("p (h c) -> p h c", h=4)
                vn5 = vn.ap.rearrange("p (c r n) -> p c r n", c=2, r=2)
                bW5 = bW.ap[:, 0:256].rearrange("p (c r n) -> p c r n", c=2, r=2)
                for r in range(2):
                    uo = 64 if r == 0 else 0
                    tt("vector", vn5[:, :, r, :], x5(X)[:, :, r, uo:uo + 64], bW5[:, :, r, :],
                       ALU.subtract, [X.res, bW.res], [vn.res])
                o1 = o1r.next()
                tt("vector", o1.ap.rearrange("p (h c) -> p h c", h=4),
                   bW.ap[:, 256:512].rearrange("p (h c) -> p h c", h=4), bc(g8.ap[:, 8:12], 64), ALU.mult,
                   [bW.res, g8.res], [o1.res])
                bO = bring.next()
                for h in range(4):
                    mm(bO.ap[:, h * 64:(h + 1) * 64], AT.ap[:, h * 128:(h + 1) * 128], vnv[:, h, :], True, True,
                       [AT.res, vn.res], [bO.res])
                for c in range(2):
                    mm(bO.ap[:, 256 + c * 128:256 + (c + 1) * 128], kd.ap[:, c * 128:(c + 1) * 128],
                       vn.ap[:, c * 128:(c + 1) * 128], True, True, [kd.res, vn.res], [bO.res])
                oo = oor.next()
                tt("vector", oo.ap, o1.ap, bO.ap[:, 0:256], ALU.add, [o1.res, bO.res], [oo.res])
                bOs = bO.ap[:, 256:512].rearrange("p (c n) -> p c n", c=2)
                for r in range(2):
                    rs_ = slice(r * 64, (r + 1) * 64)
                    for c in range(2):
                        stt("vector", Sfv[rs_, 2 * c + r, :], Sfv[rs_, 2 * c + r, :], g8.ap[rs_, 20 + c:21 + c],
                            bOs[rs_, c, r * 64:(r + 1) * 64], ALU.mult, ALU.add,
                            [Sf.res, g8.res, bO.res], [Sf.res])
                cp("scalar", Sb.ap, Sf.ap, [Sf.res], [Sb.res])
                if d == 0:
                    store(oo, ofd[tb:tb + 128, :], oo.ap, [DR("ofd", tb // 128)])
                else:
                    ofl = ofr.next()
                    load(ofl, ofl.ap, ofd[tb:tb + 128, :], R=[DR("ofd", tb // 128)])
                    zl = zr.next()
                    load(zl, zl.ap, Pzg[tb:tb + 128, 0:256], R=[DR("Pzg", tb // 128)])
                    tt("gpsimd", oo.ap, oo.ap, ofl.ap, ALU.add, [oo.res, ofl.res], [oo.res])
                    osq = sqr.next()
                    tt("gpsimd", osq.ap, oo.ap, oo.ap, ALU.mult, [oo.res], [osq.res])
                    sm = smr.next()
                    S.op("vector", (lambda o, i: (lambda e: e.tensor_reduce(o, i, AX.X, ALU.add)))(
                        sm.ap[:, 0:4], osq.ap.rearrange("p (h c) -> p h c", h=4)), [osq.res], [sm.res])
                    rsqrt_eps(sm.ap[:, 4:8], sm.ap[:, 0:4], 64.0 * RMS_EPS, [sm.res], [sm.res])
                    tt("vector", oo.ap.rearrange("p (h c) -> p h c", h=4),
                       oo.ap.rearrange("p (h c) -> p h c", h=4), bc(sm.ap[:, 4:8], 64), ALU.mult,
                       [oo.res, sm.res], [oo.res])
                    tt("gpsimd", oo.ap, oo.ap, ng8.ap, ALU.mult, [oo.res, ng8.res], [oo.res])
                    act(zl.ap, zl.ap, AF.Silu, [zl.res], [zl.res])
                    yc = ycr.next()
                    tt("gpsimd", yc.ap, oo.ap, zl.ap, ALU.mult, [oo.res, zl.res], [yc.res])
                    tv = bfview(tbank)
                    for c in range(2):
                        tr(tv[:, c * 128:(c + 1) * 128], yc.ap[:, c * 128:(c + 1) * 128], ident_b.ap,
                           [yc.res, ident_b.res], [tbank.res])
                    yct = yctr.next()
                    cp("scalar", yct.ap, tv[:, 0:256], [tbank.res], [yct.res])
                    store(yct, ycT[6:8, :, tb:tb + 128].rearrange("m p t -> p m t"),
                          yct.ap.rearrange("p (m n) -> p m n", m=2), [DR("ycT67_", tb // 128)])
        S.barrier()

    def phase_O1(l):
        A.reset()
        cur = xbuf[l % 2]
        Wo = A.bf16(8 * D, "Wo")
        Wov = Wo.ap.rearrange("p (k n) -> p k n", k=8)
        for kc in range(8):
            load(Wo, Wov[:, kc, :], w_out[l, kc * 128:(kc + 1) * 128, :], eng="gpsimd")
        G1 = A.f32(D, "G1")
        g1 = A.f32(D, "ln1g")
        b1_ = A.f32(D, "ln1b")
        bload(g1, ln1_g[l:l + 1, :])
        bload(b1_, ln1_b[l:l + 1, :])
        ycr = Ring([A.bf16(8 * 512, "ycl%d" % i) for i in range(2)])
        xr = Ring([A.f32(D, "xo%d" % i) for i in range(3)])
        tr_ = Ring([A.f32(D, "to%d" % i) for i in range(2)])
        rr = Ring([A.f32(D, "ro%d" % i) for i in range(2)])
        sr = Ring([A.f32(D, "so%d" % i) for i in range(2)])
        outr = Ring([A.f32(D, "oo%d" % i) for i in range(2)])
        smalls = Ring([A.f32(24, "small%d" % i) for i in range(2)])
        bring = Ring(banks)
        for si, (soff, slen) in enumerate(cfg.seqs):
            load(G1, G1.ap, modrow[l, si:si + 1, 2 * D:3 * D].broadcast_to([128, D]), R=[DR("modrow", l)])
            ts("vector", G1.ap, G1.ap, 1.0, None, ALU.add, None, [G1.res], [G1.res])
            for gi in range(slen // 512):
                t0 = soff + gi * 512
                yc = ycr.next()
                ycv = yc.ap.rearrange("p (m n) -> p m n", m=8)
                load(yc, ycv, ycT[:, :, t0:t0 + 512].rearrange("m p t -> p m t"),
                     R=[DR("ycT%d_" % s_, t0 // 512) for s_ in range(6)] +
                       [DR("ycT67_", t0 // 128 + b_) for b_ in range(4)])
                for b in range(4):
                    tb = t0 + b * 128
                    x = xr.next()
                    load(x, x.ap, cur[tb:tb + 128, :], R=[DR("x%d" % (l % 2), tb // 128)])
                    tmp = tr_.next()
                    for nh in range(2):
                        bk = bring.next()
                        for kc in range(8):
                            mm(bk.ap, ycv[:, kc, b * 128:(b + 1) * 128], Wov[:, kc, nh * 512:(nh + 1) * 512],
                               kc == 0, kc == 7, [yc.res, Wo.res], [bk.res])
                        tt("vector", tmp.ap[:, nh * 512:(nh + 1) * 512], bk.ap, G1.ap[:, nh * 512:(nh + 1) * 512],
                           ALU.mult, [bk.res, G1.res], [tmp.res])
                    r = rr.next()
                    stt("vector", r.ap, x.ap, ALPHA, tmp.ap, ALU.mult, ALU.add, [x.res, tmp.res], [r.res])
                    o = outr.next()
                    layer_norm(r, o, g1, b1_, smalls.next(), sr.next())
                    store(o, x1buf[tb:tb + 128, :], o.ap, [DR("x1", tb // 128)])
        S.barrier()

    def phase_O2(l):
        A.reset()
        last = l == nl - 1
        nxt = xbuf[(l + 1) % 2]
        W1 = A.bf16(8 * DFF, "W1")
        W1v = W1.ap.rearrange("p (k n) -> p k n", k=8)
        for kc in range(8):
            load(W1, W1v[:, kc, :], w1[l, kc * 128:(kc + 1) * 128, :], eng="gpsimd")
        W2 = A.bf16(32 * D, "W2")
        W2v = W2.ap.rearrange("p (k n) -> p k n", k=32)
        for kc in range(32):
            load(W2, W2v[:, kc, :], w2[l, kc * 128:(kc + 1) * 128, :], eng="gpsimd")
        b1t = A.f32(32, "b1c")
        load(b1t, b1t.ap, b1c[l])
        G2 = A.f32(D, "G2")
        g2 = A.f32(D, "ln2g")
        b2_ = A.f32(D, "ln2b")
        bb2 = A.f32(D, "b2")
        bload(g2, ln2_g[l:l + 1, :])
        bload(b2_, ln2_b[l:l + 1, :])
        bload(bb2, b2[l:l + 1, :])
        xr = Ring([A.f32(D, "x1_%d" % i) for i in range(4)])
        hr = Ring([A.bf16(8 * 256, "h2T%d" % i) for i in range(2)])
        ar = Ring([A.f32(256, "aa%d" % i) for i in range(3)])
        ur = Ring([A.bf16(256, "uu%d" % i) for i in range(3)])
        tr_ = Ring([A.f32(D, "t2%d" % i) for i in range(2)])
        sr = Ring([A.f32(D, "s2%d" % i) for i in range(1)])
        smalls = Ring([A.f32(24, "small%d" % i) for i in range(2)])
        ring1 = Ring(banks[4:8])
        for si, (soff, slen) in enumerate(cfg.seqs):
            load(G2, G2.ap, modrow[l, si:si + 1, 5 * D:6 * D].broadcast_to([128, D]), R=[DR("modrow", l)])
            ts("vector", G2.ap, G2.ap, 1.0, None, ALU.add, None, [G2.res], [G2.res])
            dst = (ys, yp)[si]
            for gi in range(slen // 256):
                t0 = soff + gi * 256
                xg = []
                for b in range(2):
                    x = xr.next()
                    tb = t0 + b * 128
                    load(x, x.ap, x1buf[tb:tb + 128, :], R=[DR("x1", tb // 128)])
                    xg.append(x)
                hT = hr.next()
                hv = hT.ap.rearrange("p (k n) -> p k n", k=8)
                for kc in range(8):
                    bk = ring1.next()
                    for b in range(2):
                        tr(bk.ap[:, b * 128:(b + 1) * 128], xg[b].ap[:, kc * 128:(kc + 1) * 128], ident_f.ap,
                           [xg[b].res, ident_f.res], [bk.res])
                    act(hv[:, kc, :], bk.ap[:, 0:256], AF.Identity, [bk.res, modcol.res, modcp1.res], [hT.res],
                        bias=mcv[:, l, 3 * 8 + kc, si:si + 1], scale=mc1v[:, l, 4 * 8 + kc, si:si + 1])

                def ffn1(m):
                    bk = ring1.next()
                    for kc in range(8):
                        mm(bk.ap[:, 0:256], W1v[:, kc, m * 128:(m + 1) * 128], hv[:, kc, :], kc == 0, kc == 7,
                           [W1.res, hT.res], [bk.res])
                    a = ar.next()
                    act(a.ap, bk.ap[:, 0:256], AF.Relu, [bk.res, b1t.res], [a.res], bias=b1t.ap[:, m:m + 1])
                    u = ur.next()
                    tt("gpsimd", u.ap, a.ap, a.ap, ALU.mult, [a.res], [u.res])
                    return u

                u_next = ffn1(0)
                for m in range(32):
                    u = u_next
                    if m + 1 < 32:
                        u_next = ffn1(m + 1)
                    for b in range(2):
                        for nh in range(2):
                            bk = banks[b * 2 + nh]
                            mm(bk.ap, u.ap[:, b * 128:(b + 1) * 128], W2v[:, m, nh * 512:(nh + 1) * 512],
                               m == 0, m == 31, [u.res, W2.res], [bk.res])
                for b in range(2):
                    tb = t0 + b * 128
                    tmp = tr_.next()
                    for nh in range(2):
                        bk = banks[b * 2 + nh]
                        hs = slice(nh * 512, (nh + 1) * 512)
                        tt("vector", tmp.ap[:, hs], bk.ap, bb2.ap[:, hs], ALU.add, [bk.res, bb2.res], [tmp.res])
                    tt("gpsimd", tmp.ap, tmp.ap, G2.ap, ALU.mult, [tmp.res, G2.res], [tmp.res])
                    stt("vector", tmp.ap, xg[b].ap, ALPHA, tmp.ap, ALU.mult, ALU.add, [xg[b].res, tmp.res], [tmp.res])
                    layer_norm(tmp, xg[b], g2, b2_, smalls.next(), sr.next())
                    if last:
                        store(xg[b], dst[tb - soff:tb - soff + 128, :], xg[b].ap, [DR("yout", tb // 128)])
                    else:
                        store(xg[b], nxt[tb:tb + 128, :], xg[b].ap, [DR("x%d" % ((l + 1) % 2), tb // 128)])
        S.barrier()

    def want(p):
        return cfg.phases is None or p in cfg.phases

    if want("mod"):
        phase_mod()
    for l in range(nl):
        if want("P"):
            phase_P(l)
        if want("MA"):
            phase_MA(l)
        if want("MB"):
            phase_MB(l)
        if want("MC0"):
            phase_MC0(l)
        if want("MCf"):
            phase_MC(l, 0)
        if want("MCb"):
            phase_MC(l, 1)
        if want("O1"):
            phase_O1(l)
        if want("O2"):
            phase_O2(l)
    S.emit(nc, es)
    es.close()
    return nc


def _consts():
    import ml_dtypes
    c = {}
    c["c_ident"] = np.eye(128, dtype=np.float32)
    p = np.arange(128)[:, None]
    q = np.arange(128)[None, :]
    slopes = np.exp2(-np.arange(1, 9, dtype=np.float64))
    em = np.zeros((128, 8, 3, 128), np.float64)
    for r in range(3):
        dist = np.abs((r - 1) * 128 + p - q)
        valid = dist <= 128
        for h in range(8):
            em[:, h, r, :] = np.where(valid, np.exp(-slopes[h] * dist), 0.0)
    c["c_emask"] = em.reshape(128, -1).astype(ml_dtypes.bfloat16)
    bo = np.zeros((128, 128), np.float32)
    bo[:64, :64] = 1.0
    bo[64:, 64:] = 1.0
    c["c_bones"] = bo
    c["c_ones"] = np.ones((128, 128), np.float32)
    s = np.arange(128)[:, None]
    cc = np.arange(128)[None, :]
    U = np.stack([(s <= cc), (s >= cc)]).astype(np.float32)
    c["c_U"] = U
    PM = np.stack([np.where(cc < s, 0.0, BIG), np.where(cc > s, 0.0, BIG)]).astype(np.float32)
    NM = np.stack([np.where(cc >= s, 0.0, -BIG), np.where(cc <= s, 0.0, -BIG)]).astype(np.float32)
    c["c_PM"] = PM
    c["c_NM"] = NM
    pi = np.arange(128)[:, None]
    ji = np.arange(128)[None, :]
    Bb = lambda b: (pi // b == ji // b).astype(np.float32)
    c["c_blk"] = np.concatenate([Bb(16), Bb(32) - Bb(16), Bb(64) - Bb(32), 1.0 - Bb(64)], axis=1).astype(np.float32)
    sel = np.zeros((4, 2, 4, 128), np.float32)
    for h in range(4):
        sel[h, 0, h, :] = 1.0
        sel[h, 1, h, :] = -1.0
    c["c_sel"] = sel.reshape(4, -1)
    return c


def _win_perm():
    cols = []
    cols += list(range(0, 768))
    for j in range(4):
        cols += list(range(OFF_ATT + j * 64, OFF_ATT + (j + 1) * 64))
        cols += list(range(OFF_ATT + (4 + j) * 64, OFF_ATT + (5 + j) * 64))
    cols += list(range(OFF_ATT + 512, OFF_ATT + 640))
    cols += list(range(OFF_DN, OFF_DN + 768))
    cols += list(range(OFF_ATT + 640, OFF_ATT + 768))
    cols += list(range(OFF_DN + 768, OFF_DN + 1024))
    cols += list(range(OFF_DN + 1024, OFF_DN + 1040))
    assert len(cols) == DIN and len(set(cols)) == DIN
    return np.array(cols)


def make_in_maps(cfg, inp, ncores=8):
    f = lambda a: np.ascontiguousarray(np.asarray(a, dtype=np.float32))
    nl = cfg.nl
    shared = {}
    shared["ln_in_g"] = f(inp["ln_in_g"]).reshape(1, D)
    shared["ln_in_b"] = f(inp["ln_in_b"]).reshape(1, D)
    shared["w_mod"] = f(inp["w_mod"][:nl])
    shared["b_mod"] = f(inp["b_mod"][:nl])
    shared["b_modc"] = f(np.asarray(inp["b_mod"])[:nl].reshape(nl, 48, 128).transpose(0, 2, 1))
    shared["w_in"] = f(np.asarray(inp["w_in"])[:nl][:, :, _win_perm()])
    ca = np.asarray(inp["conv_a_w"])[:nl]
    shared["conv_a"] = f(ca.reshape(nl, 3, 2, 128).transpose(0, 3, 2, 1))
    shared["sink"] = f(inp["attn_sink"][:nl])
    dc = np.asarray(inp["dn_conv_w"])[:nl]
    shared["dn_conv"] = f(dc.reshape(nl, 3, 6, 128).transpose(0, 3, 2, 1))
    shared["a_log"] = f(np.concatenate([inp["dn_a_log_f"][:nl], inp["dn_a_log_b"][:nl]], axis=1))
    shared["dt_bias"] = f(np.concatenate([inp["dn_dt_bias_f"][:nl], inp["dn_dt_bias_b"][:nl]], axis=1))
    shared["norm_g"] = f(np.tile(np.asarray(inp["dn_norm_g"])[:nl], (1, 4)))
    shared["w_out"] = f(inp["w_out"][:nl])
    shared["ln1_g"] = f(inp["ln1_g"][:nl])
    shared["ln1_b"] = f(inp["ln1_b"][:nl])
    shared["w1"] = f(inp["w1"][:nl])
    shared["b1c"] = f(np.asarray(inp["b1"])[:nl].reshape(nl, 32, 128).transpose(0, 2, 1))
    shared["w2"] = f(inp["w2"][:nl])
    shared["b2"] = f(inp["b2"][:nl])
    shared["ln2_g"] = f(inp["ln2_g"][:nl])
    shared["ln2_b"] = f(inp["ln2_b"][:nl])
    shared.update(_consts())
    xs_all = np.asarray(inp["x_sample"])
    xp_all = np.asarray(inp["x_prompt"])
    cs_all = np.asarray(inp["c_sample"])
    cp_all = np.asarray(inp["c_prompt"])
    maps = []
    for i in range(ncores):
        m = dict(shared)
        m["xs"] = f(xs_all[i % xs_all.shape[0]])
        m["xp"] = f(xp_all[(i // 2) % xp_all.shape[0]])
        c2 = np.stack([cs_all[i % cs_all.shape[0]], cp_all[(i // 2) % cp_all.shape[0]]], axis=1)
        m["cT"] = f(c2.reshape(8, 128, 2).transpose(1, 0, 2))
        maps.append(m)
    return maps


_NC_CACHE = {}


def kernel(**inputs):
    cfg = Cfg()
    if "nc" not in _NC_CACHE:
        _NC_CACHE["nc"] = build(cfg)
    nc = _NC_CACHE["nc"]
    maps = make_in_maps(cfg, inputs, 8)
    res = run_bass_kernel_spmd(nc, maps, core_ids=list(range(8)))
    y_s = np.stack([np.asarray(res.results[i]["ys"], dtype=np.float32) for i in range(8)], axis=0)
    y_p = np.stack([np.asarray(res.results[2 * i]["yp"], dtype=np.float32) for i in range(4)], axis=0)
    return (y_p, y_s)
```

```python
import numpy as np
import os as _os
from contextlib import ExitStack
import concourse.bass as bass
import concourse.mybir as mybir
from concourse.bass_utils import run_bass_kernel_spmd

F32 = mybir.dt.float32
BF16 = mybir.dt.bfloat16
AF = mybir.ActivationFunctionType
ALU = mybir.AluOpType
AX = mybir.AxisListType

D = 1024
DFF = 4096
DIN = 2576
NL = 4
ALPHA = (2.0 * 4) ** 0.25
LN_EPS = 1e-5
RMS_EPS = 1e-6
OFF_ATT = 768
OFF_DN = 1536
NFM = 17
NTM = 400
NPT = 15
BIG = 30000.0


class Res:
    __slots__ = ("name", "w", "rs", "chan")

    def __init__(self, name=""):
        self.name = name
        self.w = None
        self.rs = {}
        self.chan = {}


class Op:
    __slots__ = ("eng", "fn", "chan", "deps", "signal", "sem", "val", "idx")

    def __init__(self, eng, fn, chan):
        self.eng = eng
        self.fn = fn
        self.chan = chan
        self.deps = []
        self.signal = chan is not None
        self.sem = None
        self.val = 0


class Sched:
    ENGS = ("tensor", "vector", "scalar", "gpsimd", "sync")

    def __init__(self):
        self.ops = {e: [] for e in self.ENGS}
        self.all = []
        self.chan_last = {}
        self.fence = []
        self.nchan = 0

    def op(self, eng, fn, R=(), W=(), chan=None):
        o = Op(eng, fn, chan)
        deps = {}

        pos = len(self.ops[eng])

        def add(d):
            if d is None:
                return
            if d.chan is None and d.eng == eng:
                if eng == "tensor":
                    return
                if eng in ("vector", "scalar") and pos - d.idx >= 3:
                    return
            deps[id(d)] = d

        for r in R:
            add(r.w)
        for w in W:
            add(w.w)
            for d in w.rs.values():
                add(d)
        if chan is not None:
            add(self.chan_last.get(chan))
            self.chan_last[chan] = o
        for d in self.fence:
            if d is not o:
                add(d)
        o.idx = pos
        o.deps = list(deps.values())
        for d in o.deps:
            d.signal = True
        key = eng if chan is None else ("dma", chan)
        for r in R:
            r.rs[key] = o
        for w in W:
            w.w = o
            w.rs = {}
        self.ops[eng].append(o)
        self.all.append(o)
        return o

    def barrier(self):
        f = []
        for e in self.ENGS:
            for o in reversed(self.ops[e]):
                if o.chan is None:
                    f.append(o)
                    break
        f.extend(self.chan_last.values())
        for o in f:
            o.signal = True
        self.fence = f

    def emit(self, nc, es):
        esem = {}
        for e in self.ENGS:
            if e != "sync":
                esem[e] = es.enter_context(nc.semaphore("sem_" + e))
        chans = sorted(self.chan_last.keys())
        nsem = min(len(chans), 90)
        csem_list = [es.enter_context(nc.semaphore("semd%d" % i)) for i in range(nsem)]
        assert len(chans) <= 90, len(chans)
        csem = {c: csem_list[i] for i, c in enumerate(chans)}
        ccount = {c: 0 for c in chans}
        for o in self.all:
            if o.chan is not None:
                ccount[o.chan] += 16
                o.sem = csem[o.chan]
                o.val = ccount[o.chan]
        for e in self.ENGS:
            cnt = 0
            for o in self.ops[e]:
                if o.chan is None and o.signal:
                    cnt += 1
                    o.sem = esem[e]
                    o.val = cnt
        block = es.enter_context(nc.Block())

        def run(engname):
            def body(e):
                waited = {}
                for o in self.ops[engname]:
                    need = {}
                    for d in o.deps:
                        k = id(d.sem)
                        if waited.get(k, 0) >= d.val:
                            continue
                        if k not in need or need[k][1] < d.val:
                            need[k] = (d.sem, d.val)
                    for k, (s, v) in need.items():
                        e.wait_ge(s, v)
                        waited[k] = v
                    ins = o.fn(e)
                    if o.chan is not None:
                        ins.then_inc(o.sem, 16)
                    elif o.signal:
                        ins.then_inc(o.sem, 1)
                last = {}
                for o in self.ops[engname]:
                    if o.chan is not None:
                        last[id(o.sem)] = (o.sem, max(o.val, last.get(id(o.sem), (None, 0))[1]))
                for k, (s, v) in last.items():
                    if waited.get(k, 0) < v:
                        e.wait_ge(s, v)
            return body

        block.sync(run("sync"))
        block.gpsimd(run("gpsimd"))
        block.scalar(run("scalar"))
        block.vector(run("vector"))
        block.tensor(run("tensor"))


class T:
    __slots__ = ("ap", "res")

    def __init__(self, ap, res):
        self.ap = ap
        self.res = res


class Arena:
    def __init__(self, ap2d, ncols):
        self.ap = ap2d
        self.n = ncols
        self.base = 0
        self.top = 0

    def mark(self):
        self.base = self.top

    def reset(self):
        self.top = self.base

    def f32(self, cols, name=""):
        a = self.top
        self.top += cols
        assert self.top <= self.n, ("arena overflow", name, self.top, self.n)
        return T(self.ap[:, a:a + cols], Res(name))

    def bf16(self, cols, name=""):
        c32 = (cols + 1) // 2
        a = self.top
        self.top += c32
        assert self.top <= self.n, ("arena overflow", name, self.top, self.n)
        return T(self.ap[:, a:a + c32].bitcast(BF16)[:, 0:cols], Res(name))


class Ring:
    def __init__(self, tiles):
        self.tiles = tiles
        self.i = 0

    def next(self):
        t = self.tiles[self.i % len(self.tiles)]
        self.i += 1
        return t


class Cfg:
    def __init__(self, Ss=8192, Sp=4096, nl=NL, dbg=False, phases=None):
        self.phases = phases
        self.Ss = Ss
        self.Sp = Sp
        self.nl = nl
        self.St = Ss + Sp
        self.seqs = [(0, Ss), (Ss, Sp)]
        self.dbg = dbg


def build(cfg):
    nc = bass.Bass("TRN2", target_bir_lowering=False)
    S = Sched()
    St = cfg.St
    nl = cfg.nl

    def din(name, shape, dt=F32):
        return nc.dram_tensor(name, list(shape), dt, kind="ExternalInput").ap()

    def dscr(name, shape, dt=F32):
        kind = "ExternalOutput" if cfg.dbg else "Internal"
        return nc.dram_tensor(name, list(shape), dt, kind=kind).ap()

    xs = din("xs", [cfg.Ss, D])
    xp = din("xp", [cfg.Sp, D])
    cT = din("cT", [128, 8, 2])
    ln_in_g = din("ln_in_g", [1, D])
    ln_in_b = din("ln_in_b", [1, D])
    w_mod = din("w_mod", [nl, D, 6 * D])
    b_mod = din("b_mod", [nl, 6 * D])
    b_modc = din("b_modc", [nl, 128, 48])
    w_in = din("w_in", [nl, D, DIN])
    conv_a = din("conv_a", [nl, 128, 2, 3])
    sink = din("sink", [nl, 8])
    dn_conv = din("dn_conv", [nl, 128, 6, 3])
    a_log = din("a_log", [nl, 8])
    dt_bias = din("dt_bias", [nl, 8])
    norm_g = din("norm_g", [nl, 256])
    w_out = din("w_out", [nl, D, D])
    ln1_g = din("ln1_g", [nl, D])
    ln1_b = din("ln1_b", [nl, D])
    w1 = din("w1", [nl, D, DFF])
    b1c = din("b1c", [nl, 128, 32])
    w2 = din("w2", [nl, DFF, D])
    b2 = din("b2", [nl, D])
    ln2_g = din("ln2_g", [nl, D])
    ln2_b = din("ln2_b", [nl, D])
    c_ident = din("c_ident", [128, 128])
    c_emask = din("c_emask", [128, 8 * 3 * 128], BF16)
    c_bones = din("c_bones", [128, 128])
    c_ones = din("c_ones", [128, 128])
    c_U = din("c_U", [2, 128, 128])
    c_PM = din("c_PM", [2, 128, 128])
    c_NM = din("c_NM", [2, 128, 128])
    c_sel = din("c_sel", [4, 8 * 128])
    c_blk = din("c_blk", [128, 4 * 128])

    ys = nc.dram_tensor("ys", [cfg.Ss, D], F32, kind="ExternalOutput").ap()
    yp = nc.dram_tensor("yp", [cfg.Sp, D], F32, kind="ExternalOutput").ap()

    xbuf = [dscr("xbuf0", [St, D]), dscr("xbuf1", [St, D])]
    x1buf = dscr("x1buf", [St, D])
    modrow = dscr("modrow", [nl, 2, 6 * D])
    PT = dscr("PT", [NPT, 128, St], BF16)
    Pv = dscr("Pv", [St, 132], BF16)
    Pzg = dscr("Pzg", [St, 272])
    qnT = dscr("qnT", [2, 128, St], BF16)
    knT = dscr("knT", [2, 128, St], BF16)
    kvtm = dscr("kvtm", [St, 512], BF16)
    gbd = dscr("gbd", [St, 16])
    ofd = dscr("ofd", [St, 256])
    obd = dscr("obd", [St, 256])
    ycT = dscr("ycT", [8, 128, St], BF16)

    dres = {}

    def DR(name, i):
        k = (name, i)
        if k not in dres:
            dres[k] = Res("%s%s" % (name, i))
        return dres[k]

    es = ExitStack()
    ACOLS = 50000
    arena_t = es.enter_context(nc.sbuf_tensor("arena", [128, ACOLS], F32))
    A = Arena(arena_t[:], ACOLS)
    banks = []
    for i in range(8):
        pt = es.enter_context(nc.psum_tensor("psb%d" % i, [128, 512], F32))
        banks.append(T(pt[:], Res("bank%d" % i)))

    chan_cnt = {"sync": 0, "gpsimd": 0}

    def chan_of(t, eng):
        if eng not in t.res.chan:
            t.res.chan[eng] = (eng, chan_cnt[eng] % 44)
            chan_cnt[eng] += 1
        return t.res.chan[eng]

    def dma(eng, out, in_, R, W, chan, **kw):
        return S.op(eng, lambda e: e.dma_start(out=out, in_=in_, **kw), R, W, chan=chan)

    def load(t, out, in_, R=(), eng="sync", **kw):
        return dma(eng, out, in_, R, [t.res], chan_of(t, eng), **kw)

    def store(t, out, in_, W, eng="gpsimd", **kw):
        return dma(eng, out, in_, [t.res], W, chan_of(t, eng), **kw)

    def mm(out, lhsT, rhs, start, stop, R, W):
        return S.op("tensor", lambda e: e.matmul(out, lhsT, rhs, start=start, stop=stop), R, W)

    def tr(out, in_, ident, R, W):
        return S.op("tensor", lambda e: e.transpose(out, in_, ident), R, W)

    def act(out, in_, func, R, W, bias=None, scale=None):
        kw = {}
        if bias is not None:
            kw["bias"] = bias
        if scale is not None:
            kw["scale"] = scale
        return S.op("scalar", lambda e: e.activation(out, in_, func, **kw), R, W)

    def tt(eng, out, in0, in1, op, R, W):
        return S.op(eng, lambda e: e.tensor_tensor(out, in0, in1, op), R, W)

    def ts(eng, out, in0, s1, s2, op0, op1, R, W):
        if s2 is None:
            return S.op(eng, lambda e: e.tensor_scalar(out, in0, s1, None, op0), R, W)
        return S.op(eng, lambda e: e.tensor_scalar(out, in0, s1, s2, op0, op1), R, W)

    def stt(eng, out, in0, sc, in1, op0, op1, R, W):
        return S.op(eng, lambda e: e.scalar_tensor_tensor(out, in0, sc, in1, op0, op1), R, W)

    def cp(eng, out, in_, R, W):
        if eng == "scalar":
            return S.op(eng, lambda e: e.copy(out, in_), R, W)
        return S.op(eng, lambda e: e.tensor_copy(out, in_), R, W)

    def memset(eng, out, val, W):
        return S.op(eng, lambda e: e.memset(out, val), (), W)

    def rsqrt_eps(out, in_, eps, Rr, Wr):
        ts("vector", out, in_, eps, None, ALU.add, None, Rr, Wr)
        act(out, out, AF.Sqrt, Wr, Wr)
        S.op("vector", lambda e: e.reciprocal(out, out), Wr, Wr)

    def bc(ap2, n):
        return ap2.unsqueeze(2).broadcast_to([ap2.shape[0], ap2.shape[1], n])

    def bfview(bank):
        return bank.ap.bitcast(BF16)

    ident_f = A.f32(128, "ident_f")
    ident_b = A.bf16(128, "ident_b")
    modcol = A.f32(nl * 96, "modcol")
    modcp1 = A.f32(nl * 96, "modcp1")
    load(ident_f, ident_f.ap, c_ident)
    cp("vector", ident_b.ap, ident_f.ap, [ident_f.res], [ident_b.res])
    mcv = modcol.ap.rearrange("p (l j s) -> p l j s", l=nl, j=48)
    mc1v = modcp1.ap.rearrange("p (l j s) -> p l j s", l=nl, j=48)
    A.mark()

    bankring = Ring(banks)

    def phase_mod():
        A.reset()
        sc = A.f32(16, "sc")
        bm = A.f32(6 * D, "bm")
        mrow = A.f32(6 * D, "mrow")
        wts = Ring([A.f32(8 * 512, "wmod%d" % i) for i in range(2)])
        scv = sc.ap.rearrange("p (k s) -> p k s", k=8)
        load(sc, scv, cT)
        act(sc.ap, sc.ap, AF.Silu, [sc.res], [sc.res])
        mring = Ring(banks[0:6])
        bmc = A.f32(48, "bmc")
        for l in range(nl):
            load(bm, bm.ap[0:2, :], b_mod[l:l + 1, :].broadcast_to([2, 6 * D]))
            load(bmc, bmc.ap, b_modc[l])
            bkc = banks[7]
            for cc in range(12):
                wt = wts.next()
                wv = wt.ap.rearrange("p (k n) -> p k n", k=8)
                load(wt, wv, w_mod[l, :, cc * 512:(cc + 1) * 512].rearrange("(k p) n -> p k n", p=128))
                bk = mring.next()
                for kc in range(8):
                    mm(bk.ap[0:2, :], scv[:, kc, :], wv[:, kc, :], kc == 0, kc == 7,
                       [sc.res, wt.res], [bk.res])
                tt("vector", mrow.ap[0:2, cc * 512:(cc + 1) * 512], bk.ap[0:2, :],
                   bm.ap[0:2, cc * 512:(cc + 1) * 512], ALU.add, [bk.res, bm.res], [mrow.res])
                for j4 in range(4):
                    j = cc * 4 + j4
                    for kc in range(8):
                        mm(bkc.ap[:, 2 * j:2 * j + 2], wv[:, kc, j4 * 128:(j4 + 1) * 128], scv[:, kc, :],
                           kc == 0, kc == 7, [sc.res, wt.res], [bkc.res])
            store(mrow, modrow[l], mrow.ap[0:2, :], [DR("modrow", l)])
            tt("vector", modcol.ap[:, l * 96:(l + 1) * 96].rearrange("p (j s) -> p j s", s=2),
               bkc.ap[:, 0:96].rearrange("p (j s) -> p j s", s=2), bc(bmc.ap, 2), ALU.add,
               [bkc.res, bmc.res], [modcol.res])
        ts("vector", modcp1.ap, modcol.ap, 1.0, None, ALU.add, None, [modcol.res], [modcp1.res])
        if cfg.dbg:
            dbgmc = dscr("dbg_modcol", [128, nl * 96])
            store(modcp1, dbgmc, modcp1.ap, [DR("dbgmc", 0)])
        S.barrier()

    def layer_norm(r, out, gb, bb, small, scratch, eng2="gpsimd"):
        st = small.ap[:, 0:12].rearrange("p (c s) -> p c s", c=2)
        for c in range(2):
            S.op("vector", (lambda o, i: (lambda e: e.bn_stats(o, i)))(st[:, c, :], r.ap[:, c * 512:(c + 1) * 512]),
                 [r.res], [small.res])
        mv = small.ap[:, 12:14]
        S.op("vector", lambda e: e.bn_aggr(mv, small.ap[:, 0:12]), [small.res], [small.res])
        rstd = small.ap[:, 14:15]
        nb = small.ap[:, 15:16]
        rsqrt_eps(rstd, small.ap[:, 13:14], LN_EPS, [small.res], [small.res])
        stt("vector", nb, small.ap[:, 12:13], -1.0, rstd, ALU.mult, ALU.mult, [small.res], [small.res])
        act(scratch.ap, r.ap, AF.Identity, [r.res, small.res], [scratch.res], bias=nb, scale=rstd)
        tt(eng2, scratch.ap, scratch.ap, gb.ap, ALU.mult, [scratch.res, gb.res], [scratch.res])
        tt(eng2, out.ap, scratch.ap, bb.ap, ALU.add, [scratch.res, bb.res], [out.res])

    def bload(t, src_row):
        n = src_row.shape[-1]
        load(t, t.ap, src_row.broadcast_to([128, n]))

    def phase_P(l):
        A.reset()
        cur = xbuf[l % 2]
        Wt = A.bf16(8 * DIN, "Wt")
        Wv = Wt.ap.rearrange("p (k n) -> p k n", k=8)
        for kc in range(8):
            load(Wt, Wv[:, kc, :], w_in[l, kc * 128:(kc + 1) * 128, :], eng="gpsimd")
        if l == 0:
            g_in = A.f32(D, "g_in")
            b_in = A.f32(D, "b_in")
            bload(g_in, ln_in_g)
            bload(b_in, ln_in_b)
            lnscr = A.f32(D, "lnscr")
            smalls = Ring([A.f32(24, "small%d" % i) for i in range(2)])
        xr = Ring([A.f32(D, "xg%d" % i) for i in range(8)])
        hr = Ring([A.bf16(8 * 512, "hT%d" % i) for i in range(2)])
        ptr = Ring([A.bf16(512, "pt%d" % i) for i in range(4)])
        cfr = Ring([A.f32(512, "cf%d" % i) for i in range(2)])
        vr = Ring([A.bf16(132, "va%d" % i) for i in range(2)])
        zr = Ring([A.f32(272, "zg%d" % i) for i in range(2)])
        for v in vr.tiles:
            memset("vector", v.ap, 1.0, [v.res])
        for si, (soff, slen) in enumerate(cfg.seqs):
            src = (xs, xp)[si]
            for gi in range(slen // 512):
                t0 = soff + gi * 512
                xg = []
                for b in range(4):
                    x = xr.next()
                    tb = t0 + b * 128
                    if l == 0:
                        load(x, x.ap, src[tb - soff:tb - soff + 128, :])
                        layer_norm(x, x, g_in, b_in, smalls.next(), lnscr)
                        store(x, cur[tb:tb + 128, :], x.ap, [DR("x%d" % (l % 2), tb // 128)])
                    else:
                        load(x, x.ap, cur[tb:tb + 128, :], R=[DR("x%d" % (l % 2), tb // 128)])
                    xg.append(x)
                import os as _os
                pstop = int(_os.environ.get("PSTOP", "9"))
                if pstop == 0:
                    continue
                hT = hr.next()
                hv = hT.ap.rearrange("p (k n) -> p k n", k=8)
                for kc in range(8):
                    bk = bankring.next()
                    for b in range(4):
                        tr(bk.ap[:, b * 128:(b + 1) * 128], xg[b].ap[:, kc * 128:(kc + 1) * 128], ident_f.ap,
                           [xg[b].res, ident_f.res], [bk.res])
                    act(hv[:, kc, :], bk.ap, AF.Identity, [bk.res, modcol.res, modcp1.res], [hT.res],
                        bias=mcv[:, l, 0 * 8 + kc, si:si + 1], scale=mc1v[:, l, 1 * 8 + kc, si:si + 1])
                if pstop == 1:
                    continue
                cu_c = {}
                for m in range(NFM):
                    bk = bankring.next()
                    for kc in range(8):
                        mm(bk.ap, Wv[:, kc, m * 128:(m + 1) * 128], hv[:, kc, :], kc == 0, kc == 7,
                           [Wt.res, hT.res], [bk.res])
                    if m in (2, 3):
                        cf = cfr.next()
                        cp("scalar", cf.ap, bk.ap, [bk.res], [cf.res])
                        cu_c[m] = cf
                        continue
                    pt = ptr.next()
                    if m in (4, 5):
                        cf = cu_c[m - 2]
                        tt("vector", pt.ap, bk.ap, cf.ap, ALU.mult, [bk.res, cf.res], [pt.res])
                        slot = m - 2
                    else:
                        if m % 2 == 0:
                            cp("scalar", pt.ap, bk.ap, [bk.res], [pt.res])
                        else:
                            cp("vector", pt.ap, bk.ap, [bk.res], [pt.res])
                        slot = m if m < 2 else m - 2
                    store(pt, PT[slot, :, t0:t0 + 512], pt.ap, [DR("PT%d_" % slot, t0 // 512)])
                if pstop == 2:
                    continue
                for b in range(4):
                    tb = t0 + b * 128
                    bk = bankring.next()
                    for kc in range(8):
                        mm(bk.ap[:, 0:NTM], hv[:, kc, b * 128:(b + 1) * 128], Wv[:, kc, NFM * 128:DIN],
                           kc == 0, kc == 7, [Wt.res, hT.res], [bk.res])
                    if pstop == 5:
                        continue
                    va = vr.next()
                    vv = va.ap.rearrange("p (g c) -> p g c", g=2)
                    cp("vector", vv[:, :, 0:64], bk.ap[:, 0:128].rearrange("p (g c) -> p g c", g=2),
                       [bk.res], [va.res])
                    if pstop == 6:
                        continue
                    zg = zr.next()
                    cp("vector", zg.ap, bk.ap[:, 128:NTM], [bk.res], [zg.res])
                    if pstop == 3:
                        continue
                    store(va, Pv[tb:tb + 128, :], va.ap, [DR("Pv", tb // 128)])
                    if pstop == 4:
                        continue
                    store(zg, Pzg[tb:tb + 128, :], zg.ap, [DR("Pzg", tb // 128)])
        S.barrier()

    def halo_load(t, view, slot_ap, t0, soff, slen, R_name, nblk512=True):
        lo = t0 - 1
        hi = t0 + 513
        a = 0
        b_ = 514
        if t0 == soff:
            memset("gpsimd", view[:, 0:1], 0.0, [t.res])
            lo = t0
            a = 1
        if t0 + 512 == soff + slen:
            memset("gpsimd", view[:, 513:514], 0.0, [t.res])
            hi = t0 + 512
            b_ = 513
        rs = [DR(R_name, g) for g in range(max(lo, soff) // 512, (hi - 1) // 512 + 1)]
        load(t, view[:, a:b_], slot_ap[:, lo:hi], R=rs)

    def phase_MA(l):
        A.reset()
        wa = A.f32(6, "wa")
        wav = wa.ap.rearrange("p (j k) -> p j k", j=2)
        load(wa, wav, conv_a[l])
        cur_ = Ring([A.bf16(2 * 514, "cu%d" % i) for i in range(2)])
        br = Ring([A.bf16(2 * 512, "bg%d" % i) for i in range(2)])
        yr = Ring([A.f32(512, "ya%d" % i) for i in range(2)])
        yor = Ring([A.bf16(512, "yo%d" % i) for i in range(2)])
        for si, (soff, slen) in enumerate(cfg.seqs):
            for gi in range(slen // 512):
                t0 = soff + gi * 512
                cu = cur_.next()
                cuv = cu.ap.rearrange("p (j n) -> p j n", j=2)
                bg = br.next()
                bgv = bg.ap.rearrange("p (j n) -> p j n", j=2)
                for j in range(2):
                    halo_load(cu, cuv[:, j, :], PT[2 + j], t0, soff, slen, "PT%d_" % (2 + j))
                    load(bg, bgv[:, j, :], PT[j, :, t0:t0 + 512], R=[DR("PT%d_" % j, t0 // 512)])
                for j in range(2):
                    y = yr.next()
                    ts("vector", y.ap, cuv[:, j, 0:512], wav[:, j, 0:1], None, ALU.mult, None,
                       [cu.res, wa.res], [y.res])
                    stt("vector", y.ap, cuv[:, j, 1:513], wav[:, j, 1:2], y.ap, ALU.mult, ALU.add,
                        [cu.res, wa.res, y.res], [y.res])
                    stt("vector", y.ap, cuv[:, j, 2:514], wav[:, j, 2:3], y.ap, ALU.mult, ALU.add,
                        [cu.res, wa.res, y.res], [y.res])
                    yo = yor.next()
                    tt("gpsimd", yo.ap, y.ap, bgv[:, j, :], ALU.mult, [y.res, bg.res], [yo.res])
                    store(yo, ycT[j, :, t0:t0 + 512], yo.ap, [DR("ycT%d_" % j, t0 // 512)])
        S.barrier()

    def phase_MB(l):
        A.reset()
        em = A.bf16(8 * 3 * 128, "emask")
        emv = em.ap.rearrange("p (h r q) -> p h r q", h=8, r=3)
        load(em, em.ap, c_emask)
        esk = A.f32(8, "esink")
        bload(esk, sink[l:l + 1, :])
        act(esk.ap, esk.ap, AF.Exp, [esk.res], [esk.res])
        qr = Ring([A.bf16(4 * 512, "qT%d" % i) for i in range(2)])
        kr = Ring([A.bf16(768, "kT%d" % i) for i in range(2)])
        vrr = Ring([A.bf16(6 * 132, "vg%d" % i) for i in range(2)])
        per = Ring([A.bf16(512, "pe%d" % i) for i in range(4)])
        pmr = Ring([A.bf16(512, "pm%d" % i) for i in range(5)])
        dnr = Ring([A.f32(16, "den%d" % i) for i in range(2)])
        ybr = Ring([A.bf16(512, "yb%d" % i) for i in range(2)])
        ytr = Ring([A.bf16(4 * 512, "yt%d" % i) for i in range(2)])
        sring = Ring(banks[0:5])
        oring = Ring([(banks[5], banks[6])])
        tbank = banks[7]
        for si, (soff, slen) in enumerate(cfg.seqs):
            nblk = slen // 128
            for gi in range(slen // 512):
                t0 = soff + gi * 512
                qT = qr.next()
                qv = qT.ap.rearrange("p (m n) -> p m n", m=4)
                load(qT, qv, PT[4:8, :, t0:t0 + 512].rearrange("m p t -> p m t"),
                     R=[DR("PT%d_" % s_, t0 // 512) for s_ in range(4, 8)])
                kT = kr.next()
                lo = max(t0 - 128, soff)
                hi = min(t0 + 640, soff + slen)
                load(kT, kT.ap[:, lo - (t0 - 128):hi - (t0 - 128)], PT[8, :, lo:hi],
                     R=[DR("PT8_", g) for g in range(lo // 512, (hi - 1) // 512 + 1)])
                vg = vrr.next()
                vgv = vg.ap.rearrange("p (j c) -> p j c", j=6)
                j0 = (lo - (t0 - 128)) // 128
                j1 = (hi - (t0 - 128)) // 128
                load(vg, vgv[:, j0:j1, :], Pv[lo:hi, :].rearrange("(j p) c -> p j c", p=128),
                     R=[DR("Pv", b_) for b_ in range(lo // 128, hi // 128)])
                yT = ytr.next()
                ytv = yT.ap.rearrange("p (m n) -> p m n", m=4)
                for b in range(4):
                    ib = gi * 4 + b
                    rels = [r for r in range(3) if 0 <= ib + r - 1 < nblk]
                    ob = oring.next()
                    items = [(g, r) for g in range(2) for r in rels]

                    def stage1(g, r):
                        ps_ = slice(g * 64, (g + 1) * 64)
                        jb = b + r
                        sb = sring.next()
                        mm(sb.ap.rearrange("p (m n) -> p m n", m=4), kT.ap[ps_, jb * 128:(jb + 1) * 128],
                           qv[ps_, :, b * 128:(b + 1) * 128], True, True, [kT.res, qT.res], [sb.res])
                        pe = per.next()
                        act(pe.ap, sb.ap, AF.Exp, [sb.res], [pe.res], scale=0.125)
                        pm = pmr.next()
                        tt("vector" if (g + r) % 2 == 0 else "gpsimd", pm.ap.rearrange("p (h q) -> p h q", h=4),
                           pe.ap.rearrange("p (h q) -> p h q", h=4), emv[:, g * 4:(g + 1) * 4, r, :],
                           ALU.mult, [pe.res, em.res], [pm.res])
                        return pm

                    def stage2(g, r, pm):
                        jb = b + r
                        obv = ob[g].ap[:, 0:264].rearrange("p (h c) -> p h c", h=4)
                        for hl in range(4):
                            mm(obv[:, hl, :], pm.ap[:, hl * 128:(hl + 1) * 128],
                               vgv[:, jb, g * 66:(g + 1) * 66], (r == rels[0] and hl == 0), r == rels[-1],
                               [pm.res, vg.res], [ob[g].res])

                    SK = 2
                    pms = [stage1(*items[i]) for i in range(min(SK, len(items)))]
                    for i, (g, r) in enumerate(items):
                        pm = pms.pop(0)
                        if i + SK < len(items):
                            pms.append(stage1(*items[i + SK]))
                        stage2(g, r, pm)
                    den = dnr.next()
                    yb = ybr.next()
                    for g in range(2):
                        obv = ob[g].ap[:, 0:264].rearrange("p (h c) -> p h c", h=4)
                        tt("vector", den.ap[:, g * 4:(g + 1) * 4], obv[:, :, 64], esk.ap[:, g * 4:(g + 1) * 4],
                           ALU.add, [ob[g].res, esk.res], [den.res])
                    S.op("vector", (lambda o, i: (lambda e: e.reciprocal(o, i)))(den.ap[:, 8:16], den.ap[:, 0:8]),
                         [den.res], [den.res])
                    for g in range(2):
                        obv = ob[g].ap[:, 0:264].rearrange("p (h c) -> p h c", h=4)
                        tt("vector", yb.ap[:, g * 256:(g + 1) * 256].rearrange("p (h c) -> p h c", h=4),
                           obv[:, :, 0:64], bc(den.ap[:, 8 + g * 4:8 + (g + 1) * 4], 64), ALU.mult,
                           [ob[g].res, den.res], [yb.res])
                    tv = bfview(tbank)
                    for c in range(4):
                        tr(tv[:, c * 128:(c + 1) * 128], yb.ap[:, c * 128:(c + 1) * 128], ident_b.ap,
                           [yb.res, ident_b.res], [tbank.res])
                    cp("scalar", ytv[:, :, b * 128:(b + 1) * 128],
                       tv[:, 0:512].rearrange("p (m n) -> p m n", m=4), [tbank.res], [yT.res])
                store(yT, ycT[2:6, :, t0:t0 + 512].rearrange("m p t -> p m t"), ytv,
                      [DR("ycT%d_" % s_, t0 // 512) for s_ in range(2, 6)])
        S.barrier()

    def phase_MC0(l):
        A.reset()
        wd = A.f32(18, "wd")
        wdv = wd.ap.rearrange("p (j k) -> p j k", j=6)
        load(wd, wdv, dn_conv[l])
        bon = A.f32(128, "bones")
        load(bon, bon.ap, c_bones)
        nega = A.f32(8, "nega")
        bload(nega, a_log[l:l + 1, :])
        act(nega.ap, nega.ap, AF.Exp, [nega.res], [nega.res])
        ts("vector", nega.ap, nega.ap, -1.0, None, ALU.mult, None, [nega.res], [nega.res])
        dtb = A.f32(8, "dtb")
        bload(dtb, dt_bias[l:l + 1, :])
        xr_ = Ring([A.bf16(6 * 514, "xin%d" % i) for i in range(2)])
        yr = Ring([A.f32(6 * 512, "cy%d" % i) for i in range(2)])
        sqr = Ring([A.f32(512, "sq%d" % i) for i in range(2)])
        rsr = Ring([A.f32(512, "rs%d" % i) for i in range(2)])
        qkr = Ring([A.bf16(6 * 512, "qk%d" % i) for i in range(2)])
        kvr = Ring([A.bf16(512, "kv%d" % i) for i in range(2)])
        gr = Ring([A.f32(4 * 16, "gt%d" % i) for i in range(2)])
        gor = Ring([A.f32(4 * 16, "go%d" % i) for i in range(2)])
        tbr = Ring(banks[6:8])
        nbr = Ring(banks[0:6])
        for si, (soff, slen) in enumerate(cfg.seqs):
            for gi in range(slen // 512):
                t0 = soff + gi * 512
                xin = xr_.next()
                xv = xin.ap.rearrange("p (j n) -> p j n", j=6)
                for j in range(6):
                    halo_load(xin, xv[:, j, :], PT[9 + j], t0, soff, slen, "PT%d_" % (9 + j))
                y = yr.next()
                yv = y.ap.rearrange("p (j n) -> p j n", j=6)
                for j in range(6):
                    eng = "vector"
                    ts(eng, yv[:, j, :], xv[:, j, 0:512], wdv[:, j, 0:1], None, ALU.mult, None,
                       [xin.res, wd.res], [y.res])
                    stt(eng, yv[:, j, :], xv[:, j, 1:513], wdv[:, j, 1:2], yv[:, j, :], ALU.mult, ALU.add,
                        [xin.res, wd.res, y.res], [y.res])
                    stt(eng, yv[:, j, :], xv[:, j, 2:514], wdv[:, j, 2:3], yv[:, j, :], ALU.mult, ALU.add,
                        [xin.res, wd.res, y.res], [y.res])
                act(y.ap, y.ap, AF.Silu, [y.res], [y.res])
                qk = qkr.next()
                qkv = qk.ap.rearrange("p (j n) -> p j n", j=6)
                for j in range(4):
                    sq = sqr.next()
                    tt("gpsimd", sq.ap, yv[:, j, :], yv[:, j, :], ALU.mult, [y.res], [sq.res])
                    bk = nbr.next()
                    mm(bk.ap, bon.ap, sq.ap, True, True, [bon.res, sq.res], [bk.res])
                    rs = rsr.next()
                    rsqrt_eps(rs.ap, bk.ap, RMS_EPS, [bk.res], [rs.res])
                    if j < 2:
                        stt("vector", qkv[:, j, :], yv[:, j, :], 0.125, rs.ap, ALU.mult, ALU.mult,
                            [y.res, rs.res], [qk.res])
                    else:
                        tt("vector", qkv[:, j, :], yv[:, j, :], rs.ap, ALU.mult, [y.res, rs.res], [qk.res])
                cp("gpsimd", qkv[:, 4:6, :], yv[:, 4:6, :], [y.res], [qk.res])
                store(qk, qnT[:, :, t0:t0 + 512].rearrange("m p t -> p m t"), qkv[:, 0:2, :],
                      [DR("qnT", t0 // 512)])
                store(qk, knT[:, :, t0:t0 + 512].rearrange("m p t -> p m t"), qkv[:, 2:4, :],
                      [DR("knT", t0 // 512)])
                gt = gr.next()
                gtv = gt.ap.rearrange("p (b c) -> p b c", b=4)
                load(gt, gtv, Pzg[t0:t0 + 512, 256:272].rearrange("(b p) c -> p b c", p=128),
                     R=[DR("Pzg", t0 // 128 + b_) for b_ in range(4)])
                go = gor.next()
                gov = go.ap.rearrange("p (b c) -> p b c", b=4)
                tt("vector", gov[:, :, 0:8], gtv[:, :, 0:8], dtb.ap.unsqueeze(1).broadcast_to([128, 4, 8]),
                   ALU.add, [gt.res, dtb.res], [go.res])
                act(gov[:, :, 0:8], gov[:, :, 0:8], AF.Exp, [go.res], [go.res])
                act(gov[:, :, 0:8], gov[:, :, 0:8], AF.Ln, [go.res], [go.res], bias=1.0)
                tt("vector", gov[:, :, 0:8], gov[:, :, 0:8], nega.ap.unsqueeze(1).broadcast_to([128, 4, 8]),
                   ALU.mult, [go.res, nega.res], [go.res])
                act(gov[:, :, 8:16], gtv[:, :, 8:16], AF.Sigmoid, [gt.res], [go.res])
                store(go, gbd[t0:t0 + 512, :].rearrange("(b p) c -> p b c", p=128), gov,
                      [DR("gbd", t0 // 128 + b_) for b_ in range(4)])
                for b in range(4):
                    tb = t0 + b * 128
                    tbk = tbr.next()
                    tv = bfview(tbk)
                    for j in range(4):
                        tr(tv[:, j * 128:(j + 1) * 128], qkv[:, 2 + j, b * 128:(b + 1) * 128], ident_b.ap,
                           [qk.res, ident_b.res], [tbk.res])
                    kv = kvr.next()
                    cp("scalar", kv.ap, tv[:, 0:512], [tbk.res], [kv.res])
                    store(kv, kvtm[tb:tb + 128, :], kv.ap, [DR("kvtm", tb // 128)])
        S.barrier()

    def mc_gen(l, d, si_sel, mybanks, sh):
        U, PM, NM, Ub = sh["U%d" % d], sh["PM%d" % d], sh["NM%d" % d], sh["Ub%d" % d]
        B16, MK1, MK2, MK3 = sh["B16"], sh["MK1"], sh["MK2"], sh["MK3"]
        onesb, nonesb, bon = sh["onesb"], sh["nonesb"], sh["bon"]
        Mr = Ring([A.bf16(512, "Mr%d" % i) for i in range(14)])
        dgr = Ring([A.bf16(1024, "Dg%d" % i) for i in range(1)])
        ghr = Ring([A.bf16(16, "gh%d" % i) for i in range(1)])
        Sf = A.f32(256, "Sf")
        Sb = A.bf16(256, "Sb")
        Sfv = Sf.ap.rearrange("p (h n) -> p h n", h=4)
        Sbv = Sb.ap.rearrange("p (h n) -> p h n", h=4)
        knr = Ring([A.bf16(256, "kn%d" % i) for i in range(1)])
        qnr = Ring([A.bf16(256, "qn%d" % i) for i in range(1)])
        knzr = Ring([A.bf16(512, "knz%d" % i) for i in range(1)])
        kvr = Ring([A.bf16(512, "kvl%d" % i) for i in range(1)])
        gbr = Ring([A.f32(16, "gb%d" % i) for i in range(1)])
        g8r = Ring([A.f32(24, "g8%d" % i) for i in range(1)])
        x1r = Ring([A.f32(512, "x1%d" % i) for i in range(2)])
        dsr = Ring([A.bf16(512, "dst%d" % i) for i in range(1)])
        dtr = Ring([A.bf16(512, "dti%d" % i) for i in range(1)])
        tmr = Ring([A.bf16(512, "tmpf%d" % i) for i in range(1)])
        Lr = Ring([A.bf16(512, "L%d" % i) for i in range(1)])
        LTr = Ring([A.bf16(512, "LT%d" % i) for i in range(1)])
        ATr = Ring([A.bf16(512, "AT%d" % i) for i in range(1)])
        Xr = Ring([A.bf16(512, "X%d" % i) for i in range(2)])
        wTr = Ring([A.bf16(512, "wT%d" % i) for i in range(1)])
        kdr = Ring([A.bf16(256, "kd%d" % i) for i in range(1)])
        vnr = Ring([A.bf16(256, "vn%d" % i) for i in range(1)])
        o1r = Ring([A.f32(256, "o1%d" % i) for i in range(1)])
        oor = Ring([A.f32(256, "oo%d" % i) for i in range(2)])
        bring = Ring(mybanks)
        for si, (soff, slen) in [(si_sel, cfg.seqs[si_sel])]:
            nblk = slen // 128
            memset("vector", Sf.ap, 0.0, [Sf.res])
            memset("vector", Sb.ap, 0.0, [Sb.res])
            order = range(nblk) if d == 0 else range(nblk - 1, -1, -1)
            for ib in order:
                tb = soff + ib * 128
                kn = knr.next()
                knv = kn.ap.rearrange("p (c n) -> p c n", c=2)
                load(kn, knv, knT[:, :, tb:tb + 128].rearrange("m p t -> p m t"), R=[DR("knT", tb // 512)])
                qn = qnr.next()
                qnv = qn.ap.rearrange("p (c n) -> p c n", c=2)
                load(qn, qnv, qnT[:, :, tb:tb + 128].rearrange("m p t -> p m t"), R=[DR("qnT", tb // 512)])
                kv = kvr.next()
                load(kv, kv.ap, kvtm[tb:tb + 128, :], R=[DR("kvtm", tb // 128)])
                k4 = kv.ap[:, 0:256].rearrange("p (h c) -> p h c", h=4)
                v4 = kv.ap[:, 256:512].rearrange("p (h c) -> p h c", h=4)
                gb = gbr.next()
                load(gb, gb.ap, gbd[tb:tb + 128, :], R=[DR("gbd", tb // 128)])
                g_d = gb.ap[:, d * 4:(d + 1) * 4]
                beta = gb.ap[:, 8 + d * 4:8 + (d + 1) * 4]
                mcstop = int(_os.environ.get('MCSTOP', '99'))
                if mcstop == -1:
                    continue
                mcstop = int(_os.environ.get('MCSTOP', '99'))
                yield
                gh = ghr.next()
                cp("vector", gh.ap[:, 0:4], g_d, [gb.res], [gh.res])
                tt("vector", gh.ap[:, 4:8], g_d, gh.ap[:, 0:4], ALU.subtract, [gb.res, gh.res], [gh.res])
                bA = bring.next()
                mm(bA.ap[:, 0:4], Ub.ap, gh.ap[:, 0:4], True, False, [Ub.res, gh.res], [bA.res])
                mm(bA.ap[:, 0:4], Ub.ap, gh.ap[:, 4:8], False, True, [Ub.res, gh.res], [bA.res])
                mm(bA.ap[:, 4:8], onesb.ap, gh.ap[:, 0:4], True, False, [onesb.res, gh.res], [bA.res])
                mm(bA.ap[:, 4:8], onesb.ap, gh.ap[:, 4:8], False, True, [onesb.res, gh.res], [bA.res])
                g8 = g8r.next()
                cp("vector", g8.ap[:, 0:8], bA.ap[:, 0:8], [bA.res], [g8.res])
                yield
                if mcstop == 0:
                    continue
                yield
                cp("vector", gh.ap[:, 8:12], g8.ap[:, 0:4], [g8.res], [gh.res])
                tt("vector", gh.ap[:, 12:16], g8.ap[:, 0:4], gh.ap[:, 8:12], ALU.subtract, [g8.res, gh.res], [gh.res])
                Dg = dgr.next()
                Dgv = Dg.ap.rearrange("p (a h n) -> p a h n", a=2, h=4)
                for a_ in range(2):
                    tt("vector", Dgv[:, a_, :, :], ident_b.ap.unsqueeze(1).broadcast_to([128, 4, 128]),
                       bc(gh.ap[:, 8 + 4 * a_:12 + 4 * a_], 128), ALU.mult, [ident_b.res, gh.res], [Dg.res])
                bE = bring.next()
                for h in range(4):
                    hs = slice(h * 128, (h + 1) * 128)
                    mm(bE.ap[:, hs], onesb.ap, Dgv[:, 0, h, :], True, False, [onesb.res, Dg.res], [bE.res])
                    mm(bE.ap[:, hs], onesb.ap, Dgv[:, 1, h, :], False, False, [onesb.res, Dg.res], [bE.res])
                    mm(bE.ap[:, hs], Dgv[:, 0, h, :], nonesb.ap, False, False, [nonesb.res, Dg.res], [bE.res])
                    mm(bE.ap[:, hs], Dgv[:, 1, h, :], nonesb.ap, False, True, [nonesb.res, Dg.res], [bE.res])
                bEv = bE.ap.rearrange("p (h n) -> p h n", h=4)
                x1 = x1r.next()
                tt("vector", x1.ap.rearrange("p (h n) -> p h n", h=4), bEv,
                   PM.ap.unsqueeze(1).broadcast_to([128, 4, 128]), ALU.max, [bE.res, PM.res], [x1.res])
                dst = dsr.next()
                act(dst.ap, x1.ap, AF.Exp, [x1.res], [dst.res], scale=-1.0)
                yield
                x2 = x1r.next()
                tt("vector", x2.ap.rearrange("p (h n) -> p h n", h=4), bEv,
                   NM.ap.unsqueeze(1).broadcast_to([128, 4, 128]), ALU.min, [bE.res, NM.res], [x2.res])
                dti = dtr.next()
                act(dti.ap, x2.ap, AF.Exp, [x2.res], [dti.res])
                if mcstop == 1:
                    continue
                yield
                knz = knzr.next()
                knzv = knz.ap.rearrange("p (r c n) -> p r c n", r=2, c=2)
                for r in range(2):
                    ts("gpsimd", knz.ap[:, r * 256:(r + 1) * 256], kn.ap, bon.ap[:, r * 64:r * 64 + 1], None,
                       ALU.mult, None, [kn.res, bon.res], [knz.res])
                bK = bring.next()
                bQ = bring.next()
                for h in range(4):
                    c = h // 2
                    r = h % 2
                    mm(bK.ap[:, h * 128:(h + 1) * 128], knzv[:, r, c, :], knv[:, c, :], True, True,
                       [knz.res, kn.res], [bK.res])
                    mm(bQ.ap[:, h * 128:(h + 1) * 128], knzv[:, r, c, :], qnv[:, c, :], True, True,
                       [knz.res, qn.res], [bQ.res])
                if mcstop == 20:
                    continue
                tmpf = tmr.next()
                tt("vector", tmpf.ap, bK.ap, dst.ap, ALU.mult, [bK.res, dst.res], [tmpf.res])
                yield
                if mcstop == 21:
                    continue
                Lm = Lr.next()
                tt("gpsimd", Lm.ap.rearrange("p (h n) -> p h n", h=4), tmpf.ap.rearrange("p (h n) -> p h n", h=4),
                   bc(beta, 128), ALU.mult, [tmpf.res, gb.res], [Lm.res])
                if mcstop == 22:
                    continue
                AT = ATr.next()
                tt("vector", AT.ap, bQ.ap, dti.ap, ALU.mult, [bQ.res, dti.res], [AT.res])
                if mcstop == 2:
                    continue
                yield
                tbank = bring.next()
                tv = bfview(tbank)
                for h in range(4):
                    tr(tv[:, h * 128:(h + 1) * 128], Lm.ap[:, h * 128:(h + 1) * 128], ident_b.ap,
                       [Lm.res, ident_b.res], [tbank.res])
                LT = LTr.next()
                cp("scalar", LT.ap, tv[:, 0:512], [tbank.res], [LT.res])
                if mcstop == 3:
                    continue
                yield
                act(g8.ap[:, 8:12], g8.ap[:, 0:4], AF.Exp, [g8.res], [g8.res])
                tt("vector", g8.ap[:, 12:16], g8.ap[:, 8:12], beta, ALU.mult, [g8.res, gb.res], [g8.res])
                tt("vector", g8.ap[:, 16:20], g8.ap[:, 4:8], g8.ap[:, 0:4], ALU.subtract, [g8.res], [g8.res])
                act(g8.ap[:, 16:20], g8.ap[:, 16:20], AF.Exp, [g8.res], [g8.res])
                gt2 = g8.ap[:, 4:8].rearrange("p (c r) -> p c r", r=2)
                for r in range(2):
                    rs_ = slice(r * 64, (r + 1) * 64)
                    act(g8.ap[rs_, 20:22], gt2[rs_, :, r], AF.Exp, [g8.res], [g8.res])
                if mcstop == 4:
                    continue
                yield
                X = Xr.next()
                k5 = kv.ap[:, 0:256].rearrange("p (c r n) -> p c r n", c=2, r=2)
                v5 = kv.ap[:, 256:512].rearrange("p (c r n) -> p c r n", c=2, r=2)
                beta5 = beta.rearrange("p (c r) -> p c r", r=2)
                bg5 = g8.ap[:, 12:16].rearrange("p (c r) -> p c r", r=2)

                def x5(Xt):
                    return Xt.ap.rearrange("p (c r n) -> p c r n", c=2, r=2)

                for r in range(2):
                    uo = 64 if r == 0 else 0
                    wo = 0 if r == 0 else 64
                    tt("gpsimd", x5(X)[:, :, r, uo:uo + 64], v5[:, :, r, :], bc(beta5[:, :, r], 64), ALU.mult,
                       [kv.res, gb.res], [X.res])
                    tt("gpsimd", x5(X)[:, :, r, wo:wo + 64], k5[:, :, r, :], bc(bg5[:, :, r], 64), ALU.mult,
                       [kv.res, g8.res], [X.res])
                if mcstop == 5:
                    continue
                yield
                def h4(t):
                    return t.ap.rearrange("p (h n) -> p h n", h=4)

                def mbc(m):
                    return m.ap.unsqueeze(1).broadcast_to([128, 4, 128])

                def hmm(lhsT_t, rhs_t):
                    bkx = bring.next()
                    for h in range(4):
                        hs = slice(h * 128, (h + 1) * 128)
                        mm(bkx.ap[:, hs], lhsT_t.ap[:, hs], rhs_t.ap[:, hs], True, True,
                           [lhsT_t.res, rhs_t.res], [bkx.res])
                    return bkx

                L0 = Mr.next()
                tt("gpsimd", h4(L0), h4(Lm), mbc(B16), ALU.mult, [Lm.res, B16.res], [L0.res])
                L0T = Mr.next()
                tt("gpsimd", h4(L0T), h4(LT), mbc(B16), ALU.mult, [LT.res, B16.res], [L0T.res])
                Pk, PkT = L0, L0T
                Ps = []
                for j in range(3):
                    bM = hmm(PkT, Pk)
                    bMT = hmm(Pk, PkT)
                    Pn = Mr.next()
                    cp("scalar", Pn.ap, bM.ap, [bM.res], [Pn.res])
                    PnT = Mr.next()
                    cp("scalar", PnT.ap, bMT.ap, [bMT.res], [PnT.res])
                    Ps.append((Pn, PnT))
                    Pk, PkT = Pn, PnT
                    yield
                G = Mr.next()
                tt("gpsimd", h4(G), mbc(ident_b), h4(L0T), ALU.subtract, [ident_b.res, L0T.res], [G.res])
                H = Mr.next()
                tt("gpsimd", h4(H), mbc(ident_b), h4(L0), ALU.subtract, [ident_b.res, L0.res], [H.res])
                for (Pn, PnT) in Ps:
                    bG = hmm(Pn, G)
                    G2 = Mr.next()
                    tt("vector", G2.ap, G.ap, bG.ap, ALU.add, [G.res, bG.res], [G2.res])
                    G = G2
                    yield
                    bH = hmm(PnT, H)
                    H2 = Mr.next()
                    tt("vector", H2.ap, H.ap, bH.ap, ALU.add, [H.res, bH.res], [H2.res])
                    H = H2
                    yield
                Dm, DT = H, G
                for lev, Mk in enumerate((MK1, MK2, MK3)):
                    lastl = lev == 2
                    Ck = Mr.next()
                    tt("gpsimd", h4(Ck), h4(Lm), mbc(Mk), ALU.mult, [Lm.res, Mk.res], [Ck.res])
                    if not lastl:
                        CkT = Mr.next()
                        tt("vector", h4(CkT), h4(LT), mbc(Mk), ALU.mult, [LT.res, Mk.res], [CkT.res])
                    yield
                    bY = hmm(Ck, DT)
                    if not lastl:
                        bY2 = hmm(CkT, Dm)
                    YT = Mr.next()
                    cp("scalar", YT.ap, bY.ap, [bY.res], [YT.res])
                    if not lastl:
                        Y = Mr.next()
                        cp("scalar", Y.ap, bY2.ap, [bY2.res], [Y.res])
                    yield
                    bZ = hmm(Dm, YT)
                    if not lastl:
                        bZ2 = hmm(DT, Y)
                    DT2 = Mr.next()
                    tt("vector", DT2.ap, DT.ap, bZ.ap, ALU.subtract, [DT.res, bZ.res], [DT2.res])
                    if not lastl:
                        D2 = Mr.next()
                        tt("vector", D2.ap, Dm.ap, bZ2.ap, ALU.subtract, [Dm.res, bZ2.res], [D2.res])
                        Dm = D2
                    DT = DT2
                    yield
                bX = hmm(DT, X)
                X2 = Xr.next()
                cp("scalar", X2.ap, bX.ap, [bX.res], [X2.res])
                X = X2
                if mcstop == 6:
                    continue
                yield
                tbank = bring.next()
                tv = bfview(tbank)
                for h in range(4):
                    tr(tv[:, h * 128:(h + 1) * 128], X.ap[:, h * 128:(h + 1) * 128], ident_b.ap,
                       [X.res, ident_b.res], [tbank.res])
                wT = wTr.next()
                wTv = wT.ap.rearrange("p (h n) -> p h n", h=4)
                cp("scalar", wT.ap, tv[:, 0:512], [tbank.res], [wT.res])
                yield
                kd = kdr.next()
                kdv = kd.ap.rearrange("p (h c) -> p h c", h=4)
                tt("gpsimd", kdv, k4, bc(g8.ap[:, 16:20], 64), ALU.mult, [kv.res, g8.res], [kd.res])
                if mcstop == 7:
                    continue
                yield
                bW = bring.next()
                for h in range(4):
                    hp = slice((h % 2) * 64, (h % 2) * 64 + 64)
                    c = h // 2
                    mm(bW.ap[:, h * 64:(h + 1) * 64], wTv[:, h, :], Sbv[:, h, :], True, True,
                       [wT.res, Sb.res], [bW.res])
                    mm(bW.ap[:, 256 + h * 64:256 + (h + 1) * 64], qnv[:, c, :], Sbv[:, h, :], True, True,
                       [qn.res, Sb.res], [bW.res])
                vn = vnr.next()
                vnv = vn.ap.rearrange("p (h c) -> p h c", h=4)
                vn5 = vn.ap.rearrange("p (c r n) -> p c r n", c=2, r=2)
                bW5 = bW.ap[:, 0:256].rearrange("p (c r n) -> p c r n", c=2, r=2)
                for r in range(2):
                    uo = 64 if r == 0 else 0
                    tt("vector", vn5[:, :, r, :], x5(X)[:, :, r, uo:uo + 64], bW5[:, :, r, :],
                       ALU.subtract, [X.res, bW.res], [vn.res])
                o1 = o1r.next()
                tt("vector", o1.ap.rearrange("p (h c) -> p h c", h=4),
                   bW.ap[:, 256:512].rearrange("p (h c) -> p h c", h=4), bc(g8.ap[:, 8:12], 64), ALU.mult,
                   [bW.res, g8.res], [o1.res])
                yield
                bO = bring.next()
                for h in range(4):
                    mm(bO.ap[:, h * 64:(h + 1) * 64], AT.ap[:, h * 128:(h + 1) * 128], vnv[:, h, :], True, True,
                       [AT.res, vn.res], [bO.res])
                for c in range(2):
                    mm(bO.ap[:, 256 + c * 128:256 + (c + 1) * 128], kd.ap[:, c * 128:(c + 1) * 128],
                       vn.ap[:, c * 128:(c + 1) * 128], True, True, [kd.res, vn.res], [bO.res])
                oo = oor.next()
                tt("vector", oo.ap, o1.ap, bO.ap[:, 0:256], ALU.add, [o1.res, bO.res], [oo.res])
                yield
                bOs = bO.ap[:, 256:512].rearrange("p (c n) -> p c n", c=2)
                for r in range(2):
                    rs_ = slice(r * 64, (r + 1) * 64)
                    for c in range(2):
                        stt("vector", Sfv[rs_, 2 * c + r, :], Sfv[rs_, 2 * c + r, :], g8.ap[rs_, 20 + c:21 + c],
                            bOs[rs_, c, r * 64:(r + 1) * 64], ALU.mult, ALU.add,
                            [Sf.res, g8.res, bO.res], [Sf.res])
                cp("scalar", Sb.ap, Sf.ap, [Sf.res], [Sb.res])
                if d == 0:
                    store(oo, ofd[tb:tb + 128, :], oo.ap, [DR("ofd", tb // 128)])
                else:
                    store(oo, obd[tb:tb + 128, :], oo.ap, [DR("obd", tb // 128)])
                yield


    def phase_MCx(l):
        A.reset()
        sh = {}
        mk_f = A.f32(512, "mk_f")
        load(mk_f, mk_f.ap, c_blk)
        for i_, nm_ in enumerate(("B16", "MK1", "MK2", "MK3")):
            t_ = A.bf16(128, nm_)
            cp("vector", t_.ap, mk_f.ap[:, i_ * 128:(i_ + 1) * 128], [mk_f.res], [t_.res])
            sh[nm_] = t_
        sh["onesb"] = A.bf16(128, "onesb")
        memset("vector", sh["onesb"].ap, 1.0, [sh["onesb"].res])
        sh["nonesb"] = A.bf16(128, "nonesb")
        memset("vector", sh["nonesb"].ap, -1.0, [sh["nonesb"].res])
        sh["bon"] = A.f32(128, "bonesMC")
        load(sh["bon"], sh["bon"].ap, c_bones)
        for d in range(2):
            for nm_, src in (("U", c_U), ("PM", c_PM), ("NM", c_NM)):
                t_ = A.f32(128, nm_ + str(d))
                load(t_, t_.ap, src[d])
                sh[nm_ + str(d)] = t_
            ub = A.bf16(128, "Ub%d" % d)
            cp("vector", ub.ap, sh["U%d" % d].ap, [sh["U%d" % d].res], [ub.res])
            sh["Ub%d" % d] = ub
        gens = [mc_gen(l, 0, 0, banks[0:2], sh), mc_gen(l, 1, 0, banks[2:4], sh),
                mc_gen(l, 0, 1, banks[4:6], sh), mc_gen(l, 1, 1, banks[6:8], sh)]
        while gens:
            for g_ in list(gens):
                try:
                    next(g_)
                except StopIteration:
                    gens.remove(g_)
        S.barrier()

    def phase_MCz(l):
        A.reset()
        ng8 = A.f32(256, "ng8")
        bload(ng8, norm_g[l:l + 1, :])
        ts("vector", ng8.ap, ng8.ap, 8.0, None, ALU.mult, None, [ng8.res], [ng8.res])
        ofr = Ring([A.f32(256, "ofl%d" % i) for i in range(3)])
        obr = Ring([A.f32(256, "obl%d" % i) for i in range(3)])
        zr = Ring([A.f32(256, "zl%d" % i) for i in range(3)])
        sqr = Ring([A.f32(256, "osq%d" % i) for i in range(2)])
        smr = Ring([A.f32(24, "smz%d" % i) for i in range(3)])
        ycr = Ring([A.bf16(256, "yc%d" % i) for i in range(2)])
        yctr = Ring([A.bf16(256, "yct%d" % i) for i in range(2)])
        tbr = Ring(banks[0:4])
        for si, (soff, slen) in enumerate(cfg.seqs):
            for ib in range(slen // 128):
                tb = soff + ib * 128
                oo = obr.next()
                load(oo, oo.ap, obd[tb:tb + 128, :], R=[DR("obd", tb // 128)])
                ofl = ofr.next()
                load(ofl, ofl.ap, ofd[tb:tb + 128, :], R=[DR("ofd", tb // 128)])
                zl = zr.next()
                load(zl, zl.ap, Pzg[tb:tb + 128, 0:256], R=[DR("Pzg", tb // 128)])
                tt("gpsimd", oo.ap, oo.ap, ofl.ap, ALU.add, [oo.res, ofl.res], [oo.res])
                osq = sqr.next()
                tt("gpsimd", osq.ap, oo.ap, oo.ap, ALU.mult, [oo.res], [osq.res])
                sm = smr.next()
                S.op("vector", (lambda o, i: (lambda e: e.tensor_reduce(o, i, AX.X, ALU.add)))(
                    sm.ap[:, 0:4], osq.ap.rearrange("p (h c) -> p h c", h=4)), [osq.res], [sm.res])
                rsqrt_eps(sm.ap[:, 4:8], sm.ap[:, 0:4], 64.0 * RMS_EPS, [sm.res], [sm.res])
                tt("vector", oo.ap.rearrange("p (h c) -> p h c", h=4),
                   oo.ap.rearrange("p (h c) -> p h c", h=4), bc(sm.ap[:, 4:8], 64), ALU.mult,
                   [oo.res, sm.res], [oo.res])
                tt("gpsimd", oo.ap, oo.ap, ng8.ap, ALU.mult, [oo.res, ng8.res], [oo.res])
                act(zl.ap, zl.ap, AF.Silu, [zl.res], [zl.res])
                yc = ycr.next()
                tt("gpsimd", yc.ap, oo.ap, zl.ap, ALU.mult, [oo.res, zl.res], [yc.res])
                tbank = tbr.next()
                tv = bfview(tbank)
                for c in range(2):
                    tr(tv[:, c * 128:(c + 1) * 128], yc.ap[:, c * 128:(c + 1) * 128], ident_b.ap,
                       [yc.res, ident_b.res], [tbank.res])
                yct = yctr.next()
                cp("scalar", yct.ap, tv[:, 0:256], [tbank.res], [yct.res])
                store(yct, ycT[6:8, :, tb:tb + 128].rearrange("m p t -> p m t"),
                      yct.ap.rearrange("p (m n) -> p m n", m=2), [DR("ycT67_", tb // 128)])
        S.barrier()

    def phase_O1(l):
        A.reset()
        cur = xbuf[l % 2]
        Wo = A.bf16(8 * D, "Wo")
        Wov = Wo.ap.rearrange("p (k n) -> p k n", k=8)
        for kc in range(8):
            load(Wo, Wov[:, kc, :], w_out[l, kc * 128:(kc + 1) * 128, :], eng="gpsimd")
        G1 = A.f32(D, "G1")
        g1 = A.f32(D, "ln1g")
        b1_ = A.f32(D, "ln1b")
        bload(g1, ln1_g[l:l + 1, :])
        bload(b1_, ln1_b[l:l + 1, :])
        ycr = Ring([A.bf16(8 * 512, "ycl%d" % i) for i in range(2)])
        xr = Ring([A.f32(D, "xo%d" % i) for i in range(3)])
        tr_ = Ring([A.f32(D, "to%d" % i) for i in range(2)])
        rr = Ring([A.f32(D, "ro%d" % i) for i in range(2)])
        sr = Ring([A.f32(D, "so%d" % i) for i in range(2)])
        outr = Ring([A.f32(D, "oo%d" % i) for i in range(2)])
        smalls = Ring([A.f32(24, "small%d" % i) for i in range(2)])
        bring = Ring(banks)
        for si, (soff, slen) in enumerate(cfg.seqs):
            load(G1, G1.ap, modrow[l, si:si + 1, 2 * D:3 * D].broadcast_to([128, D]), R=[DR("modrow", l)])
            ts("vector", G1.ap, G1.ap, 1.0, None, ALU.add, None, [G1.res], [G1.res])
            for gi in range(slen // 512):
                t0 = soff + gi * 512
                yc = ycr.next()
                ycv = yc.ap.rearrange("p (m n) -> p m n", m=8)
                load(yc, ycv, ycT[:, :, t0:t0 + 512].rearrange("m p t -> p m t"),
                     R=[DR("ycT%d_" % s_, t0 // 512) for s_ in range(6)] +
                       [DR("ycT67_", t0 // 128 + b_) for b_ in range(4)])
                for b in range(4):
                    tb = t0 + b * 128
                    x = xr.next()
                    load(x, x.ap, cur[tb:tb + 128, :], R=[DR("x%d" % (l % 2), tb // 128)])
                    tmp = tr_.next()
                    for nh in range(2):
                        bk = bring.next()
                        for kc in range(8):
                            mm(bk.ap, ycv[:, kc, b * 128:(b + 1) * 128], Wov[:, kc, nh * 512:(nh + 1) * 512],
                               kc == 0, kc == 7, [yc.res, Wo.res], [bk.res])
                        tt("vector", tmp.ap[:, nh * 512:(nh + 1) * 512], bk.ap, G1.ap[:, nh * 512:(nh + 1) * 512],
                           ALU.mult, [bk.res, G1.res], [tmp.res])
                    r = rr.next()
                    stt("vector", r.ap, x.ap, ALPHA, tmp.ap, ALU.mult, ALU.add, [x.res, tmp.res], [r.res])
                    o = outr.next()
                    layer_norm(r, o, g1, b1_, smalls.next(), sr.next())
                    store(o, x1buf[tb:tb + 128, :], o.ap, [DR("x1", tb // 128)])
        S.barrier()

    def phase_O2(l):
        A.reset()
        last = l == nl - 1
        nxt = xbuf[(l + 1) % 2]
        W1 = A.bf16(8 * DFF, "W1")
        W1v = W1.ap.rearrange("p (k n) -> p k n", k=8)
        for kc in range(8):
            load(W1, W1v[:, kc, :], w1[l, kc * 128:(kc + 1) * 128, :], eng="gpsimd")
        W2 = A.bf16(32 * D, "W2")
        W2v = W2.ap.rearrange("p (k n) -> p k n", k=32)
        for kc in range(32):
            load(W2, W2v[:, kc, :], w2[l, kc * 128:(kc + 1) * 128, :], eng="gpsimd")
        b1t = A.f32(32, "b1c")
        load(b1t, b1t.ap, b1c[l])
        G2 = A.f32(D, "G2")
        g2 = A.f32(D, "ln2g")
        b2_ = A.f32(D, "ln2b")
        bb2 = A.f32(D, "b2")
        bload(g2, ln2_g[l:l + 1, :])
        bload(b2_, ln2_b[l:l + 1, :])
        bload(bb2, b2[l:l + 1, :])
        xr = Ring([A.f32(D, "x1_%d" % i) for i in range(4)])
        hr = Ring([A.bf16(8 * 256, "h2T%d" % i) for i in range(2)])
        ar = Ring([A.f32(256, "aa%d" % i) for i in range(4)])
        ur = Ring([A.bf16(256, "uu%d" % i) for i in range(5)])
        tr_ = Ring([A.f32(D, "t2%d" % i) for i in range(2)])
        sr = Ring([A.f32(D, "s2%d" % i) for i in range(1)])
        smalls = Ring([A.f32(24, "small%d" % i) for i in range(2)])
        ring1 = Ring(banks[4:8])
        for si, (soff, slen) in enumerate(cfg.seqs):
            load(G2, G2.ap, modrow[l, si:si + 1, 5 * D:6 * D].broadcast_to([128, D]), R=[DR("modrow", l)])
            ts("vector", G2.ap, G2.ap, 1.0, None, ALU.add, None, [G2.res], [G2.res])
            dst = (ys, yp)[si]
            for gi in range(slen // 256):
                t0 = soff + gi * 256
                xg = []
                for b in range(2):
                    x = xr.next()
                    tb = t0 + b * 128
                    load(x, x.ap, x1buf[tb:tb + 128, :], R=[DR("x1", tb // 128)])
                    xg.append(x)
                hT = hr.next()
                hv = hT.ap.rearrange("p (k n) -> p k n", k=8)
                for kc in range(8):
                    bk = ring1.next()
                    for b in range(2):
                        tr(bk.ap[:, b * 128:(b + 1) * 128], xg[b].ap[:, kc * 128:(kc + 1) * 128], ident_f.ap,
                           [xg[b].res, ident_f.res], [bk.res])
                    act(hv[:, kc, :], bk.ap[:, 0:256], AF.Identity, [bk.res, modcol.res, modcp1.res], [hT.res],
                        bias=mcv[:, l, 3 * 8 + kc, si:si + 1], scale=mc1v[:, l, 4 * 8 + kc, si:si + 1])

                def ffn1(m):
                    bk = ring1.next()
                    for kc in range(8):
                        mm(bk.ap[:, 0:256], W1v[:, kc, m * 128:(m + 1) * 128], hv[:, kc, :], kc == 0, kc == 7,
                           [W1.res, hT.res], [bk.res])
                    a = ar.next()
                    act(a.ap, bk.ap[:, 0:256], AF.Relu, [bk.res, b1t.res], [a.res], bias=b1t.ap[:, m:m + 1])
                    u = ur.next()
                    tt("gpsimd", u.ap, a.ap, a.ap, ALU.mult, [a.res], [u.res])
                    return u

                us = [ffn1(0), ffn1(1)]
                for m in range(32):
                    u = us.pop(0)
                    if m + 2 < 32:
                        us.append(ffn1(m + 2))
                    for b in range(2):
                        for nh in range(2):
                            bk = banks[b * 2 + nh]
                            mm(bk.ap, u.ap[:, b * 128:(b + 1) * 128], W2v[:, m, nh * 512:(nh + 1) * 512],
                               m == 0, m == 31, [u.res, W2.res], [bk.res])
                for b in range(2):
                    tb = t0 + b * 128
                    tmp = tr_.next()
                    for nh in range(2):
                        bk = banks[b * 2 + nh]
                        hs = slice(nh * 512, (nh + 1) * 512)
                        tt("vector", tmp.ap[:, hs], bk.ap, bb2.ap[:, hs], ALU.add, [bk.res, bb2.res], [tmp.res])
                    tt("gpsimd", tmp.ap, tmp.ap, G2.ap, ALU.mult, [tmp.res, G2.res], [tmp.res])
                    stt("vector", tmp.ap, xg[b].ap, ALPHA, tmp.ap, ALU.mult, ALU.add, [xg[b].res, tmp.res], [tmp.res])
                    layer_norm(tmp, xg[b], g2, b2_, smalls.next(), sr.next())
                    if last:
                        store(xg[b], dst[tb - soff:tb - soff + 128, :], xg[b].ap, [DR("yout", tb // 128)])
                    else:
                        store(xg[b], nxt[tb:tb + 128, :], xg[b].ap, [DR("x%d" % ((l + 1) % 2), tb // 128)])
        S.barrier()

    def want(p):
        return cfg.phases is None or p in cfg.phases

    if want("mod"):
        phase_mod()
    for l in range(nl):
        if want("P"):
            phase_P(l)
        if want("MA"):
            phase_MA(l)
        if want("MB"):
            phase_MB(l)
        if want("MC0"):
            phase_MC0(l)
        if want("MCx"):
            phase_MCx(l)
        if want("MCz"):
            phase_MCz(l)
        if want("O1"):
            phase_O1(l)
        if want("O2"):
            phase_O2(l)
    S.emit(nc, es)
    es.close()
    return nc


def _consts():
    import ml_dtypes
    c = {}
    c["c_ident"] = np.eye(128, dtype=np.float32)
    p = np.arange(128)[:, None]
    q = np.arange(128)[None, :]
    slopes = np.exp2(-np.arange(1, 9, dtype=np.float64))
    em = np.zeros((128, 8, 3, 128), np.float64)
    for r in range(3):
        dist = np.abs((r - 1) * 128 + p - q)
        valid = dist <= 128
        for h in range(8):
            em[:, h, r, :] = np.where(valid, np.exp(-slopes[h] * dist), 0.0)
    c["c_emask"] = em.reshape(128, -1).astype(ml_dtypes.bfloat16)
    bo = np.zeros((128, 128), np.float32)
    bo[:64, :64] = 1.0
    bo[64:, 64:] = 1.0
    c["c_bones"] = bo
    c["c_ones"] = np.ones((128, 128), np.float32)
    s = np.arange(128)[:, None]
    cc = np.arange(128)[None, :]
    U = np.stack([(s <= cc), (s >= cc)]).astype(np.float32)
    c["c_U"] = U
    PM = np.stack([np.where(cc < s, 0.0, BIG), np.where(cc > s, 0.0, BIG)]).astype(np.float32)
    NM = np.stack([np.where(cc >= s, 0.0, -BIG), np.where(cc <= s, 0.0, -BIG)]).astype(np.float32)
    c["c_PM"] = PM
    c["c_NM"] = NM
    pi = np.arange(128)[:, None]
    ji = np.arange(128)[None, :]
    Bb = lambda b: (pi // b == ji // b).astype(np.float32)
    c["c_blk"] = np.concatenate([Bb(16), Bb(32) - Bb(16), Bb(64) - Bb(32), 1.0 - Bb(64)], axis=1).astype(np.float32)
    sel = np.zeros((4, 2, 4, 128), np.float32)
    for h in range(4):
        sel[h, 0, h, :] = 1.0
        sel[h, 1, h, :] = -1.0
    c["c_sel"] = sel.reshape(4, -1)
    return c


def _win_perm():
    cols = []
    cols += list(range(0, 768))
    for j in range(4):
        cols += list(range(OFF_ATT + j * 64, OFF_ATT + (j + 1) * 64))
        cols += list(range(OFF_ATT + (4 + j) * 64, OFF_ATT + (5 + j) * 64))
    cols += list(range(OFF_ATT + 512, OFF_ATT + 640))
    cols += list(range(OFF_DN, OFF_DN + 768))
    cols += list(range(OFF_ATT + 640, OFF_ATT + 768))
    cols += list(range(OFF_DN + 768, OFF_DN + 1024))
    cols += list(range(OFF_DN + 1024, OFF_DN + 1040))
    assert len(cols) == DIN and len(set(cols)) == DIN
    return np.array(cols)


def make_in_maps(cfg, inp, ncores=8):
    f = lambda a: np.ascontiguousarray(np.asarray(a, dtype=np.float32))
    nl = cfg.nl
    shared = {}
    shared["ln_in_g"] = f(inp["ln_in_g"]).reshape(1, D)
    shared["ln_in_b"] = f(inp["ln_in_b"]).reshape(1, D)
    shared["w_mod"] = f(inp["w_mod"][:nl])
    shared["b_mod"] = f(inp["b_mod"][:nl])
    shared["b_modc"] = f(np.asarray(inp["b_mod"])[:nl].reshape(nl, 48, 128).transpose(0, 2, 1))
    shared["w_in"] = f(np.asarray(inp["w_in"])[:nl][:, :, _win_perm()])
    ca = np.asarray(inp["conv_a_w"])[:nl]
    shared["conv_a"] = f(ca.reshape(nl, 3, 2, 128).transpose(0, 3, 2, 1))
    shared["sink"] = f(inp["attn_sink"][:nl])
    dc = np.asarray(inp["dn_conv_w"])[:nl]
    shared["dn_conv"] = f(dc.reshape(nl, 3, 6, 128).transpose(0, 3, 2, 1))
    shared["a_log"] = f(np.concatenate([inp["dn_a_log_f"][:nl], inp["dn_a_log_b"][:nl]], axis=1))
    shared["dt_bias"] = f(np.concatenate([inp["dn_dt_bias_f"][:nl], inp["dn_dt_bias_b"][:nl]], axis=1))
    shared["norm_g"] = f(np.tile(np.asarray(inp["dn_norm_g"])[:nl], (1, 4)))
    shared["w_out"] = f(inp["w_out"][:nl])
    shared["ln1_g"] = f(inp["ln1_g"][:nl])
    shared["ln1_b"] = f(inp["ln1_b"][:nl])
    shared["w1"] = f(inp["w1"][:nl])
    shared["b1c"] = f(np.asarray(inp["b1"])[:nl].reshape(nl, 32, 128).transpose(0, 2, 1))
    shared["w2"] = f(inp["w2"][:nl])
    shared["b2"] = f(inp["b2"][:nl])
    shared["ln2_g"] = f(inp["ln2_g"][:nl])
    shared["ln2_b"] = f(inp["ln2_b"][:nl])
    shared.update(_consts())
    xs_all = np.asarray(inp["x_sample"])
    xp_all = np.asarray(inp["x_prompt"])
    cs_all = np.asarray(inp["c_sample"])
    cp_all = np.asarray(inp["c_prompt"])
    maps = []
    for i in range(ncores):
        m = dict(shared)
        m["xs"] = f(xs_all[i % xs_all.shape[0]])
        m["xp"] = f(xp_all[(i // 2) % xp_all.shape[0]])
        c2 = np.stack([cs_all[i % cs_all.shape[0]], cp_all[(i // 2) % cp_all.shape[0]]], axis=1)
        m["cT"] = f(c2.reshape(8, 128, 2).transpose(1, 0, 2))
        maps.append(m)
    return maps


_NC_CACHE = {}


def kernel(**inputs):
    cfg = Cfg()
    if "nc" not in _NC_CACHE:
        _NC_CACHE["nc"] = build(cfg)
    nc = _NC_CACHE["nc"]
    maps = make_in_maps(cfg, inputs, 8)
    res = run_bass_kernel_spmd(nc, maps, core_ids=list(range(8)))
    y_s = np.stack([np.asarray(res.results[i]["ys"], dtype=np.float32) for i in range(8)], axis=0)
    y_p = np.stack([np.asarray(res.results[2 * i]["yp"], dtype=np.float32) for i in range(4)], axis=0)
    return (y_p, y_s)
```

```python
import numpy as np
import os as _os
from contextlib import ExitStack
import concourse.bass as bass
import concourse.mybir as mybir
from concourse.bass_utils import run_bass_kernel_spmd

F32 = mybir.dt.float32
BF16 = mybir.dt.bfloat16
AF = mybir.ActivationFunctionType
ALU = mybir.AluOpType
AX = mybir.AxisListType

D = 1024
DFF = 4096
DIN = 2576
NL = 4
ALPHA = (2.0 * 4) ** 0.25
LN_EPS = 1e-5
RMS_EPS = 1e-6
OFF_ATT = 768
OFF_DN = 1536
NFM = 17
NTM = 400
NPT = 15
BIG = 30000.0


class Res:
    __slots__ = ("name", "w", "rs", "chan")

    def __init__(self, name=""):
        self.name = name
        self.w = None
        self.rs = {}
        self.chan = {}


class Op:
    __slots__ = ("eng", "fn", "chan", "deps", "signal", "sem", "val", "idx")

    def __init__(self, eng, fn, chan):
        self.eng = eng
        self.fn = fn
        self.chan = chan
        self.deps = []
        self.signal = chan is not None
        self.sem = None
        self.val = 0


class Sched:
    ENGS = ("tensor", "vector", "scalar", "gpsimd", "sync")

    def __init__(self):
        self.ops = {e: [] for e in self.ENGS}
        self.all = []
        self.chan_last = {}
        self.fence = []
        self.nchan = 0

    def op(self, eng, fn, R=(), W=(), chan=None):
        o = Op(eng, fn, chan)
        deps = {}

        pos = len(self.ops[eng])

        def add(d):
            if d is None:
                return
            if d.chan is None and d.eng == eng:
                if eng == "tensor":
                    return
                if eng in ("vector", "scalar") and pos - d.idx >= 3:
                    return
            deps[id(d)] = d

        for r in R:
            add(r.w)
        for w in W:
            add(w.w)
            for d in w.rs.values():
                add(d)
        if chan is not None:
            add(self.chan_last.get(chan))
            self.chan_last[chan] = o
        for d in self.fence:
            if d is not o:
                add(d)
        o.idx = pos
        o.deps = list(deps.values())
        for d in o.deps:
            d.signal = True
        key = eng if chan is None else ("dma", chan)
        for r in R:
            r.rs[key] = o
        for w in W:
            w.w = o
            w.rs = {}
        self.ops[eng].append(o)
        self.all.append(o)
        return o

    def barrier(self):
        f = []
        for e in self.ENGS:
            for o in reversed(self.ops[e]):
                if o.chan is None:
                    f.append(o)
                    break
        f.extend(self.chan_last.values())
        for o in f:
            o.signal = True
        self.fence = f

    def emit(self, nc, es):
        esem = {}
        for e in self.ENGS:
            if e != "sync":
                esem[e] = es.enter_context(nc.semaphore("sem_" + e))
        chans = sorted(self.chan_last.keys())
        nsem = min(len(chans), 90)
        csem_list = [es.enter_context(nc.semaphore("semd%d" % i)) for i in range(nsem)]
        assert len(chans) <= 90, len(chans)
        csem = {c: csem_list[i] for i, c in enumerate(chans)}
        ccount = {c: 0 for c in chans}
        for o in self.all:
            if o.chan is not None:
                ccount[o.chan] += 16
                o.sem = csem[o.chan]
                o.val = ccount[o.chan]
        for e in self.ENGS:
            cnt = 0
            for o in self.ops[e]:
                if o.chan is None and o.signal:
                    cnt += 1
                    o.sem = esem[e]
                    o.val = cnt
        block = es.enter_context(nc.Block())

        def run(engname):
            def body(e):
                waited = {}
                for o in self.ops[engname]:
                    need = {}
                    for d in o.deps:
                        k = id(d.sem)
                        if waited.get(k, 0) >= d.val:
                            continue
                        if k not in need or need[k][1] < d.val:
                            need[k] = (d.sem, d.val)
                    for k, (s, v) in need.items():
                        e.wait_ge(s, v)
                        waited[k] = v
                    ins = o.fn(e)
                    if o.chan is not None:
                        ins.then_inc(o.sem, 16)
                    elif o.signal:
                        ins.then_inc(o.sem, 1)
                last = {}
                for o in self.ops[engname]:
                    if o.chan is not None:
                        last[id(o.sem)] = (o.sem, max(o.val, last.get(id(o.sem), (None, 0))[1]))
                for k, (s, v) in last.items():
                    if waited.get(k, 0) < v:
                        e.wait_ge(s, v)
            return body

        block.sync(run("sync"))
        block.gpsimd(run("gpsimd"))
        block.scalar(run("scalar"))
        block.vector(run("vector"))
        block.tensor(run("tensor"))


class T:
    __slots__ = ("ap", "res")

    def __init__(self, ap, res):
        self.ap = ap
        self.res = res


class Arena:
    def __init__(self, ap2d, ncols):
        self.ap = ap2d
        self.n = ncols
        self.base = 0
        self.top = 0

    def mark(self):
        self.base = self.top

    def reset(self):
        self.top = self.base

    def f32(self, cols, name=""):
        a = self.top
        self.top += cols
        assert self.top <= self.n, ("arena overflow", name, self.top, self.n)
        return T(self.ap[:, a:a + cols], Res(name))

    def bf16(self, cols, name=""):
        c32 = (cols + 1) // 2
        a = self.top
        self.top += c32
        assert self.top <= self.n, ("arena overflow", name, self.top, self.n)
        return T(self.ap[:, a:a + c32].bitcast(BF16)[:, 0:cols], Res(name))


class Ring:
    def __init__(self, tiles):
        self.tiles = tiles
        self.i = 0

    def next(self):
        t = self.tiles[self.i % len(self.tiles)]
        self.i += 1
        return t


class Cfg:
    def __init__(self, Ss=8192, Sp=4096, nl=NL, dbg=False, phases=None):
        self.phases = phases
        self.Ss = Ss
        self.Sp = Sp
        self.nl = nl
        self.St = Ss + Sp
        self.seqs = [(0, Ss), (Ss, Sp)]
        self.dbg = dbg


def build(cfg):
    nc = bass.Bass("TRN2", target_bir_lowering=False)
    S = Sched()
    St = cfg.St
    nl = cfg.nl

    def din(name, shape, dt=F32):
        return nc.dram_tensor(name, list(shape), dt, kind="ExternalInput").ap()

    def dscr(name, shape, dt=F32):
        kind = "ExternalOutput" if cfg.dbg else "Internal"
        return nc.dram_tensor(name, list(shape), dt, kind=kind).ap()

    xs = din("xs", [cfg.Ss, D])
    xp = din("xp", [cfg.Sp, D])
    cT = din("cT", [128, 8, 2])
    ln_in_g = din("ln_in_g", [1, D])
    ln_in_b = din("ln_in_b", [1, D])
    w_mod = din("w_mod", [nl, D, 6 * D])
    b_mod = din("b_mod", [nl, 6 * D])
    b_modc = din("b_modc", [nl, 128, 48])
    w_in = din("w_in", [nl, D, DIN])
    conv_a = din("conv_a", [nl, 128, 2, 3])
    sink = din("sink", [nl, 8])
    dn_conv = din("dn_conv", [nl, 128, 6, 3])
    a_log = din("a_log", [nl, 8])
    dt_bias = din("dt_bias", [nl, 8])
    norm_g = din("norm_g", [nl, 256])
    w_out = din("w_out", [nl, D, D])
    ln1_g = din("ln1_g", [nl, D])
    ln1_b = din("ln1_b", [nl, D])
    w1 = din("w1", [nl, D, DFF])
    b1c = din("b1c", [nl, 128, 32])
    w2 = din("w2", [nl, DFF, D])
    b2 = din("b2", [nl, D])
    ln2_g = din("ln2_g", [nl, D])
    ln2_b = din("ln2_b", [nl, D])
    c_ident = din("c_ident", [128, 128])
    c_emask = din("c_emask", [128, 8 * 3 * 128], BF16)
    c_bones = din("c_bones", [128, 128])
    c_ones = din("c_ones", [128, 128])
    c_U = din("c_U", [2, 128, 128])
    c_PM = din("c_PM", [2, 128, 128])
    c_NM = din("c_NM", [2, 128, 128])
    c_sel = din("c_sel", [4, 8 * 128])
    c_blk = din("c_blk", [128, 4 * 128])

    ys = nc.dram_tensor("ys", [cfg.Ss, D], F32, kind="ExternalOutput").ap()
    yp = nc.dram_tensor("yp", [cfg.Sp, D], F32, kind="ExternalOutput").ap()

    xbuf = [dscr("xbuf0", [St, D]), dscr("xbuf1", [St, D])]
    x1buf = dscr("x1buf", [St, D])
    modrow = dscr("modrow", [nl, 2, 6 * D])
    PT = dscr("PT", [NPT, 128, St], BF16)
    Pv = dscr("Pv", [St, 132], BF16)
    Pzg = dscr("Pzg", [St, 272])
    qnT = dscr("qnT", [2, 128, St], BF16)
    knT = dscr("knT", [2, 128, St], BF16)
    kvtm = dscr("kvtm", [St, 512], BF16)
    gbd = dscr("gbd", [St, 16])
    ofd = dscr("ofd", [St, 256])
    obd = dscr("obd", [St, 256])
    ycT = dscr("ycT", [8, 128, St], BF16)

    dres = {}

    def DR(name, i):
        k = (name, i)
        if k not in dres:
            dres[k] = Res("%s%s" % (name, i))
        return dres[k]

    es = ExitStack()
    ACOLS = 50000
    arena_t = es.enter_context(nc.sbuf_tensor("arena", [128, ACOLS], F32))
    A = Arena(arena_t[:], ACOLS)
    banks = []
    for i in range(8):
        pt = es.enter_context(nc.psum_tensor("psb%d" % i, [128, 512], F32))
        banks.append(T(pt[:], Res("bank%d" % i)))

    chan_cnt = {"sync": 0, "gpsimd": 0}

    def chan_of(t, eng):
        if eng not in t.res.chan:
            t.res.chan[eng] = (eng, chan_cnt[eng] % 44)
            chan_cnt[eng] += 1
        return t.res.chan[eng]

    def dma(eng, out, in_, R, W, chan, **kw):
        return S.op(eng, lambda e: e.dma_start(out=out, in_=in_, **kw), R, W, chan=chan)

    def load(t, out, in_, R=(), eng="sync", **kw):
        return dma(eng, out, in_, R, [t.res], chan_of(t, eng), **kw)

    def store(t, out, in_, W, eng="gpsimd", **kw):
        return dma(eng, out, in_, [t.res], W, chan_of(t, eng), **kw)

    def mm(out, lhsT, rhs, start, stop, R, W):
        return S.op("tensor", lambda e: e.matmul(out, lhsT, rhs, start=start, stop=stop), R, W)

    def tr(out, in_, ident, R, W):
        return S.op("tensor", lambda e: e.transpose(out, in_, ident), R, W)

    def act(out, in_, func, R, W, bias=None, scale=None):
        kw = {}
        if bias is not None:
            kw["bias"] = bias
        if scale is not None:
            kw["scale"] = scale
        return S.op("scalar", lambda e: e.activation(out, in_, func, **kw), R, W)

    def tt(eng, out, in0, in1, op, R, W):
        return S.op(eng, lambda e: e.tensor_tensor(out, in0, in1, op), R, W)

    def ts(eng, out, in0, s1, s2, op0, op1, R, W):
        if s2 is None:
            return S.op(eng, lambda e: e.tensor_scalar(out, in0, s1, None, op0), R, W)
        return S.op(eng, lambda e: e.tensor_scalar(out, in0, s1, s2, op0, op1), R, W)

    def stt(eng, out, in0, sc, in1, op0, op1, R, W):
        return S.op(eng, lambda e: e.scalar_tensor_tensor(out, in0, sc, in1, op0, op1), R, W)

    def cp(eng, out, in_, R, W):
        if eng == "scalar":
            return S.op(eng, lambda e: e.copy(out, in_), R, W)
        return S.op(eng, lambda e: e.tensor_copy(out, in_), R, W)

    def memset(eng, out, val, W):
        return S.op(eng, lambda e: e.memset(out, val), (), W)

    def rsqrt_eps(out, in_, eps, Rr, Wr):
        ts("vector", out, in_, eps, None, ALU.add, None, Rr, Wr)
        act(out, out, AF.Sqrt, Wr, Wr)
        S.op("vector", lambda e: e.reciprocal(out, out), Wr, Wr)

    def bc(ap2, n):
        return ap2.unsqueeze(2).broadcast_to([ap2.shape[0], ap2.shape[1], n])

    def bfview(bank):
        return bank.ap.bitcast(BF16)

    ident_f = A.f32(128, "ident_f")
    ident_b = A.bf16(128, "ident_b")
    modcol = A.f32(nl * 96, "modcol")
    modcp1 = A.f32(nl * 96, "modcp1")
    load(ident_f, ident_f.ap, c_ident)
    cp("vector", ident_b.ap, ident_f.ap, [ident_f.res], [ident_b.res])
    mcv = modcol.ap.rearrange("p (l j s) -> p l j s", l=nl, j=48)
    mc1v = modcp1.ap.rearrange("p (l j s) -> p l j s", l=nl, j=48)
    A.mark()

    bankring = Ring(banks)

    def phase_mod():
        A.reset()
        sc = A.f32(16, "sc")
        bm = A.f32(6 * D, "bm")
        mrow = A.f32(6 * D, "mrow")
        wts = Ring([A.f32(8 * 512, "wmod%d" % i) for i in range(2)])
        scv = sc.ap.rearrange("p (k s) -> p k s", k=8)
        load(sc, scv, cT)
        act(sc.ap, sc.ap, AF.Silu, [sc.res], [sc.res])
        mring = Ring(banks[0:6])
        bmc = A.f32(48, "bmc")
        for l in range(nl):
            load(bm, bm.ap[0:2, :], b_mod[l:l + 1, :].broadcast_to([2, 6 * D]))
            load(bmc, bmc.ap, b_modc[l])
            bkc = banks[7]
            for cc in range(12):
                wt = wts.next()
                wv = wt.ap.rearrange("p (k n) -> p k n", k=8)
                load(wt, wv, w_mod[l, :, cc * 512:(cc + 1) * 512].rearrange("(k p) n -> p k n", p=128))
                bk = mring.next()
                for kc in range(8):
                    mm(bk.ap[0:2, :], scv[:, kc, :], wv[:, kc, :], kc == 0, kc == 7,
                       [sc.res, wt.res], [bk.res])
                tt("vector", mrow.ap[0:2, cc * 512:(cc + 1) * 512], bk.ap[0:2, :],
                   bm.ap[0:2, cc * 512:(cc + 1) * 512], ALU.add, [bk.res, bm.res], [mrow.res])
                for j4 in range(4):
                    j = cc * 4 + j4
                    for kc in range(8):
                        mm(bkc.ap[:, 2 * j:2 * j + 2], wv[:, kc, j4 * 128:(j4 + 1) * 128], scv[:, kc, :],
                           kc == 0, kc == 7, [sc.res, wt.res], [bkc.res])
            store(mrow, modrow[l], mrow.ap[0:2, :], [DR("modrow", l)])
            tt("vector", modcol.ap[:, l * 96:(l + 1) * 96].rearrange("p (j s) -> p j s", s=2),
               bkc.ap[:, 0:96].rearrange("p (j s) -> p j s", s=2), bc(bmc.ap, 2), ALU.add,
               [bkc.res, bmc.res], [modcol.res])
        ts("vector", modcp1.ap, modcol.ap, 1.0, None, ALU.add, None, [modcol.res], [modcp1.res])
        if cfg.dbg:
            dbgmc = dscr("dbg_modcol", [128, nl * 96])
            store(modcp1, dbgmc, modcp1.ap, [DR("dbgmc", 0)])
        S.barrier()

    def layer_norm(r, out, gb, bb, small, scratch, eng2="gpsimd"):
        st = small.ap[:, 0:12].rearrange("p (c s) -> p c s", c=2)
        for c in range(2):
            S.op("vector", (lambda o, i: (lambda e: e.bn_stats(o, i)))(st[:, c, :], r.ap[:, c * 512:(c + 1) * 512]),
                 [r.res], [small.res])
        mv = small.ap[:, 12:14]
        S.op("vector", lambda e: e.bn_aggr(mv, small.ap[:, 0:12]), [small.res], [small.res])
        rstd = small.ap[:, 14:15]
        nb = small.ap[:, 15:16]
        rsqrt_eps(rstd, small.ap[:, 13:14], LN_EPS, [small.res], [small.res])
        stt("vector", nb, small.ap[:, 12:13], -1.0, rstd, ALU.mult, ALU.mult, [small.res], [small.res])
        act(scratch.ap, r.ap, AF.Identity, [r.res, small.res], [scratch.res], bias=nb, scale=rstd)
        tt("vector" if eng2 == "gpsimd" else eng2, scratch.ap, scratch.ap, gb.ap, ALU.mult, [scratch.res, gb.res], [scratch.res])
        tt(eng2, out.ap, scratch.ap, bb.ap, ALU.add, [scratch.res, bb.res], [out.res])

    def bload(t, src_row):
        n = src_row.shape[-1]
        load(t, t.ap, src_row.broadcast_to([128, n]))

    def phase_P(l):
        A.reset()
        cur = xbuf[l % 2]
        Wt = A.bf16(8 * DIN, "Wt")
        Wv = Wt.ap.rearrange("p (k n) -> p k n", k=8)
        for kc in range(8):
            load(Wt, Wv[:, kc, :], w_in[l, kc * 128:(kc + 1) * 128, :], eng="gpsimd")
        if l == 0:
            g_in = A.f32(D, "g_in")
            b_in = A.f32(D, "b_in")
            bload(g_in, ln_in_g)
            bload(b_in, ln_in_b)
            lnscr = A.f32(D, "lnscr")
            smalls = Ring([A.f32(24, "small%d" % i) for i in range(2)])
        xr = Ring([A.f32(D, "xg%d" % i) for i in range(8)])
        hr = Ring([A.bf16(8 * 512, "hT%d" % i) for i in range(2)])
        ptr = Ring([A.bf16(512, "pt%d" % i) for i in range(4)])
        cfr = Ring([A.f32(512, "cf%d" % i) for i in range(2)])
        vr = Ring([A.bf16(132, "va%d" % i) for i in range(2)])
        zr = Ring([A.f32(272, "zg%d" % i) for i in range(2)])
        for v in vr.tiles:
            memset("vector", v.ap, 1.0, [v.res])
        for si, (soff, slen) in enumerate(cfg.seqs):
            src = (xs, xp)[si]
            for gi in range(slen // 512):
                t0 = soff + gi * 512
                xg = []
                for b in range(4):
                    x = xr.next()
                    tb = t0 + b * 128
                    if l == 0:
                        load(x, x.ap, src[tb - soff:tb - soff + 128, :])
                        layer_norm(x, x, g_in, b_in, smalls.next(), lnscr)
                        store(x, cur[tb:tb + 128, :], x.ap, [DR("x%d" % (l % 2), tb // 128)])
                    else:
                        load(x, x.ap, cur[tb:tb + 128, :], R=[DR("x%d" % (l % 2), tb // 128)])
                    xg.append(x)
                import os as _os
                pstop = int(_os.environ.get("PSTOP", "9"))
                if pstop == 0:
                    continue
                hT = hr.next()
                hv = hT.ap.rearrange("p (k n) -> p k n", k=8)
                for kc in range(8):
                    bk = bankring.next()
                    for b in range(4):
                        tr(bk.ap[:, b * 128:(b + 1) * 128], xg[b].ap[:, kc * 128:(kc + 1) * 128], ident_f.ap,
                           [xg[b].res, ident_f.res], [bk.res])
                    act(hv[:, kc, :], bk.ap, AF.Identity, [bk.res, modcol.res, modcp1.res], [hT.res],
                        bias=mcv[:, l, 0 * 8 + kc, si:si + 1], scale=mc1v[:, l, 1 * 8 + kc, si:si + 1])
                if pstop == 1:
                    continue
                cu_c = {}
                for m in range(NFM):
                    bk = bankring.next()
                    for kc in range(8):
                        mm(bk.ap, Wv[:, kc, m * 128:(m + 1) * 128], hv[:, kc, :], kc == 0, kc == 7,
                           [Wt.res, hT.res], [bk.res])
                    if m in (2, 3):
                        cf = cfr.next()
                        cp("scalar", cf.ap, bk.ap, [bk.res], [cf.res])
                        cu_c[m] = cf
                        continue
                    pt = ptr.next()
                    if m in (4, 5):
                        cf = cu_c[m - 2]
                        tt("vector", pt.ap, bk.ap, cf.ap, ALU.mult, [bk.res, cf.res], [pt.res])
                        slot = m - 2
                    else:
                        if m % 2 == 0:
                            cp("scalar", pt.ap, bk.ap, [bk.res], [pt.res])
                        else:
                            cp("vector", pt.ap, bk.ap, [bk.res], [pt.res])
                        slot = m if m < 2 else m - 2
                    store(pt, PT[slot, :, t0:t0 + 512], pt.ap, [DR("PT%d_" % slot, t0 // 512)])
                if pstop == 2:
                    continue
                for b in range(4):
                    tb = t0 + b * 128
                    bk = bankring.next()
                    for kc in range(8):
                        mm(bk.ap[:, 0:NTM], hv[:, kc, b * 128:(b + 1) * 128], Wv[:, kc, NFM * 128:DIN],
                           kc == 0, kc == 7, [Wt.res, hT.res], [bk.res])
                    if pstop == 5:
                        continue
                    va = vr.next()
                    vv = va.ap.rearrange("p (g c) -> p g c", g=2)
                    cp("vector", vv[:, :, 0:64], bk.ap[:, 0:128].rearrange("p (g c) -> p g c", g=2),
                       [bk.res], [va.res])
                    if pstop == 6:
                        continue
                    zg = zr.next()
                    cp("vector", zg.ap, bk.ap[:, 128:NTM], [bk.res], [zg.res])
                    if pstop == 3:
                        continue
                    store(va, Pv[tb:tb + 128, :], va.ap, [DR("Pv", tb // 128)])
                    if pstop == 4:
                        continue
                    store(zg, Pzg[tb:tb + 128, :], zg.ap, [DR("Pzg", tb // 128)])
        S.barrier()

    def halo_load(t, view, slot_ap, t0, soff, slen, R_name, nblk512=True):
        lo = t0 - 1
        hi = t0 + 513
        a = 0
        b_ = 514
        if t0 == soff:
            memset("gpsimd", view[:, 0:1], 0.0, [t.res])
            lo = t0
            a = 1
        if t0 + 512 == soff + slen:
            memset("gpsimd", view[:, 513:514], 0.0, [t.res])
            hi = t0 + 512
            b_ = 513
        rs = [DR(R_name, g) for g in range(max(lo, soff) // 512, (hi - 1) // 512 + 1)]
        load(t, view[:, a:b_], slot_ap[:, lo:hi], R=rs)

    def phase_MA(l):
        A.reset()
        wa = A.f32(6, "wa")
        wav = wa.ap.rearrange("p (j k) -> p j k", j=2)
        load(wa, wav, conv_a[l])
        cur_ = Ring([A.bf16(2 * 514, "cu%d" % i) for i in range(2)])
        br = Ring([A.bf16(2 * 512, "bg%d" % i) for i in range(2)])
        yr = Ring([A.f32(512, "ya%d" % i) for i in range(2)])
        yor = Ring([A.bf16(512, "yo%d" % i) for i in range(2)])
        for si, (soff, slen) in enumerate(cfg.seqs):
            for gi in range(slen // 512):
                t0 = soff + gi * 512
                cu = cur_.next()
                cuv = cu.ap.rearrange("p (j n) -> p j n", j=2)
                bg = br.next()
                bgv = bg.ap.rearrange("p (j n) -> p j n", j=2)
                for j in range(2):
                    halo_load(cu, cuv[:, j, :], PT[2 + j], t0, soff, slen, "PT%d_" % (2 + j))
                    load(bg, bgv[:, j, :], PT[j, :, t0:t0 + 512], R=[DR("PT%d_" % j, t0 // 512)])
                for j in range(2):
                    y = yr.next()
                    ts("vector", y.ap, cuv[:, j, 0:512], wav[:, j, 0:1], None, ALU.mult, None,
                       [cu.res, wa.res], [y.res])
                    stt("vector", y.ap, cuv[:, j, 1:513], wav[:, j, 1:2], y.ap, ALU.mult, ALU.add,
                        [cu.res, wa.res, y.res], [y.res])
                    stt("vector", y.ap, cuv[:, j, 2:514], wav[:, j, 2:3], y.ap, ALU.mult, ALU.add,
                        [cu.res, wa.res, y.res], [y.res])
                    yo = yor.next()
                    tt("gpsimd", yo.ap, y.ap, bgv[:, j, :], ALU.mult, [y.res, bg.res], [yo.res])
                    store(yo, ycT[j, :, t0:t0 + 512], yo.ap, [DR("ycT%d_" % j, t0 // 512)])
        S.barrier()

    def phase_MB(l):
        A.reset()
        em = A.bf16(8 * 3 * 128, "emask")
        emv = em.ap.rearrange("p (h r q) -> p h r q", h=8, r=3)
        load(em, em.ap, c_emask)
        esk = A.f32(8, "esink")
        bload(esk, sink[l:l + 1, :])
        act(esk.ap, esk.ap, AF.Exp, [esk.res], [esk.res])
        qr = Ring([A.bf16(4 * 512, "qT%d" % i) for i in range(2)])
        kr = Ring([A.bf16(768, "kT%d" % i) for i in range(2)])
        vrr = Ring([A.bf16(6 * 132, "vg%d" % i) for i in range(2)])
        per = Ring([A.bf16(512, "pe%d" % i) for i in range(4)])
        pmr = Ring([A.bf16(512, "pm%d" % i) for i in range(5)])
        dnr = Ring([A.f32(16, "den%d" % i) for i in range(2)])
        ybr = Ring([A.bf16(512, "yb%d" % i) for i in range(2)])
        ytr = Ring([A.bf16(4 * 512, "yt%d" % i) for i in range(2)])
        sring = Ring(banks[0:5])
        oring = Ring([(banks[5], banks[6])])
        tbank = banks[7]
        for si, (soff, slen) in enumerate(cfg.seqs):
            nblk = slen // 128
            for gi in range(slen // 512):
                t0 = soff + gi * 512
                qT = qr.next()
                qv = qT.ap.rearrange("p (m n) -> p m n", m=4)
                load(qT, qv, PT[4:8, :, t0:t0 + 512].rearrange("m p t -> p m t"),
                     R=[DR("PT%d_" % s_, t0 // 512) for s_ in range(4, 8)])
                kT = kr.next()
                lo = max(t0 - 128, soff)
                hi = min(t0 + 640, soff + slen)
                load(kT, kT.ap[:, lo - (t0 - 128):hi - (t0 - 128)], PT[8, :, lo:hi],
                     R=[DR("PT8_", g) for g in range(lo // 512, (hi - 1) // 512 + 1)])
                vg = vrr.next()
                vgv = vg.ap.rearrange("p (j c) -> p j c", j=6)
                j0 = (lo - (t0 - 128)) // 128
                j1 = (hi - (t0 - 128)) // 128
                load(vg, vgv[:, j0:j1, :], Pv[lo:hi, :].rearrange("(j p) c -> p j c", p=128),
                     R=[DR("Pv", b_) for b_ in range(lo // 128, hi // 128)])
                yT = ytr.next()
                ytv = yT.ap.rearrange("p (m n) -> p m n", m=4)
                for b in range(4):
                    ib = gi * 4 + b
                    rels = [r for r in range(3) if 0 <= ib + r - 1 < nblk]
                    ob = oring.next()
                    items = [(g, r) for g in range(2) for r in rels]

                    def stage1(g, r):
                        ps_ = slice(g * 64, (g + 1) * 64)
                        jb = b + r
                        sb = sring.next()
                        mm(sb.ap.rearrange("p (m n) -> p m n", m=4), kT.ap[ps_, jb * 128:(jb + 1) * 128],
                           qv[ps_, :, b * 128:(b + 1) * 128], True, True, [kT.res, qT.res], [sb.res])
                        pe = per.next()
                        act(pe.ap, sb.ap, AF.Exp, [sb.res], [pe.res], scale=0.125)
                        pm = pmr.next()
                        tt("vector" if (g + r) % 2 == 0 else "gpsimd", pm.ap.rearrange("p (h q) -> p h q", h=4),
                           pe.ap.rearrange("p (h q) -> p h q", h=4), emv[:, g * 4:(g + 1) * 4, r, :],
                           ALU.mult, [pe.res, em.res], [pm.res])
                        return pm

                    def stage2(g, r, pm):
                        jb = b + r
                        obv = ob[g].ap[:, 0:264].rearrange("p (h c) -> p h c", h=4)
                        for hl in range(4):
                            mm(obv[:, hl, :], pm.ap[:, hl * 128:(hl + 1) * 128],
                               vgv[:, jb, g * 66:(g + 1) * 66], (r == rels[0] and hl == 0), r == rels[-1],
                               [pm.res, vg.res], [ob[g].res])

                    SK = 2
                    pms = [stage1(*items[i]) for i in range(min(SK, len(items)))]
                    for i, (g, r) in enumerate(items):
                        pm = pms.pop(0)
                        if i + SK < len(items):
                            pms.append(stage1(*items[i + SK]))
                        stage2(g, r, pm)
                    den = dnr.next()
                    yb = ybr.next()
                    for g in range(2):
                        obv = ob[g].ap[:, 0:264].rearrange("p (h c) -> p h c", h=4)
                        tt("vector", den.ap[:, g * 4:(g + 1) * 4], obv[:, :, 64], esk.ap[:, g * 4:(g + 1) * 4],
                           ALU.add, [ob[g].res, esk.res], [den.res])
                    S.op("vector", (lambda o, i: (lambda e: e.reciprocal(o, i)))(den.ap[:, 8:16], den.ap[:, 0:8]),
                         [den.res], [den.res])
                    for g in range(2):
                        obv = ob[g].ap[:, 0:264].rearrange("p (h c) -> p h c", h=4)
                        tt("vector", yb.ap[:, g * 256:(g + 1) * 256].rearrange("p (h c) -> p h c", h=4),
                           obv[:, :, 0:64], bc(den.ap[:, 8 + g * 4:8 + (g + 1) * 4], 64), ALU.mult,
                           [ob[g].res, den.res], [yb.res])
                    tv = bfview(tbank)
                    for c in range(4):
                        tr(tv[:, c * 128:(c + 1) * 128], yb.ap[:, c * 128:(c + 1) * 128], ident_b.ap,
                           [yb.res, ident_b.res], [tbank.res])
                    cp("scalar", ytv[:, :, b * 128:(b + 1) * 128],
                       tv[:, 0:512].rearrange("p (m n) -> p m n", m=4), [tbank.res], [yT.res])
                store(yT, ycT[2:6, :, t0:t0 + 512].rearrange("m p t -> p m t"), ytv,
                      [DR("ycT%d_" % s_, t0 // 512) for s_ in range(2, 6)])
        S.barrier()

    def phase_MC0(l):
        A.reset()
        wd = A.f32(18, "wd")
        wdv = wd.ap.rearrange("p (j k) -> p j k", j=6)
        load(wd, wdv, dn_conv[l])
        bon = A.f32(128, "bones")
        load(bon, bon.ap, c_bones)
        nega = A.f32(8, "nega")
        bload(nega, a_log[l:l + 1, :])
        act(nega.ap, nega.ap, AF.Exp, [nega.res], [nega.res])
        ts("vector", nega.ap, nega.ap, -1.0, None, ALU.mult, None, [nega.res], [nega.res])
        dtb = A.f32(8, "dtb")
        bload(dtb, dt_bias[l:l + 1, :])
        xr_ = Ring([A.bf16(6 * 514, "xin%d" % i) for i in range(2)])
        yr = Ring([A.f32(6 * 512, "cy%d" % i) for i in range(2)])
        sqr = Ring([A.f32(512, "sq%d" % i) for i in range(2)])
        rsr = Ring([A.f32(512, "rs%d" % i) for i in range(2)])
        qkr = Ring([A.bf16(6 * 512, "qk%d" % i) for i in range(2)])
        kvr = Ring([A.bf16(512, "kv%d" % i) for i in range(2)])
        gr = Ring([A.f32(4 * 16, "gt%d" % i) for i in range(2)])
        gor = Ring([A.f32(4 * 16, "go%d" % i) for i in range(2)])
        tbr = Ring(banks[6:8])
        nbr = Ring(banks[0:6])
        for si, (soff, slen) in enumerate(cfg.seqs):
            for gi in range(slen // 512):
                t0 = soff + gi * 512
                xin = xr_.next()
                xv = xin.ap.rearrange("p (j n) -> p j n", j=6)
                for j in range(6):
                    halo_load(xin, xv[:, j, :], PT[9 + j], t0, soff, slen, "PT%d_" % (9 + j))
                y = yr.next()
                yv = y.ap.rearrange("p (j n) -> p j n", j=6)
                for j in range(6):
                    eng = "vector"
                    ts(eng, yv[:, j, :], xv[:, j, 0:512], wdv[:, j, 0:1], None, ALU.mult, None,
                       [xin.res, wd.res], [y.res])
                    stt(eng, yv[:, j, :], xv[:, j, 1:513], wdv[:, j, 1:2], yv[:, j, :], ALU.mult, ALU.add,
                        [xin.res, wd.res, y.res], [y.res])
                    stt(eng, yv[:, j, :], xv[:, j, 2:514], wdv[:, j, 2:3], yv[:, j, :], ALU.mult, ALU.add,
                        [xin.res, wd.res, y.res], [y.res])
                act(y.ap, y.ap, AF.Silu, [y.res], [y.res])
                qk = qkr.next()
                qkv = qk.ap.rearrange("p (j n) -> p j n", j=6)
                for j in range(4):
                    sq = sqr.next()
                    tt("gpsimd", sq.ap, yv[:, j, :], yv[:, j, :], ALU.mult, [y.res], [sq.res])
                    bk = nbr.next()
                    mm(bk.ap, bon.ap, sq.ap, True, True, [bon.res, sq.res], [bk.res])
                    rs = rsr.next()
                    rsqrt_eps(rs.ap, bk.ap, RMS_EPS, [bk.res], [rs.res])
                    if j < 2:
                        stt("vector", qkv[:, j, :], yv[:, j, :], 0.125, rs.ap, ALU.mult, ALU.mult,
                            [y.res, rs.res], [qk.res])
                    else:
                        tt("vector", qkv[:, j, :], yv[:, j, :], rs.ap, ALU.mult, [y.res, rs.res], [qk.res])
                cp("gpsimd", qkv[:, 4:6, :], yv[:, 4:6, :], [y.res], [qk.res])
                store(qk, qnT[:, :, t0:t0 + 512].rearrange("m p t -> p m t"), qkv[:, 0:2, :],
                      [DR("qnT", t0 // 512)])
                store(qk, knT[:, :, t0:t0 + 512].rearrange("m p t -> p m t"), qkv[:, 2:4, :],
                      [DR("knT", t0 // 512)])
                gt = gr.next()
                gtv = gt.ap.rearrange("p (b c) -> p b c", b=4)
                load(gt, gtv, Pzg[t0:t0 + 512, 256:272].rearrange("(b p) c -> p b c", p=128),
                     R=[DR("Pzg", t0 // 128 + b_) for b_ in range(4)])
                go = gor.next()
                gov = go.ap.rearrange("p (b c) -> p b c", b=4)
                tt("vector", gov[:, :, 0:8], gtv[:, :, 0:8], dtb.ap.unsqueeze(1).broadcast_to([128, 4, 8]),
                   ALU.add, [gt.res, dtb.res], [go.res])
                act(gov[:, :, 0:8], gov[:, :, 0:8], AF.Exp, [go.res], [go.res])
                act(gov[:, :, 0:8], gov[:, :, 0:8], AF.Ln, [go.res], [go.res], bias=1.0)
                tt("vector", gov[:, :, 0:8], gov[:, :, 0:8], nega.ap.unsqueeze(1).broadcast_to([128, 4, 8]),
                   ALU.mult, [go.res, nega.res], [go.res])
                act(gov[:, :, 8:16], gtv[:, :, 8:16], AF.Sigmoid, [gt.res], [go.res])
                store(go, gbd[t0:t0 + 512, :].rearrange("(b p) c -> p b c", p=128), gov,
                      [DR("gbd", t0 // 128 + b_) for b_ in range(4)])
                for b in range(4):
                    tb = t0 + b * 128
                    tbk = tbr.next()
                    tv = bfview(tbk)
                    for j in range(4):
                        tr(tv[:, j * 128:(j + 1) * 128], qkv[:, 2 + j, b * 128:(b + 1) * 128], ident_b.ap,
                           [qk.res, ident_b.res], [tbk.res])
                    kv = kvr.next()
                    cp("scalar", kv.ap, tv[:, 0:512], [tbk.res], [kv.res])
                    store(kv, kvtm[tb:tb + 128, :], kv.ap, [DR("kvtm", tb // 128)])
        S.barrier()

    def mc_gen(l, d, si_sel, mybanks, sh):
        U, PM, NM, Ub = sh["U%d" % d], sh["PM%d" % d], sh["NM%d" % d], sh["Ub%d" % d]
        B16, MK1, MK2, MK3 = sh["B16"], sh["MK1"], sh["MK2"], sh["MK3"]
        onesb, nonesb, bon = sh["onesb"], sh["nonesb"], sh["bon"]
        Mr = Ring([A.bf16(512, "Mr%d" % i) for i in range(14)])
        dgr = Ring([A.bf16(1024, "Dg%d" % i) for i in range(1)])
        ghr = Ring([A.bf16(16, "gh%d" % i) for i in range(1)])
        Sf = A.f32(256, "Sf")
        Sb = A.bf16(256, "Sb")
        Sfv = Sf.ap.rearrange("p (h n) -> p h n", h=4)
        Sbv = Sb.ap.rearrange("p (h n) -> p h n", h=4)
        knr = Ring([A.bf16(256, "kn%d" % i) for i in range(1)])
        qnr = Ring([A.bf16(256, "qn%d" % i) for i in range(1)])
        knzr = Ring([A.bf16(512, "knz%d" % i) for i in range(1)])
        kvr = Ring([A.bf16(512, "kvl%d" % i) for i in range(1)])
        gbr = Ring([A.f32(16, "gb%d" % i) for i in range(1)])
        g8r = Ring([A.f32(24, "g8%d" % i) for i in range(1)])
        x1r = Ring([A.f32(512, "x1%d" % i) for i in range(2)])
        dsr = Ring([A.bf16(512, "dst%d" % i) for i in range(1)])
        dtr = Ring([A.bf16(512, "dti%d" % i) for i in range(1)])
        tmr = Ring([A.bf16(512, "tmpf%d" % i) for i in range(1)])
        Lr = Ring([A.bf16(512, "L%d" % i) for i in range(1)])
        LTr = Ring([A.bf16(512, "LT%d" % i) for i in range(1)])
        ATr = Ring([A.bf16(512, "AT%d" % i) for i in range(1)])
        Xr = Ring([A.bf16(512, "X%d" % i) for i in range(2)])
        wTr = Ring([A.bf16(512, "wT%d" % i) for i in range(1)])
        kdr = Ring([A.bf16(256, "kd%d" % i) for i in range(1)])
        vnr = Ring([A.bf16(256, "vn%d" % i) for i in range(1)])
        o1r = Ring([A.f32(256, "o1%d" % i) for i in range(1)])
        oor = Ring([A.f32(256, "oo%d" % i) for i in range(2)])
        bring = Ring(mybanks)
        for si, (soff, slen) in [(si_sel, cfg.seqs[si_sel])]:
            nblk = slen // 128
            memset("vector", Sf.ap, 0.0, [Sf.res])
            memset("vector", Sb.ap, 0.0, [Sb.res])
            order = range(nblk) if d == 0 else range(nblk - 1, -1, -1)
            for ib in order:
                tb = soff + ib * 128
                kn = knr.next()
                knv = kn.ap.rearrange("p (c n) -> p c n", c=2)
                load(kn, knv, knT[:, :, tb:tb + 128].rearrange("m p t -> p m t"), R=[DR("knT", tb // 512)])
                qn = qnr.next()
                qnv = qn.ap.rearrange("p (c n) -> p c n", c=2)
                load(qn, qnv, qnT[:, :, tb:tb + 128].rearrange("m p t -> p m t"), R=[DR("qnT", tb // 512)])
                kv = kvr.next()
                load(kv, kv.ap, kvtm[tb:tb + 128, :], R=[DR("kvtm", tb // 128)])
                k4 = kv.ap[:, 0:256].rearrange("p (h c) -> p h c", h=4)
                v4 = kv.ap[:, 256:512].rearrange("p (h c) -> p h c", h=4)
                gb = gbr.next()
                load(gb, gb.ap, gbd[tb:tb + 128, :], R=[DR("gbd", tb // 128)])
                g_d = gb.ap[:, d * 4:(d + 1) * 4]
                beta = gb.ap[:, 8 + d * 4:8 + (d + 1) * 4]
                mcstop = int(_os.environ.get('MCSTOP', '99'))
                if mcstop == -1:
                    continue
                mcstop = int(_os.environ.get('MCSTOP', '99'))
                yield
                gh = ghr.next()
                cp("vector", gh.ap[:, 0:4], g_d, [gb.res], [gh.res])
                tt("vector", gh.ap[:, 4:8], g_d, gh.ap[:, 0:4], ALU.subtract, [gb.res, gh.res], [gh.res])
                bA = bring.next()
                mm(bA.ap[:, 0:4], Ub.ap, gh.ap[:, 0:4], True, False, [Ub.res, gh.res], [bA.res])
                mm(bA.ap[:, 0:4], Ub.ap, gh.ap[:, 4:8], False, True, [Ub.res, gh.res], [bA.res])
                mm(bA.ap[:, 4:8], onesb.ap, gh.ap[:, 0:4], True, False, [onesb.res, gh.res], [bA.res])
                mm(bA.ap[:, 4:8], onesb.ap, gh.ap[:, 4:8], False, True, [onesb.res, gh.res], [bA.res])
                g8 = g8r.next()
                cp("vector", g8.ap[:, 0:8], bA.ap[:, 0:8], [bA.res], [g8.res])
                yield
                if mcstop == 0:
                    continue
                yield
                cp("vector", gh.ap[:, 8:12], g8.ap[:, 0:4], [g8.res], [gh.res])
                tt("vector", gh.ap[:, 12:16], g8.ap[:, 0:4], gh.ap[:, 8:12], ALU.subtract, [g8.res, gh.res], [gh.res])
                Dg = dgr.next()
                Dgv = Dg.ap.rearrange("p (a h n) -> p a h n", a=2, h=4)
                for a_ in range(2):
                    tt("vector", Dgv[:, a_, :, :], ident_b.ap.unsqueeze(1).broadcast_to([128, 4, 128]),
                       bc(gh.ap[:, 8 + 4 * a_:12 + 4 * a_], 128), ALU.mult, [ident_b.res, gh.res], [Dg.res])
                bE = bring.next()
                for h in range(4):
                    hs = slice(h * 128, (h + 1) * 128)
                    mm(bE.ap[:, hs], onesb.ap, Dgv[:, 0, h, :], True, False, [onesb.res, Dg.res], [bE.res])
                    mm(bE.ap[:, hs], onesb.ap, Dgv[:, 1, h, :], False, False, [onesb.res, Dg.res], [bE.res])
                    mm(bE.ap[:, hs], Dgv[:, 0, h, :], nonesb.ap, False, False, [nonesb.res, Dg.res], [bE.res])
                    mm(bE.ap[:, hs], Dgv[:, 1, h, :], nonesb.ap, False, True, [nonesb.res, Dg.res], [bE.res])
                bEv = bE.ap.rearrange("p (h n) -> p h n", h=4)
                x1 = x1r.next()
                tt("vector", x1.ap.rearrange("p (h n) -> p h n", h=4), bEv,
                   PM.ap.unsqueeze(1).broadcast_to([128, 4, 128]), ALU.max, [bE.res, PM.res], [x1.res])
                dst = dsr.next()
                act(dst.ap, x1.ap, AF.Exp, [x1.res], [dst.res], scale=-1.0)
                yield
                x2 = x1r.next()
                tt("vector", x2.ap.rearrange("p (h n) -> p h n", h=4), bEv,
                   NM.ap.unsqueeze(1).broadcast_to([128, 4, 128]), ALU.min, [bE.res, NM.res], [x2.res])
                dti = dtr.next()
                act(dti.ap, x2.ap, AF.Exp, [x2.res], [dti.res])
                if mcstop == 1:
                    continue
                yield
                knz = knzr.next()
                knzv = knz.ap.rearrange("p (r c n) -> p r c n", r=2, c=2)
                for r in range(2):
                    ts("gpsimd", knz.ap[:, r * 256:(r + 1) * 256], kn.ap, bon.ap[:, r * 64:r * 64 + 1], None,
                       ALU.mult, None, [kn.res, bon.res], [knz.res])
                bK = bring.next()
                bQ = bring.next()
                for h in range(4):
                    c = h // 2
                    r = h % 2
                    mm(bK.ap[:, h * 128:(h + 1) * 128], knzv[:, r, c, :], knv[:, c, :], True, True,
                       [knz.res, kn.res], [bK.res])
                    mm(bQ.ap[:, h * 128:(h + 1) * 128], knzv[:, r, c, :], qnv[:, c, :], True, True,
                       [knz.res, qn.res], [bQ.res])
                if mcstop == 20:
                    continue
                tmpf = tmr.next()
                tt("vector", tmpf.ap, bK.ap, dst.ap, ALU.mult, [bK.res, dst.res], [tmpf.res])
                yield
                if mcstop == 21:
                    continue
                Lm = Lr.next()
                tt("gpsimd", Lm.ap.rearrange("p (h n) -> p h n", h=4), tmpf.ap.rearrange("p (h n) -> p h n", h=4),
                   bc(beta, 128), ALU.mult, [tmpf.res, gb.res], [Lm.res])
                if mcstop == 22:
                    continue
                AT = ATr.next()
                tt("vector", AT.ap, bQ.ap, dti.ap, ALU.mult, [bQ.res, dti.res], [AT.res])
                if mcstop == 2:
                    continue
                yield
                tbank = bring.next()
                tv = bfview(tbank)
                for h in range(4):
                    tr(tv[:, h * 128:(h + 1) * 128], Lm.ap[:, h * 128:(h + 1) * 128], ident_b.ap,
                       [Lm.res, ident_b.res], [tbank.res])
                LT = LTr.next()
                cp("scalar", LT.ap, tv[:, 0:512], [tbank.res], [LT.res])
                if mcstop == 3:
                    continue
                yield
                act(g8.ap[:, 8:12], g8.ap[:, 0:4], AF.Exp, [g8.res], [g8.res])
                tt("vector", g8.ap[:, 12:16], g8.ap[:, 8:12], beta, ALU.mult, [g8.res, gb.res], [g8.res])
                tt("vector", g8.ap[:, 16:20], g8.ap[:, 4:8], g8.ap[:, 0:4], ALU.subtract, [g8.res], [g8.res])
                act(g8.ap[:, 16:20], g8.ap[:, 16:20], AF.Exp, [g8.res], [g8.res])
                gt2 = g8.ap[:, 4:8].rearrange("p (c r) -> p c r", r=2)
                for r in range(2):
                    rs_ = slice(r * 64, (r + 1) * 64)
                    act(g8.ap[rs_, 20:22], gt2[rs_, :, r], AF.Exp, [g8.res], [g8.res])
                if mcstop == 4:
                    continue
                yield
                X = Xr.next()
                k5 = kv.ap[:, 0:256].rearrange("p (c r n) -> p c r n", c=2, r=2)
                v5 = kv.ap[:, 256:512].rearrange("p (c r n) -> p c r n", c=2, r=2)
                beta5 = beta.rearrange("p (c r) -> p c r", r=2)
                bg5 = g8.ap[:, 12:16].rearrange("p (c r) -> p c r", r=2)

                def x5(Xt):
                    return Xt.ap.rearrange("p (c r n) -> p c r n", c=2, r=2)

                for r in range(2):
                    uo = 64 if r == 0 else 0
                    wo = 0 if r == 0 else 64
                    tt("gpsimd", x5(X)[:, :, r, uo:uo + 64], v5[:, :, r, :], bc(beta5[:, :, r], 64), ALU.mult,
                       [kv.res, gb.res], [X.res])
                    tt("gpsimd", x5(X)[:, :, r, wo:wo + 64], k5[:, :, r, :], bc(bg5[:, :, r], 64), ALU.mult,
                       [kv.res, g8.res], [X.res])
                if mcstop == 5:
                    continue
                yield
                def h4(t):
                    return t.ap.rearrange("p (h n) -> p h n", h=4)

                def mbc(m):
                    return m.ap.unsqueeze(1).broadcast_to([128, 4, 128])

                def hmm(lhsT_t, rhs_t):
                    bkx = bring.next()
                    for h in range(4):
                        hs = slice(h * 128, (h + 1) * 128)
                        mm(bkx.ap[:, hs], lhsT_t.ap[:, hs], rhs_t.ap[:, hs], True, True,
                           [lhsT_t.res, rhs_t.res], [bkx.res])
                    return bkx

                L0 = Mr.next()
                tt("gpsimd", h4(L0), h4(Lm), mbc(B16), ALU.mult, [Lm.res, B16.res], [L0.res])
                L0T = Mr.next()
                tt("gpsimd", h4(L0T), h4(LT), mbc(B16), ALU.mult, [LT.res, B16.res], [L0T.res])
                Pk, PkT = L0, L0T
                Ps = []
                for j in range(3):
                    bM = hmm(PkT, Pk)
                    bMT = hmm(Pk, PkT)
                    Pn = Mr.next()
                    cp("scalar", Pn.ap, bM.ap, [bM.res], [Pn.res])
                    PnT = Mr.next()
                    cp("scalar", PnT.ap, bMT.ap, [bMT.res], [PnT.res])
                    Ps.append((Pn, PnT))
                    Pk, PkT = Pn, PnT
                    yield
                G = Mr.next()
                tt("gpsimd", h4(G), mbc(ident_b), h4(L0T), ALU.subtract, [ident_b.res, L0T.res], [G.res])
                H = Mr.next()
                tt("gpsimd", h4(H), mbc(ident_b), h4(L0), ALU.subtract, [ident_b.res, L0.res], [H.res])
                for (Pn, PnT) in Ps:
                    bG = hmm(Pn, G)
                    G2 = Mr.next()
                    tt("vector", G2.ap, G.ap, bG.ap, ALU.add, [G.res, bG.res], [G2.res])
                    G = G2
                    yield
                    bH = hmm(PnT, H)
                    H2 = Mr.next()
                    tt("vector", H2.ap, H.ap, bH.ap, ALU.add, [H.res, bH.res], [H2.res])
                    H = H2
                    yield
                Dm, DT = H, G
                for lev, Mk in enumerate((MK1, MK2, MK3)):
                    lastl = lev == 2
                    Ck = Mr.next()
                    tt("gpsimd", h4(Ck), h4(Lm), mbc(Mk), ALU.mult, [Lm.res, Mk.res], [Ck.res])
                    if not lastl:
                        CkT = Mr.next()
                        tt("vector", h4(CkT), h4(LT), mbc(Mk), ALU.mult, [LT.res, Mk.res], [CkT.res])
                    yield
                    bY = hmm(Ck, DT)
                    if not lastl:
                        bY2 = hmm(CkT, Dm)
                    YT = Mr.next()
                    cp("scalar", YT.ap, bY.ap, [bY.res], [YT.res])
                    if not lastl:
                        Y = Mr.next()
                        cp("scalar", Y.ap, bY2.ap, [bY2.res], [Y.res])
                    yield
                    bZ = hmm(Dm, YT)
                    if not lastl:
                        bZ2 = hmm(DT, Y)
                    DT2 = Mr.next()
                    tt("vector", DT2.ap, DT.ap, bZ.ap, ALU.subtract, [DT.res, bZ.res], [DT2.res])
                    if not lastl:
                        D2 = Mr.next()
                        tt("vector", D2.ap, Dm.ap, bZ2.ap, ALU.subtract, [Dm.res, bZ2.res], [D2.res])
                        Dm = D2
                    DT = DT2
                    yield
                bX = hmm(DT, X)
                X2 = Xr.next()
                cp("scalar", X2.ap, bX.ap, [bX.res], [X2.res])
                X = X2
                if mcstop == 6:
                    continue
                yield
                tbank = bring.next()
                tv = bfview(tbank)
                for h in range(4):
                    tr(tv[:, h * 128:(h + 1) * 128], X.ap[:, h * 128:(h + 1) * 128], ident_b.ap,
                       [X.res, ident_b.res], [tbank.res])
                wT = wTr.next()
                wTv = wT.ap.rearrange("p (h n) -> p h n", h=4)
                cp("scalar", wT.ap, tv[:, 0:512], [tbank.res], [wT.res])
                yield
                kd = kdr.next()
                kdv = kd.ap.rearrange("p (h c) -> p h c", h=4)
                tt("gpsimd", kdv, k4, bc(g8.ap[:, 16:20], 64), ALU.mult, [kv.res, g8.res], [kd.res])
                if mcstop == 7:
                    continue
                yield
                bW = bring.next()
                for h in range(4):
                    hp = slice((h % 2) * 64, (h % 2) * 64 + 64)
                    c = h // 2
                    mm(bW.ap[:, h * 64:(h + 1) * 64], wTv[:, h, :], Sbv[:, h, :], True, True,
                       [wT.res, Sb.res], [bW.res])
                    mm(bW.ap[:, 256 + h * 64:256 + (h + 1) * 64], qnv[:, c, :], Sbv[:, h, :], True, True,
                       [qn.res, Sb.res], [bW.res])
                vn = vnr.next()
                vnv = vn.ap.rearrange("p (h c) -> p h c", h=4)
                vn5 = vn.ap.rearrange("p (c r n) -> p c r n", c=2, r=2)
                bW5 = bW.ap[:, 0:256].rearrange("p (c r n) -> p c r n", c=2, r=2)
                for r in range(2):
                    uo = 64 if r == 0 else 0
                    tt("vector", vn5[:, :, r, :], x5(X)[:, :, r, uo:uo + 64], bW5[:, :, r, :],
                       ALU.subtract, [X.res, bW.res], [vn.res])
                o1 = o1r.next()
                tt("vector", o1.ap.rearrange("p (h c) -> p h c", h=4),
                   bW.ap[:, 256:512].rearrange("p (h c) -> p h c", h=4), bc(g8.ap[:, 8:12], 64), ALU.mult,
                   [bW.res, g8.res], [o1.res])
                yield
                bO = bring.next()
                for h in range(4):
                    mm(bO.ap[:, h * 64:(h + 1) * 64], AT.ap[:, h * 128:(h + 1) * 128], vnv[:, h, :], True, True,
                       [AT.res, vn.res], [bO.res])
                for c in range(2):
                    mm(bO.ap[:, 256 + c * 128:256 + (c + 1) * 128], kd.ap[:, c * 128:(c + 1) * 128],
                       vn.ap[:, c * 128:(c + 1) * 128], True, True, [kd.res, vn.res], [bO.res])
                oo = oor.next()
                tt("vector", oo.ap, o1.ap, bO.ap[:, 0:256], ALU.add, [o1.res, bO.res], [oo.res])
                yield
                bOs = bO.ap[:, 256:512].rearrange("p (c n) -> p c n", c=2)
                for r in range(2):
                    rs_ = slice(r * 64, (r + 1) * 64)
                    for c in range(2):
                        stt("vector", Sfv[rs_, 2 * c + r, :], Sfv[rs_, 2 * c + r, :], g8.ap[rs_, 20 + c:21 + c],
                            bOs[rs_, c, r * 64:(r + 1) * 64], ALU.mult, ALU.add,
                            [Sf.res, g8.res, bO.res], [Sf.res])
                cp("scalar", Sb.ap, Sf.ap, [Sf.res], [Sb.res])
                if d == 0:
                    store(oo, ofd[tb:tb + 128, :], oo.ap, [DR("ofd", tb // 128)])
                else:
                    store(oo, obd[tb:tb + 128, :], oo.ap, [DR("obd", tb // 128)])
                yield


    def phase_MCx(l):
        A.reset()
        sh = {}
        mk_f = A.f32(512, "mk_f")
        load(mk_f, mk_f.ap, c_blk)
        for i_, nm_ in enumerate(("B16", "MK1", "MK2", "MK3")):
            t_ = A.bf16(128, nm_)
            cp("vector", t_.ap, mk_f.ap[:, i_ * 128:(i_ + 1) * 128], [mk_f.res], [t_.res])
            sh[nm_] = t_
        sh["onesb"] = A.bf16(128, "onesb")
        memset("vector", sh["onesb"].ap, 1.0, [sh["onesb"].res])
        sh["nonesb"] = A.bf16(128, "nonesb")
        memset("vector", sh["nonesb"].ap, -1.0, [sh["nonesb"].res])
        sh["bon"] = A.f32(128, "bonesMC")
        load(sh["bon"], sh["bon"].ap, c_bones)
        for d in range(2):
            for nm_, src in (("U", c_U), ("PM", c_PM), ("NM", c_NM)):
                t_ = A.f32(128, nm_ + str(d))
                load(t_, t_.ap, src[d])
                sh[nm_ + str(d)] = t_
            ub = A.bf16(128, "Ub%d" % d)
            cp("vector", ub.ap, sh["U%d" % d].ap, [sh["U%d" % d].res], [ub.res])
            sh["Ub%d" % d] = ub
        gens = [mc_gen(l, 0, 0, banks[0:2], sh), mc_gen(l, 1, 0, banks[2:4], sh),
                mc_gen(l, 0, 1, banks[4:6], sh), mc_gen(l, 1, 1, banks[6:8], sh)]
        while gens:
            for g_ in list(gens):
                try:
                    next(g_)
                except StopIteration:
                    gens.remove(g_)
        S.barrier()

    def phase_MCz(l):
        A.reset()
        ng8 = A.f32(256, "ng8")
        bload(ng8, norm_g[l:l + 1, :])
        ts("vector", ng8.ap, ng8.ap, 8.0, None, ALU.mult, None, [ng8.res], [ng8.res])
        ofr = Ring([A.f32(256, "ofl%d" % i) for i in range(3)])
        obr = Ring([A.f32(256, "obl%d" % i) for i in range(3)])
        zr = Ring([A.f32(256, "zl%d" % i) for i in range(3)])
        sqr = Ring([A.f32(256, "osq%d" % i) for i in range(2)])
        smr = Ring([A.f32(24, "smz%d" % i) for i in range(3)])
        ycr = Ring([A.bf16(256, "yc%d" % i) for i in range(2)])
        yctr = Ring([A.bf16(256, "yct%d" % i) for i in range(2)])
        tbr = Ring(banks[0:4])
        for si, (soff, slen) in enumerate(cfg.seqs):
            for ib in range(slen // 128):
                tb = soff + ib * 128
                oo = obr.next()
                load(oo, oo.ap, obd[tb:tb + 128, :], R=[DR("obd", tb // 128)])
                ofl = ofr.next()
                load(ofl, ofl.ap, ofd[tb:tb + 128, :], R=[DR("ofd", tb // 128)])
                zl = zr.next()
                load(zl, zl.ap, Pzg[tb:tb + 128, 0:256], R=[DR("Pzg", tb // 128)])
                tt("vector", oo.ap, oo.ap, ofl.ap, ALU.add, [oo.res, ofl.res], [oo.res])
                osq = sqr.next()
                tt("vector", osq.ap, oo.ap, oo.ap, ALU.mult, [oo.res], [osq.res])
                sm = smr.next()
                S.op("vector", (lambda o, i: (lambda e: e.tensor_reduce(o, i, AX.X, ALU.add)))(
                    sm.ap[:, 0:4], osq.ap.rearrange("p (h c) -> p h c", h=4)), [osq.res], [sm.res])
                rsqrt_eps(sm.ap[:, 4:8], sm.ap[:, 0:4], 64.0 * RMS_EPS, [sm.res], [sm.res])
                tt("vector", oo.ap.rearrange("p (h c) -> p h c", h=4),
                   oo.ap.rearrange("p (h c) -> p h c", h=4), bc(sm.ap[:, 4:8], 64), ALU.mult,
                   [oo.res, sm.res], [oo.res])
                tt("gpsimd", oo.ap, oo.ap, ng8.ap, ALU.mult, [oo.res, ng8.res], [oo.res])
                act(zl.ap, zl.ap, AF.Silu, [zl.res], [zl.res])
                yc = ycr.next()
                tt("gpsimd", yc.ap, oo.ap, zl.ap, ALU.mult, [oo.res, zl.res], [yc.res])
                tbank = tbr.next()
                tv = bfview(tbank)
                for c in range(2):
                    tr(tv[:, c * 128:(c + 1) * 128], yc.ap[:, c * 128:(c + 1) * 128], ident_b.ap,
                       [yc.res, ident_b.res], [tbank.res])
                yct = yctr.next()
                cp("scalar", yct.ap, tv[:, 0:256], [tbank.res], [yct.res])
                store(yct, ycT[6:8, :, tb:tb + 128].rearrange("m p t -> p m t"),
                      yct.ap.rearrange("p (m n) -> p m n", m=2), [DR("ycT67_", tb // 128)])
        S.barrier()

    def phase_O1(l):
        A.reset()
        cur = xbuf[l % 2]
        Wo = A.bf16(8 * D, "Wo")
        Wov = Wo.ap.rearrange("p (k n) -> p k n", k=8)
        for kc in range(8):
            load(Wo, Wov[:, kc, :], w_out[l, kc * 128:(kc + 1) * 128, :], eng="gpsimd")
        G1 = A.f32(D, "G1")
        g1 = A.f32(D, "ln1g")
        b1_ = A.f32(D, "ln1b")
        bload(g1, ln1_g[l:l + 1, :])
        bload(b1_, ln1_b[l:l + 1, :])
        ycr = Ring([A.bf16(8 * 512, "ycl%d" % i) for i in range(2)])
        xr = Ring([A.f32(D, "xo%d" % i) for i in range(3)])
        tr_ = Ring([A.f32(D, "to%d" % i) for i in range(2)])
        rr = Ring([A.f32(D, "ro%d" % i) for i in range(2)])
        sr = Ring([A.f32(D, "so%d" % i) for i in range(2)])
        outr = Ring([A.f32(D, "oo%d" % i) for i in range(2)])
        smalls = Ring([A.f32(24, "small%d" % i) for i in range(2)])
        bring = Ring(banks)
        for si, (soff, slen) in enumerate(cfg.seqs):
            load(G1, G1.ap, modrow[l, si:si + 1, 2 * D:3 * D].broadcast_to([128, D]), R=[DR("modrow", l)])
            ts("vector", G1.ap, G1.ap, 1.0, None, ALU.add, None, [G1.res], [G1.res])
            for gi in range(slen // 512):
                t0 = soff + gi * 512
                yc = ycr.next()
                ycv = yc.ap.rearrange("p (m n) -> p m n", m=8)
                load(yc, ycv, ycT[:, :, t0:t0 + 512].rearrange("m p t -> p m t"),
                     R=[DR("ycT%d_" % s_, t0 // 512) for s_ in range(6)] +
                       [DR("ycT67_", t0 // 128 + b_) for b_ in range(4)])
                for b in range(4):
                    tb = t0 + b * 128
                    x = xr.next()
                    load(x, x.ap, cur[tb:tb + 128, :], R=[DR("x%d" % (l % 2), tb // 128)])
                    tmp = tr_.next()
                    for nh in range(2):
                        bk = bring.next()
                        for kc in range(8):
                            mm(bk.ap, ycv[:, kc, b * 128:(b + 1) * 128], Wov[:, kc, nh * 512:(nh + 1) * 512],
                               kc == 0, kc == 7, [yc.res, Wo.res], [bk.res])
                        tt("vector", tmp.ap[:, nh * 512:(nh + 1) * 512], bk.ap, G1.ap[:, nh * 512:(nh + 1) * 512],
                           ALU.mult, [bk.res, G1.res], [tmp.res])
                    r = rr.next()
                    stt("vector", r.ap, x.ap, ALPHA, tmp.ap, ALU.mult, ALU.add, [x.res, tmp.res], [r.res])
                    o = outr.next()
                    layer_norm(r, o, g1, b1_, smalls.next(), sr.next())
                    store(o, x1buf[tb:tb + 128, :], o.ap, [DR("x1", tb // 128)])
        S.barrier()

    def phase_O2(l):
        A.reset()
        last = l == nl - 1
        nxt = xbuf[(l + 1) % 2]
        W1 = A.bf16(8 * DFF, "W1")
        W1v = W1.ap.rearrange("p (k n) -> p k n", k=8)
        for kc in range(8):
            load(W1, W1v[:, kc, :], w1[l, kc * 128:(kc + 1) * 128, :], eng="gpsimd")
        W2 = A.bf16(32 * D, "W2")
        W2v = W2.ap.rearrange("p (k n) -> p k n", k=32)
        for kc in range(32):
            load(W2, W2v[:, kc, :], w2[l, kc * 128:(kc + 1) * 128, :], eng="gpsimd")
        b1t = A.f32(32, "b1c")
        load(b1t, b1t.ap, b1c[l])
        G2 = A.f32(D, "G2")
        g2 = A.f32(D, "ln2g")
        b2_ = A.f32(D, "ln2b")
        bb2 = A.f32(D, "b2")
        bload(g2, ln2_g[l:l + 1, :])
        bload(b2_, ln2_b[l:l + 1, :])
        bload(bb2, b2[l:l + 1, :])
        xr = Ring([A.f32(D, "x1_%d" % i) for i in range(4)])
        hr = Ring([A.bf16(8 * 256, "h2T%d" % i) for i in range(2)])
        ar = Ring([A.f32(256, "aa%d" % i) for i in range(4)])
        ur = Ring([A.bf16(256, "uu%d" % i) for i in range(5)])
        tr_ = Ring([A.f32(D, "t2%d" % i) for i in range(2)])
        sr = Ring([A.f32(D, "s2%d" % i) for i in range(1)])
        smalls = Ring([A.f32(24, "small%d" % i) for i in range(2)])
        ring1 = Ring(banks[4:8])
        for si, (soff, slen) in enumerate(cfg.seqs):
            load(G2, G2.ap, modrow[l, si:si + 1, 5 * D:6 * D].broadcast_to([128, D]), R=[DR("modrow", l)])
            ts("vector", G2.ap, G2.ap, 1.0, None, ALU.add, None, [G2.res], [G2.res])
            dst = (ys, yp)[si]
            for gi in range(slen // 256):
                t0 = soff + gi * 256
                xg = []
                for b in range(2):
                    x = xr.next()
                    tb = t0 + b * 128
                    load(x, x.ap, x1buf[tb:tb + 128, :], R=[DR("x1", tb // 128)])
                    xg.append(x)
                hT = hr.next()
                hv = hT.ap.rearrange("p (k n) -> p k n", k=8)
                for kc in range(8):
                    bk = ring1.next()
                    for b in range(2):
                        tr(bk.ap[:, b * 128:(b + 1) * 128], xg[b].ap[:, kc * 128:(kc + 1) * 128], ident_f.ap,
                           [xg[b].res, ident_f.res], [bk.res])
                    act(hv[:, kc, :], bk.ap[:, 0:256], AF.Identity, [bk.res, modcol.res, modcp1.res], [hT.res],
                        bias=mcv[:, l, 3 * 8 + kc, si:si + 1], scale=mc1v[:, l, 4 * 8 + kc, si:si + 1])

                def ffn1(m):
                    bk = ring1.next()
                    for kc in range(8):
                        mm(bk.ap[:, 0:256], W1v[:, kc, m * 128:(m + 1) * 128], hv[:, kc, :], kc == 0, kc == 7,
                           [W1.res, hT.res], [bk.res])
                    a = ar.next()
                    act(a.ap, bk.ap[:, 0:256], AF.Relu, [bk.res, b1t.res], [a.res], bias=b1t.ap[:, m:m + 1])
                    u = ur.next()
                    tt("gpsimd", u.ap, a.ap, a.ap, ALU.mult, [a.res], [u.res])
                    return u

                us = [ffn1(0), ffn1(1)]
                for m in range(32):
                    u = us.pop(0)
                    if m + 2 < 32:
                        us.append(ffn1(m + 2))
                    for b in range(2):
                        for nh in range(2):
                            bk = banks[b * 2 + nh]
                            mm(bk.ap, u.ap[:, b * 128:(b + 1) * 128], W2v[:, m, nh * 512:(nh + 1) * 512],
                               m == 0, m == 31, [u.res, W2.res], [bk.res])
                for b in range(2):
                    tb = t0 + b * 128
                    tmp = tr_.next()
                    for nh in range(2):
                        bk = banks[b * 2 + nh]
                        hs = slice(nh * 512, (nh + 1) * 512)
                        tt("vector", tmp.ap[:, hs], bk.ap, bb2.ap[:, hs], ALU.add, [bk.res, bb2.res], [tmp.res])
                    tt("gpsimd", tmp.ap, tmp.ap, G2.ap, ALU.mult, [tmp.res, G2.res], [tmp.res])
                    stt("vector", tmp.ap, xg[b].ap, ALPHA, tmp.ap, ALU.mult, ALU.add, [xg[b].res, tmp.res], [tmp.res])
                    layer_norm(tmp, xg[b], g2, b2_, smalls.next(), sr.next())
                    if last:
                        store(xg[b], dst[tb - soff:tb - soff + 128, :], xg[b].ap, [DR("yout", tb // 128)])
                    else:
                        store(xg[b], nxt[tb:tb + 128, :], xg[b].ap, [DR("x%d" % ((l + 1) % 2), tb // 128)])
        S.barrier()

    def want(p):
        return cfg.phases is None or p in cfg.phases

    if want("mod"):
        phase_mod()
    for l in range(nl):
        if want("P"):
            phase_P(l)
        if want("MA"):
            phase_MA(l)
        if want("MB"):
            phase_MB(l)
        if want("MC0"):
            phase_MC0(l)
        if want("MCx"):
            phase_MCx(l)
        if want("MCz"):
            phase_MCz(l)
        if want("O1"):
            phase_O1(l)
        if want("O2"):
            phase_O2(l)
    S.emit(nc, es)
    es.close()
    return nc


def _consts():
    import ml_dtypes
    c = {}
    c["c_ident"] = np.eye(128, dtype=np.float32)
    p = np.arange(128)[:, None]
    q = np.arange(128)[None, :]
    slopes = np.exp2(-np.arange(1, 9, dtype=np.float64))
    em = np.zeros((128, 8, 3, 128), np.float64)
    for r in range(3):
        dist = np.abs((r - 1) * 128 + p - q)
        valid = dist <= 128
        for h in range(8):
            em[:, h, r, :] = np.where(valid, np.exp(-slopes[h] * dist), 0.0)
    c["c_emask"] = em.reshape(128, -1).astype(ml_dtypes.bfloat16)
    bo = np.zeros((128, 128), np.float32)
    bo[:64, :64] = 1.0
    bo[64:, 64:] = 1.0
    c["c_bones"] = bo
    c["c_ones"] = np.ones((128, 128), np.float32)
    s = np.arange(128)[:, None]
    cc = np.arange(128)[None, :]
    U = np.stack([(s <= cc), (s >= cc)]).astype(np.float32)
    c["c_U"] = U
    PM = np.stack([np.where(cc < s, 0.0, BIG), np.where(cc > s, 0.0, BIG)]).astype(np.float32)
    NM = np.stack([np.where(cc >= s, 0.0, -BIG), np.where(cc <= s, 0.0, -BIG)]).astype(np.float32)
    c["c_PM"] = PM
    c["c_NM"] = NM
    pi = np.arange(128)[:, None]
    ji = np.arange(128)[None, :]
    Bb = lambda b: (pi // b == ji // b).astype(np.float32)
    c["c_blk"] = np.concatenate([Bb(16), Bb(32) - Bb(16), Bb(64) - Bb(32), 1.0 - Bb(64)], axis=1).astype(np.float32)
    sel = np.zeros((4, 2, 4, 128), np.float32)
    for h in range(4):
        sel[h, 0, h, :] = 1.0
        sel[h, 1, h, :] = -1.0
    c["c_sel"] = sel.reshape(4, -1)
    return c


def _win_perm():
    cols = []
    cols += list(range(0, 768))
    for j in range(4):
        cols += list(range(OFF_ATT + j * 64, OFF_ATT + (j + 1) * 64))
        cols += list(range(OFF_ATT + (4 + j) * 64, OFF_ATT + (5 + j) * 64))
    cols += list(range(OFF_ATT + 512, OFF_ATT + 640))
    cols += list(range(OFF_DN, OFF_DN + 768))
    cols += list(range(OFF_ATT + 640, OFF_ATT + 768))
    cols += list(range(OFF_DN + 768, OFF_DN + 1024))
    cols += list(range(OFF_DN + 1024, OFF_DN + 1040))
    assert len(cols) == DIN and len(set(cols)) == DIN
    return np.array(cols)


def make_in_maps(cfg, inp, ncores=8):
    f = lambda a: np.ascontiguousarray(np.asarray(a, dtype=np.float32))
    nl = cfg.nl
    shared = {}
    shared["ln_in_g"] = f(inp["ln_in_g"]).reshape(1, D)
    shared["ln_in_b"] = f(inp["ln_in_b"]).reshape(1, D)
    shared["w_mod"] = f(inp["w_mod"][:nl])
    shared["b_mod"] = f(inp["b_mod"][:nl])
    shared["b_modc"] = f(np.asarray(inp["b_mod"])[:nl].reshape(nl, 48, 128).transpose(0, 2, 1))
    shared["w_in"] = f(np.asarray(inp["w_in"])[:nl][:, :, _win_perm()])
    ca = np.asarray(inp["conv_a_w"])[:nl]
    shared["conv_a"] = f(ca.reshape(nl, 3, 2, 128).transpose(0, 3, 2, 1))
    shared["sink"] = f(inp["attn_sink"][:nl])
    dc = np.asarray(inp["dn_conv_w"])[:nl]
    shared["dn_conv"] = f(dc.reshape(nl, 3, 6, 128).transpose(0, 3, 2, 1))
    shared["a_log"] = f(np.concatenate([inp["dn_a_log_f"][:nl], inp["dn_a_log_b"][:nl]], axis=1))
    shared["dt_bias"] = f(np.concatenate([inp["dn_dt_bias_f"][:nl], inp["dn_dt_bias_b"][:nl]], axis=1))
    shared["norm_g"] = f(np.tile(np.asarray(inp["dn_norm_g"])[:nl], (1, 4)))
    shared["w_out"] = f(inp["w_out"][:nl])
    shared["ln1_g"] = f(inp["ln1_g"][:nl])
    shared["ln1_b"] = f(inp["ln1_b"][:nl])
    shared["w1"] = f(inp["w1"][:nl])
    shared["b1c"] = f(np.asarray(inp["b1"])[:nl].reshape(nl, 32, 128).transpose(0, 2, 1))
    shared["w2"] = f(inp["w2"][:nl])
    shared["b2"] = f(inp["b2"][:nl])
    shared["ln2_g"] = f(inp["ln2_g"][:nl])
    shared["ln2_b"] = f(inp["ln2_b"][:nl])
    shared.update(_consts())
    xs_all = np.asarray(inp["x_sample"])
    xp_all = np.asarray(inp["x_prompt"])
    cs_all = np.asarray(inp["c_sample"])
    cp_all = np.asarray(inp["c_prompt"])
    maps = []
    for i in range(ncores):
        m = dict(shared)
        m["xs"] = f(xs_all[i % xs_all.shape[0]])
        m["xp"] = f(xp_all[(i // 2) % xp_all.shape[0]])
        c2 = np.stack([cs_all[i % cs_all.shape[0]], cp_all[(i // 2) % cp_all.shape[0]]], axis=1)
        m["cT"] = f(c2.reshape(8, 128, 2).transpose(1, 0, 2))
        maps.append(m)
    return maps


_NC_CACHE = {}


def kernel(**inputs):
    cfg = Cfg()
    if "nc" not in _NC_CACHE:
        _NC_CACHE["nc"] = build(cfg)
    nc = _NC_CACHE["nc"]
    maps = make_in_maps(cfg, inputs, 8)
    res = run_bass_kernel_spmd(nc, maps, core_ids=list(range(8)))
    y_s = np.stack([np.asarray(res.results[i]["ys"], dtype=np.float32) for i in range(8)], axis=0)
    y_p = np.stack([np.asarray(res.results[2 * i]["yp"], dtype=np.float32) for i in range(4)], axis=0)
    return (y_p, y_s)
```
